# Optimizing a Trainium2 kernel written in Bass

```python
import math
import jax, jax.numpy as jnp
from jax import lax
import numpy as np

D_MODEL = 2048
BATCH = 4
SEQ = 8192
DEPTH = 1

N_META = 16
LEAD = 128
N_PAD = LEAD - N_META

MIX_WIDTH = D_MODEL
ATTN_HEADS = 8
ATTN_QK_DIM = 64
ATTN_V_DIM = 2 * ATTN_QK_DIM
ATTN_WIDTH = ATTN_HEADS * ATTN_V_DIM
DN_HEADS = 8
DN_DK = 128
DN_DV = 128
DN_WIDTH = DN_HEADS * DN_DV
CONV_K = 4
CHUNK = 64
Q_BLOCK = 128
FFN_HIDDEN = -(-8 * D_MODEL // (3 * 256)) * 256
EPS = 1e-6
NEG = -1e30

A_Q = ATTN_HEADS * 2 * ATTN_QK_DIM
A_K = ATTN_HEADS * 2 * ATTN_QK_DIM
A_V = ATTN_WIDTH
D_Q = DN_HEADS * DN_DK
D_K = DN_HEADS * DN_DK
D_V = DN_WIDTH
D_Z = DN_WIDTH
D_B = DN_HEADS
D_A = DN_HEADS
COL_SIZES = (A_Q, A_K, A_V, D_Q, D_K, D_V, D_Z, D_B, D_A)
SPLITS = tuple(int(s) for s in np.cumsum(COL_SIZES)[:-1])
IN_COLS = int(sum(COL_SIZES))
CONV_CH = D_Q + D_K + D_V

kernel_name = "hymba_diffattn_gdn_alibi_meta"


def rmsnorm(x, w):
    xf = x.astype(jnp.float32)
    y = xf * lax.rsqrt(jnp.mean(xf * xf, axis=-1, keepdims=True) + EPS)
    return (y * w.astype(jnp.float32)).astype(x.dtype)


def l2norm(x):
    return x * lax.rsqrt(jnp.sum(x * x, axis=-1, keepdims=True) + EPS)


def diff_attention(q, k, v, q_norm_w, k_norm_w, lq1, lk1, lq2, lk2, subln_w, lambda_init):
    dtype = q.dtype
    B, L = q.shape[0], q.shape[1]
    f32 = jnp.float32
    q = rmsnorm(q.reshape(B, L, ATTN_HEADS, 2, ATTN_QK_DIM), q_norm_w).astype(f32)
    k = rmsnorm(k.reshape(B, L, ATTN_HEADS, 2, ATTN_QK_DIM), k_norm_w).astype(f32)
    v = v.reshape(B, L, ATTN_HEADS, ATTN_V_DIM).astype(f32)
    lam = (jnp.exp(jnp.sum(lq1.astype(f32) * lk1.astype(f32)))
           - jnp.exp(jnp.sum(lq2.astype(f32) * lk2.astype(f32))) + lambda_init)
    slopes = 2.0 ** (-8.0 * jnp.arange(1, ATTN_HEADS + 1, dtype=f32) / ATTN_HEADS)
    kpos = jnp.arange(L)
    key_ok = kpos >= N_PAD
    key_real = kpos >= LEAD
    n_blocks = L // Q_BLOCK
    qb = jnp.moveaxis(q.reshape(B, n_blocks, Q_BLOCK, ATTN_HEADS, 2, ATTN_QK_DIM), 1, 0)
    scale = ATTN_QK_DIM ** -0.5

    def one_block(args):
        q_blk, start = args
        qpos = start + jnp.arange(Q_BLOCK)
        s = jnp.einsum('bqhmd,bkhmd->bhmqk', q_blk, k) * scale
        dist = (qpos[:, None] - kpos[None, :]).astype(f32)
        bias = jnp.where(key_real[None, :], -slopes[:, None, None] * dist, 0.0)
        allowed = (kpos[None, :] <= qpos[:, None]) & key_ok[None, :]
        s = jnp.where(allowed, s + bias[None, :, None], NEG)
        p = jax.nn.softmax(s, axis=-1)
        p = p[:, :, 0] - lam * p[:, :, 1]
        return jnp.einsum('bhqk,bkhd->bqhd', p, v)

    o = lax.map(one_block, (qb, jnp.arange(n_blocks) * Q_BLOCK))
    o = jnp.moveaxis(o, 0, 1).reshape(B, L, ATTN_HEADS, ATTN_V_DIM)
    o = rmsnorm(o, subln_w) * (1.0 - lambda_init)
    return o.reshape(B, L, ATTN_WIDTH).astype(dtype)


def causal_conv(x, w):
    L = x.shape[1]
    xp = jnp.pad(x, ((0, 0), (CONV_K - 1, 0), (0, 0)))
    return sum(xp[:, j:j + L, :] * w[:, j].astype(x.dtype) for j in range(CONV_K))


def gated_deltanet(q, k, v, z, b, a, conv_w, a_log, dt_bias, o_norm_w):
    dtype = q.dtype
    B, L = q.shape[0], q.shape[1]
    f32 = jnp.float32
    valid = jnp.arange(L) >= N_PAD
    qkv = jnp.concatenate([q, k, v], axis=-1) * valid[None, :, None].astype(dtype)
    qkv = jax.nn.silu(causal_conv(qkv, conv_w)).astype(f32)
    q, k, v = jnp.split(qkv, [D_Q, D_Q + D_K], axis=-1)
    q = l2norm(q.reshape(B, L, DN_HEADS, DN_DK)) * (DN_DK ** -0.5)
    k = l2norm(k.reshape(B, L, DN_HEADS, DN_DK))
    v = v.reshape(B, L, DN_HEADS, DN_DV)
    vmask = valid.astype(f32)[None, :, None]
    beta = jax.nn.sigmoid(b.astype(f32)) * vmask
    g = -jnp.exp(a_log.astype(f32)) * jax.nn.softplus(a.astype(f32) + dt_bias.astype(f32)) * vmask
    n = L // CHUNK

    def chunks(t):
        t = t.reshape((B, n, CHUNK) + t.shape[2:])
        return jnp.moveaxis(t, 3, 1)

    qc, kc, vc = chunks(q), chunks(k), chunks(v)
    bc = chunks(beta)
    gc = jnp.cumsum(chunks(g), axis=-1)
    idx = jnp.arange(CHUNK)
    incl = idx[:, None] >= idx[None, :]
    strict = idx[:, None] > idx[None, :]
    diff = gc[..., :, None] - gc[..., None, :]
    decay = jnp.where(incl, jnp.exp(jnp.where(incl, diff, 0.0)), 0.0)
    kb = kc * bc[..., None]
    lmat = jnp.where(strict, jnp.einsum('bhnid,bhnjd->bhnij', kb, kc) * decay, 0.0)
    tmat = lmat + jnp.eye(CHUNK, dtype=f32)
    u = lax.linalg.triangular_solve(tmat, vc * bc[..., None], left_side=True, lower=True,
                                    unit_diagonal=True)
    w = lax.linalg.triangular_solve(tmat, kb * jnp.exp(gc)[..., None], left_side=True,
                                    lower=True, unit_diagonal=True)
    qk = jnp.einsum('bhnid,bhnjd->bhnij', qc, kc) * decay

    def step(S, xs):
        q_c, k_c, u_c, w_c, qk_c, g_c = xs
        v_new = u_c - jnp.einsum('bhck,bhkv->bhcv', w_c, S)
        o = (jnp.einsum('bhck,bhkv->bhcv', q_c * jnp.exp(g_c)[..., None], S)
             + jnp.einsum('bhij,bhjv->bhiv', qk_c, v_new))
        g_last = g_c[..., -1:]
        S = (S * jnp.exp(g_last)[..., None]
             + jnp.einsum('bhck,bhcv->bhkv', k_c * jnp.exp(g_last - g_c)[..., None], v_new))
        return S, o

    xs = (jnp.moveaxis(qc, 2, 0), jnp.moveaxis(kc, 2, 0), jnp.moveaxis(u, 2, 0),
          jnp.moveaxis(w, 2, 0), jnp.moveaxis(qk, 2, 0), jnp.moveaxis(gc, 2, 0))
    S0 = jnp.zeros((B, DN_HEADS, DN_DK, DN_DV), f32)
    _, o = lax.scan(step, S0, xs)
    o = jnp.moveaxis(o, 0, 2).reshape(B, DN_HEADS, L, DN_DV).transpose(0, 2, 1, 3)
    o = rmsnorm(o, o_norm_w) * jax.nn.silu(z.reshape(B, L, DN_HEADS, DN_DV).astype(f32))
    return o.reshape(B, L, DN_WIDTH).astype(dtype)


def setup_inputs(seed: int = 0) -> dict:
    key = jax.random.key(seed)
    ks = jax.random.split(key, 20)
    f32 = jnp.float32

    def normal(k, shape, scale):
        return jax.random.normal(k, shape, f32) * scale

    def gain(k, shape):
        return 1.0 + 0.02 * jax.random.normal(k, shape, f32)

    dt = jnp.exp(jax.random.uniform(ks[13], (DEPTH, DN_HEADS), f32,
                                    math.log(1e-3), math.log(1e-1)))
    return {
        "x": normal(ks[0], (BATCH, SEQ, D_MODEL), 1.0),
        "meta_tokens": normal(ks[1], (N_META, D_MODEL), 1.0),
        "attn_norm_w": gain(ks[2], (DEPTH, D_MODEL)),
        "w_in": normal(ks[3], (DEPTH, D_MODEL, IN_COLS), D_MODEL ** -0.5),
        "q_norm_w": gain(ks[4], (DEPTH, ATTN_QK_DIM)),
        "k_norm_w": gain(ks[5], (DEPTH, ATTN_QK_DIM)),
        "lambda_q1": normal(ks[6], (DEPTH, ATTN_QK_DIM), 0.1),
        "lambda_k1": normal(ks[7], (DEPTH, ATTN_QK_DIM), 0.1),
        "lambda_q2": normal(ks[8], (DEPTH, ATTN_QK_DIM), 0.1),
        "lambda_k2": normal(ks[9], (DEPTH, ATTN_QK_DIM), 0.1),
        "subln_w": gain(ks[10], (DEPTH, ATTN_V_DIM)),
        "conv_w": normal(ks[11], (DEPTH, CONV_CH, CONV_K), CONV_K ** -0.5),
        "a_log": jnp.log(jax.random.uniform(ks[12], (DEPTH, DN_HEADS), f32, 1.0, 16.0)),
        "dt_bias": dt + jnp.log(-jnp.expm1(-dt)),
        "o_norm_w": gain(ks[14], (DEPTH, DN_DV)),
        "w_out": normal(ks[15], (DEPTH, MIX_WIDTH, D_MODEL), MIX_WIDTH ** -0.5),
        "ffn_norm_w": gain(ks[16], (DEPTH, D_MODEL)),
        "w_gate": normal(ks[17], (DEPTH, D_MODEL, FFN_HIDDEN), D_MODEL ** -0.5),
        "w_up": normal(ks[18], (DEPTH, D_MODEL, FFN_HIDDEN), D_MODEL ** -0.5),
        "w_down": normal(ks[19], (DEPTH, FFN_HIDDEN, D_MODEL), FFN_HIDDEN ** -0.5),
    }


def reference(x, meta_tokens, attn_norm_w, w_in, q_norm_w, k_norm_w, lambda_q1, lambda_k1,
              lambda_q2, lambda_k2, subln_w, conv_w, a_log, dt_bias, o_norm_w, w_out,
              ffn_norm_w, w_gate, w_up, w_down):
    B = x.shape[0]
    lead = jnp.concatenate(
        [jnp.zeros((B, N_PAD, D_MODEL), x.dtype),
         jnp.broadcast_to(meta_tokens[None].astype(x.dtype), (B, N_META, D_MODEL))], axis=1)
    h = jnp.concatenate([lead, x], axis=1)
    for l in range(DEPTH):
        lambda_init = 0.8 - 0.6 * math.exp(-0.3 * l)
        u = rmsnorm(h, attn_norm_w[l])
        proj = u @ w_in[l]
        aq, ak, av, dq, dk, dv, dz, db, da = jnp.split(proj, SPLITS, axis=-1)
        o_a = diff_attention(aq, ak, av, q_norm_w[l], k_norm_w[l], lambda_q1[l], lambda_k1[l],
                             lambda_q2[l], lambda_k2[l], subln_w[l], lambda_init)
        o_d = gated_deltanet(dq, dk, dv, dz, db, da, conv_w[l], a_log[l], dt_bias[l], o_norm_w[l])
        h = h + jnp.concatenate([o_a, o_d], axis=-1) @ w_out[l]
        u = rmsnorm(h, ffn_norm_w[l])
        h = h + (jax.nn.silu(u @ w_gate[l]) * (u @ w_up[l])) @ w_down[l]
    return h[:, LEAD:]
```

```python
import math
import numpy as np
from contextlib import ExitStack
import concourse.bass as bass
import concourse.mybir as mybir
from concourse.bass_utils import run_bass_kernel_spmd

F32 = mybir.dt.float32
BF16 = mybir.dt.bfloat16
AF = mybir.ActivationFunctionType
ALU = mybir.AluOpType

D = 2048
HID = 5632
EPS = 1e-6
LAMBDA_INIT = 0.8 - 0.6 * math.exp(0.0)
ENGS = ("pe", "act", "dve", "pool", "sp")
ALIBI_CUT = 60.0


class Buf:
    __slots__ = ("name", "w", "r", "dsem", "dcnt")

    def __init__(self, name):
        self.name = name
        self.w = None
        self.r = []
        self.dsem = None
        self.dcnt = 0


class Op:
    __slots__ = ("eng", "fn", "raw", "oth", "sig", "cnt", "isdma", "sem", "ndma")

    def __init__(self, eng, fn, isdma=False):
        self.eng = eng
        self.fn = fn
        self.raw = set()
        self.oth = set()
        self.sig = False
        self.cnt = 0
        self.isdma = isdma
        self.sem = None
        self.ndma = 0


class Prog:
    def __init__(self, nc, stack):
        self.nc = nc
        self.stack = stack
        self.esem = {e: stack.enter_context(nc.semaphore("es_" + e)) for e in ENGS}
        self.ecnt = {e: 0 for e in ENGS}
        self.waited = {e: {} for e in ENGS}
        self.ops = {e: [] for e in ENGS}
        self.dma_bufs = []
        self.nsem = 0

    def _deps(self, o, reads, writes):
        for b in reads:
            if b.w is not None:
                o.raw.add(b.w)
        for b in writes:
            if b.w is not None:
                o.oth.add(b.w)
            for r in b.r:
                o.oth.add(r)
        o.raw.discard(o)
        o.oth.discard(o)
        for b in reads:
            b.r.append(o)
        for b in writes:
            b.w = o
            b.r = []

    def op(self, eng, fn, reads=(), writes=()):
        o = Op(eng, fn)
        self._deps(o, reads, writes)
        self.ops[eng].append(o)
        return o

    def dma(self, fn, reads=(), writes=(), n=1, key=None, eng="sp"):
        o = Op(eng, fn, isdma=True)
        o.ndma = n
        self._deps(o, reads, writes)
        if key is None:
            key = writes[0] if writes else reads[0]
        if key.dsem is None:
            key.dsem = self.stack.enter_context(self.nc.semaphore("ds%d" % self.nsem))
            self.nsem += 1
            self.dma_bufs.append(key)
        key.dcnt += 16 * n
        o.sem = key.dsem
        o.cnt = key.dcnt
        self.ops[eng].append(o)
        return o

    def flush(self):
        nc = self.nc
        ops = self.ops
        for e in ENGS:
            for o in ops[e]:
                for d in o.raw | o.oth:
                    if d.isdma:
                        continue
                    if d.eng == e and (e in ("pe", "sp") or d not in o.raw):
                        continue
                    d.sig = True
        for e in ENGS:
            for o in reversed(ops[e]):
                if not o.isdma:
                    o.sig = True
                    break
        for e in ENGS:
            c = self.ecnt[e]
            for o in ops[e]:
                if o.isdma:
                    continue
                if o.sig:
                    c += 1
                o.cnt = c
            self.ecnt[e] = c
        final_e = dict(self.ecnt)
        final_d = [(b.dsem, b.dcnt) for b in self.dma_bufs]
        esem = self.esem
        waited = self.waited

        def emit(e, eng):
            wd = waited[e]

            def wait(sem, val):
                k = id(sem)
                if wd.get(k, 0) < val:
                    eng.wait_ge(sem, val)
                    wd[k] = val

            for o in ops[e]:
                need = {}
                for d in o.raw | o.oth:
                    if d.isdma:
                        s, v = d.sem, d.cnt
                    else:
                        if d.eng == e and (e in ("pe", "sp") or d not in o.raw):
                            continue
                        s, v = esem[d.eng], d.cnt
                    k = id(s)
                    if k not in need or need[k][1] < v:
                        need[k] = (s, v)
                for s, v in need.values():
                    wait(s, v)
                r = o.fn(eng)
                if o.isdma:
                    assert len(r) == o.ndma, (len(r), o.ndma)
                    for ins in r:
                        ins.then_inc(o.sem, 16)
                elif o.sig:
                    r.then_inc(esem[e], 1)
            if e == "sp":
                for s, v in final_d:
                    wait(s, v)
                eng.sem_inc(esem["sp"], 1)
            for e2 in ENGS:
                v = final_e[e2] + (1 if e2 == "sp" else 0)
                if e2 == e:
                    wd[id(esem[e2])] = v
                    continue
                wait(esem[e2], v)

        with nc.Block() as block:
            @block.tensor
            def _(eng):
                emit("pe", eng)

            @block.scalar
            def _(eng):
                emit("act", eng)

            @block.vector
            def _(eng):
                emit("dve", eng)

            @block.gpsimd
            def _(eng):
                emit("pool", eng)

            @block.sync
            def _(eng):
                emit("sp", eng)
        self.ecnt["sp"] += 1
        self.ops = {e: [] for e in ENGS}


class Rot:
    def __init__(self, items):
        self.items = items
        self.i = 0

    def next(self):
        it = self.items[self.i % len(self.items)]
        self.i += 1
        return it


def _cst_layout():
    off = {}
    o = 0
    for name, w in (("ident", 128), ("ones", 128), ("utf", 128), ("su4", 512), ("iu4", 512), ("valid", 1),
                    ("kbias", 4), ("eps", 1), ("zero", 1), ("one", 1), ("F32END", 0),
                    ("blk64", 128), ("d32", 512), ("o1t", 512), ("o2t", 512), ("i4", 512), ("cneg", 128)):
        off[name] = (o, w)
        o += w
    return off, o


CST, NCST = _cst_layout()
NF32 = CST["F32END"][0]


def make_consts(slopes4):
    c = np.zeros((128, NCST), np.float32)

    def put(name, arr):
        o, w = CST[name]
        c[:, o:o + w] = arr

    p = np.arange(128)
    put("ident", np.eye(128))
    put("blk64", ((p[:, None] // 64) == (p[None, :] // 64)) / 64.0)
    put("ones", np.ones((128, 128)))
    put("utf", (p[:, None] <= p[None, :]))
    su = (p[None, :] > p[:, None]).astype(np.float32)
    iu = (p[None, :] >= p[:, None]).astype(np.float32)
    put("su4", np.tile(su, (1, 4)))
    put("iu4", np.tile(iu, (1, 4)))
    blk = p // 32
    d32 = (blk[:, None] == blk[None, :]).astype(np.float32)
    put("d32", np.tile(d32, (1, 4)))
    o1t = ((blk[:, None] == blk[None, :] + 1) & (blk[:, None] % 2 == 1)).astype(np.float32)
    o2t = ((blk[:, None] >= 2) & (blk[None, :] < 2)).astype(np.float32)
    put("o1t", np.tile(o1t, (1, 4)))
    put("o2t", np.tile(o2t, (1, 4)))
    put("i4", np.tile(np.eye(128), (1, 4)))
    put("cneg", np.where(p[:, None] > p[None, :], -30000.0, 0.0))
    put("valid", (p >= 112).astype(np.float32)[:, None])
    put("kbias", p[:, None] * np.asarray(slopes4, np.float32)[None, :])
    put("eps", np.full((128, 1), EPS))
    put("one", np.ones((128, 1)))
    return c


def build_program(NT, debug=False, upto=99):
    L = NT * 128
    S = L - 128
    NTH = (NT - 1) // 2
    SH = NTH * 128
    nc = bass.Bass("TRN2", target_bir_lowering=False)

    def din(name, shape, dt=F32):
        return nc.dram_tensor(name, list(shape), dt, kind="ExternalInput").ap()

    def dscr(name, shape, dt, out=False):
        return nc.dram_tensor(name, list(shape), dt, kind="ExternalOutput" if (out and debug) else "Internal").ap()

    x = din("x", [S, D])
    meta = din("meta", [16, D])
    anw = din("anw", [128, D])
    fnw = din("fnw", [128, D])
    wA = din("wA", [D, 2056])
    wDn = din("wD", [D, 1536])
    wo = din("wo", [D, D])
    wg = din("wg", [D, HID])
    wu = din("wu", [D, HID])
    wd = din("wd", [HID, D])
    cst = din("cst", [128, NCST])
    small = din("small", [128, 572])
    aug = din("aug", [4, 2, 3, L])
    xh = din("xh", [SH, D])
    sel = din("sel", [128, 2])
    out = nc.dram_tensor("out", [SH, D], F32, kind="ExternalOutput").ap()

    QT = dscr("QT", [4, 128, L], BF16, True)
    KT = dscr("KT", [4, 128, L], BF16, True)
    VV = dscr("VV", [L, 512], BF16, True)
    DQ = dscr("DQ", [4, 128, L], BF16, True)
    DK = dscr("DK", [4, 128, L], BF16, True)
    DV = dscr("DV", [4, 128, L], BF16, True)
    ZZ = dscr("ZZ", [L, 512], BF16, True)
    PW = min(1024, S)
    NPC = S // PW
    OTs = [dscr("OT%d" % k, [1024, PW], BF16, debug and upto < 4) for k in range(NPC)]
    OTFs = [dscr("OTF%d" % k, [2048, PW], BF16) for k in range(NPC)]

    def OTc(r0, r1, c0, c1):
        k = c0 // PW
        assert (c1 - 1) // PW == k
        return OTs[k][r0:r1, c0 - k * PW:c1 - k * PW]

    def OTFc(r0, r1, c0, c1):
        k = c0 // PW
        assert (c1 - 1) // PW == k
        return OTFs[k][r0:r1, c0 - k * PW:c1 - k * PW]
    WG2 = dscr("WG2", [22, 128, 16 * 256], BF16)
    WU2 = dscr("WU2", [22, 128, 16 * 256], BF16)
    WD2 = dscr("WD2", [HID, D], BF16)
    WO2 = dscr("WO2", [D, D], BF16)
    BGo = nc.dram_tensor("BGo", [128, NT * 8], F32, kind="ExternalOutput").ap() if debug else None

    SM = {"qkw": (0, 2), "lam": (2, 256), "subw": (258, 128), "onw": (386, 128), "convw": (514, 48),
          "alog": (562, 4), "dtb": (566, 4)}

    with ExitStack() as outer:
        P = Prog(nc, outer)

        def OP(eng, method, *args, reads=(), writes=(), **kw):
            return P.op(eng, lambda e: getattr(e, method)(*args, **kw), reads, writes)

        def MM(out_, lhsT, rhs, start, stop, reads, writes):
            return P.op("pe", lambda e: e.matmul(out_, lhsT, rhs, start=start, stop=stop, skip_group_check=True), reads, writes)

        def DMA(out_, in_, reads=(), writes=(), eng="sp", key=None):
            return P.dma(lambda e: [e.dma_start(out=out_, in_=in_)], reads, writes, 1, key, eng)

        uniq = [0]

        def sbuf(stack, name, shape, dt):
            uniq[0] += 1
            return stack.enter_context(nc.sbuf_tensor("%s_%d" % (name, uniq[0]), list(shape), dt))

        def psum(stack, name, shape, dt):
            uniq[0] += 1
            return stack.enter_context(nc.psum_tensor("%s_%d" % (name, uniq[0]), list(shape), dt))

        def rot(stack, name, n, shape, dt, ps=False):
            mk = psum if ps else sbuf
            return Rot([(mk(stack, "%s%d" % (name, i), shape, dt), Buf("%s%d" % (name, i))) for i in range(n)])

        cf = sbuf(outer, "cf", [128, NF32], F32)
        cb = sbuf(outer, "cb", [128, NCST], BF16)
        sm = sbuf(outer, "sm", [128, 572], F32)
        bg = sbuf(outer, "bg", [128, NT * 8], F32)
        lamt = sbuf(outer, "lamt", [128, 8], F32)
        negA = sbuf(outer, "negA", [128, 4], F32)
        Bcf, Bcb, Bsm, Bbg, Blam, BnegA = [Buf(n) for n in ("cf", "cb", "sm", "bg", "lam", "negA")]

        def C(name, bf=False, lo=0, hi=None):
            o, w = CST[name]
            t = cb if bf else cf
            if not bf:
                assert o + w <= NF32, name
            return t[:, o + lo:o + (w if hi is None else hi)]

        def SMc(name, lo=0, hi=None):
            o, w = SM[name]
            return sm[:, o + lo:o + (w if hi is None else hi)]

        conv_jobs = []
        for c in range(16):
            for hf in range(2):
                for (wsrc_, wdst_) in ((wg, WG2), (wu, WU2)):
                    conv_jobs.append((wsrc_[c * 128:(c + 1) * 128, hf * 2816:(hf + 1) * 2816],
                                      wdst_[hf * 11:(hf + 1) * 11, :, c * 256:(c + 1) * 256].rearrange("g p k -> p g k"), 2816, 11))
        for r in range(HID // 128):
            conv_jobs.append((wd[r * 128:(r + 1) * 128, :], WD2[r * 128:(r + 1) * 128, :], 2048, 0))
        for r in range(16):
            conv_jobs.append((wo[r * 128:(r + 1) * 128, :], WO2[r * 128:(r + 1) * 128, :], 2048, 0))
        conv_state = {"i": 0}

        with ExitStack() as ph:
            ctmp = sbuf(ph, "ctmp", [128, NCST], F32)
            Bct = Buf("ctmp")
            DMA(ctmp[:], cst[:, :], writes=[Bct])
            DMA(cf[:], cst[:, 0:NF32], writes=[Bcf])
            DMA(sm[:], small[:, :], writes=[Bsm])
            OP("pool", "tensor_copy", cb[:], ctmp[:], reads=[Bct], writes=[Bcb])
            lo = SM["lam"][0]
            tmp = sbuf(ph, "ltmp", [128, 64], F32)
            Bt = Buf("ltmp")
            for k in range(2):
                OP("dve", "tensor_tensor", tmp[:], sm[:, lo + k * 128:lo + k * 128 + 64],
                   sm[:, lo + k * 128 + 64:lo + k * 128 + 128], ALU.mult, reads=[Bsm], writes=[Bt])
                OP("dve", "tensor_reduce", lamt[:, k:k + 1], tmp[:], mybir.AxisListType.X, ALU.add, reads=[Bt], writes=[Blam])
            OP("act", "activation", lamt[:, 2:4], lamt[:, 0:2], AF.Exp, reads=[Blam], writes=[Blam])
            OP("dve", "tensor_tensor", lamt[:, 4:5], lamt[:, 3:4], lamt[:, 2:3], ALU.subtract, reads=[Blam], writes=[Blam])
            OP("dve", "tensor_scalar", lamt[:, 5:6], lamt[:, 4:5], -LAMBDA_INIT, None, op0=ALU.add, reads=[Blam], writes=[Blam])
            OP("act", "activation", negA[:], SMc("alog"), AF.Exp, reads=[Bsm], writes=[BnegA])
            OP("dve", "tensor_scalar", negA[:], negA[:], -1.0, None, op0=ALU.mult, reads=[BnegA], writes=[BnegA])
            P.flush()

        blocks = [(0, 1)] + [(1 + 4 * i, 4) for i in range((NT - 1) // 4)]
        assert (NT - 1) % 4 == 0

        def phase_A(pass_id):
            CW = 2056 if pass_id == 0 else 1536
            wsrc = wA if pass_id == 0 else wDn
            with ExitStack() as ph:
                wres = sbuf(ph, "wres", [128, 16 * CW], BF16)
                Bw = Buf("wres")
                xts = rot(ph, "xt", 2, [128, D], F32)
                if pass_id == 0:
                    stR = rot(ph, "stg", 2, [128, 2816], F32)
                    stbR = rot(ph, "stgb", 2, [128, 2816], BF16)
                else:
                    stR = xts
                for c in range(16):
                    st, Bs = stR.next()
                    DMA(st[:, 0:CW], wsrc[c * 128:(c + 1) * 128, :], writes=[Bs])
                    OP("pool", "tensor_copy", wres[:, c * CW:(c + 1) * CW], st[:, 0:CW], reads=[Bs], writes=[Bw])

                def convert_some(n):
                    for _ in range(n):
                        if conv_state["i"] >= len(conv_jobs):
                            return
                        src, dst, w, g = conv_jobs[conv_state["i"]]
                        conv_state["i"] += 1
                        st, Bs = stR.next()
                        sb_, Bb = stbR.next()
                        DMA(st[:, 0:w], src, writes=[Bs])
                        OP("pool", "tensor_copy", sb_[:, 0:w], st[:, 0:w], reads=[Bs], writes=[Bb])
                        if g:
                            DMA(dst, sb_[:, 0:w].rearrange("p (g k) -> p g k", g=g), reads=[Bb])
                        else:
                            DMA(dst, sb_[:, 0:w], reads=[Bb])

                us = rot(ph, "u", 2, [128, D], BF16)
                uTs = rot(ph, "uT", 2, [128, 16 * 512], BF16)
                sts = rot(ph, "st", 2, [128, 4], F32)
                anwt = sbuf(ph, "anwt", [128, D], F32)
                Banw = Buf("anwt")
                DMA(anwt[:], anw[:, :], writes=[Banw])
                tps = rot(ph, "tp", 2, [128, 1024], BF16, ps=True)
                mms = rot(ph, "mm", 4, [128, 512], F32, ps=True)
                aux = rot(ph, "aux", 2, [128, 512], F32, ps=True)
                sqs = rot(ph, "sq", 2, [128, 512], BF16)
                rrs = rot(ph, "rr", 2, [128, 512], F32)
                obs = rot(ph, "ob", 3, [128, 512], BF16)
                if pass_id == 0:
                    bat = rot(ph, "bat", 2, [128, 16], F32)
                else:
                    cbuf = sbuf(ph, "cbuf", [128, 12 * 515], F32)
                    Bcbuf = [Buf("cbuf%d" % i) for i in range(12)]
                    OP("pool", "memset", cbuf[:], 0.0, writes=Bcbuf)
                    accs = rot(ph, "acc", 2, [128, 512], F32)
                    sls = rot(ph, "sl", 2, [128, 512], F32)

                for (t0, nt) in blocks:
                    T = nt * 128
                    uT, BuT = uTs.next()
                    uTv = uT[:, :].rearrange("p (c t) -> p c t", c=16)
                    for i in range(nt):
                        ti = t0 + i
                        xt, Bx = xts.next()
                        if ti == 0:
                            OP("pool", "memset", xt[:], 0.0, writes=[Bx])
                            DMA(xt[112:128, :], meta[:, :], writes=[Bx])
                        else:
                            DMA(xt[:], x[(ti - 1) * 128:ti * 128, :], writes=[Bx])
                        st, Bst = sts.next()
                        u, Bu = us.next()
                        OP("act", "activation", u[:], xt[:], AF.Square, accum_out=st[:, 0:1], reads=[Bx], writes=[Bu, Bst])
                        OP("act", "activation", st[:, 1:2], st[:, 0:1], AF.Ln, bias=C("eps"), scale=1.0 / D, reads=[Bst, Bcf], writes=[Bst])
                        OP("act", "activation", st[:, 2:3], st[:, 1:2], AF.Exp, scale=-0.5, reads=[Bst], writes=[Bst])
                        OP("dve", "scalar_tensor_tensor", u[:], xt[:], st[:, 2:3], anwt[:], op0=ALU.mult, op1=ALU.mult,
                           reads=[Bx, Bst, Banw], writes=[Bu])
                        for half in range(2):
                            tp, Btp = tps.next()
                            for c8 in range(8):
                                c = half * 8 + c8
                                OP("pe", "transpose", tp[:, c8 * 128:(c8 + 1) * 128], u[:, c * 128:(c + 1) * 128], C("ident", True),
                                   reads=[Bu, Bcb], writes=[Btp])
                            dstv = uTv[:, half * 8:half * 8 + 8, i * 128:(i + 1) * 128]
                            srcv = tp[:, :].rearrange("p (c t) -> p c t", c=8)
                            if half == 0:
                                OP("act", "activation", dstv, srcv, AF.Copy, reads=[Btp], writes=[BuT])
                            else:
                                OP("dve", "tensor_copy", dstv, srcv, reads=[Btp], writes=[BuT])

                    def fm_block(col0):
                        mm, Bmm = mms.next()
                        for c in range(16):
                            MM(mm[:, 0:T], wres[:, c * CW + col0:c * CW + col0 + 128], uT[:, c * 512:c * 512 + T],
                               c == 0, c == 15, [Bw, BuT], [Bmm])
                        return mm, Bmm

                    def tm_block(i, col0, cw):
                        mm, Bmm = mms.next()
                        for c in range(16):
                            MM(mm[:, 0:cw], uT[:, c * 512 + i * 128:c * 512 + (i + 1) * 128], wres[:, c * CW + col0:c * CW + col0 + cw],
                               c == 0, c == 15, [Bw, BuT], [Bmm])
                        return mm, Bmm

                    def rs_norm(mm, Bmm, src_sb, Bsrc, onesname, lnscale, wcol, dst):
                        sq, Bsq = sqs.next()
                        if src_sb is None:
                            OP("act", "activation", sq[:, 0:T], mm[:, 0:T], AF.Square, reads=[Bmm], writes=[Bsq])
                        else:
                            OP("pool", "tensor_tensor", sq[:, 0:T], src_sb[:, 0:T], src_sb[:, 0:T], ALU.mult, reads=[Bsrc], writes=[Bsq])
                        ax, Bax = aux.next()
                        MM(ax[:, 0:T], C(onesname, True), sq[:, 0:T], True, True, [Bsq, Bcb], [Bax])
                        rr, Brr = rrs.next()
                        OP("act", "activation", rr[:, 0:T], ax[:, 0:T], AF.Ln, bias=C("eps"), scale=lnscale, reads=[Bax, Bcf], writes=[Brr])
                        OP("act", "activation", rr[:, 0:T], rr[:, 0:T], AF.Exp, scale=-0.5, reads=[Brr], writes=[Brr])
                        ob, Bob = obs.next()
                        if src_sb is None:
                            OP("dve", "scalar_tensor_tensor", ob[:, 0:T], mm[:, 0:T], wcol, rr[:, 0:T], op0=ALU.mult, op1=ALU.mult,
                               reads=[Bmm, Brr, Bsm], writes=[Bob])
                        else:
                            OP("dve", "tensor_tensor", ob[:, 0:T], src_sb[:, 0:T], rr[:, 0:T], ALU.mult, reads=[Bsrc, Brr], writes=[Bob])
                        DMA(dst, ob[:, 0:T], reads=[Bob])

                    if pass_id == 0:
                        for qk in range(2):
                            for h in range(4):
                                mm, Bmm = fm_block(qk * 512 + h * 128)
                                dstT = (QT if qk == 0 else KT)[h, :, t0 * 128:t0 * 128 + T]
                                rs_norm(mm, Bmm, None, None, "blk64", 1.0, SMc("qkw", qk, qk + 1), dstT)
                        for i in range(nt):
                            ti = t0 + i
                            for (col0, dstD) in ((1024, VV), (1536, ZZ)):
                                mm, Bmm = tm_block(i, col0, 512)
                                ob, Bob = obs.next()
                                OP("act", "activation", ob[:, :], mm[:, :], AF.Copy, reads=[Bmm], writes=[Bob])
                                DMA(dstD[ti * 128:(ti + 1) * 128, :], ob[:, :], reads=[Bob])
                            mm, Bmm = tm_block(i, 2048, 8)
                            bt, Bbt = bat.next()
                            OP("act", "activation", bt[:, 0:4], mm[:, 0:4], AF.Exp, scale=-1.0, reads=[Bmm], writes=[Bbt])
                            OP("dve", "tensor_scalar", bt[:, 0:4], bt[:, 0:4], 1.0, None, op0=ALU.add, reads=[Bbt], writes=[Bbt])
                            OP("dve", "reciprocal", bg[:, ti * 8:ti * 8 + 4], bt[:, 0:4], reads=[Bbt], writes=[Bbg])
                            OP("dve", "tensor_tensor", bt[:, 4:8], mm[:, 4:8], SMc("dtb"), ALU.add, reads=[Bmm, Bsm], writes=[Bbt])
                            OP("act", "activation", bt[:, 8:12], bt[:, 4:8], AF.Exp, reads=[Bbt], writes=[Bbt])
                            OP("act", "activation", bt[:, 12:16], bt[:, 8:12], AF.Ln, bias=C("one"), reads=[Bbt, Bcf], writes=[Bbt])
                            OP("dve", "tensor_tensor", bg[:, ti * 8 + 4:ti * 8 + 8], bt[:, 12:16], negA[:], ALU.mult,
                               reads=[Bbt, BnegA], writes=[Bbg])
                            if ti == 0:
                                OP("dve", "tensor_scalar", bg[:, 0:8], bg[:, 0:8], C("valid"), None, op0=ALU.mult, reads=[Bbg, Bcf], writes=[Bbg])
                        convert_some(7)
                    else:
                        for typ in range(3):
                            for h in range(4):
                                s = typ * 4 + h
                                mm, Bmm = fm_block(typ * 512 + h * 128)
                                cbs = cbuf[:, s * 515:(s + 1) * 515]
                                Bc = Bcbuf[s]
                                OP("act", "activation", cbs[:, 3:3 + T], mm[:, 0:T], AF.Copy, reads=[Bmm], writes=[Bc])
                                acc, Bacc = accs.next()
                                cw0 = SM["convw"][0] + s * 4
                                OP("dve", "tensor_scalar", acc[:, 0:T], cbs[:, 0:T], sm[:, cw0:cw0 + 1], None, op0=ALU.mult,
                                   reads=[Bc, Bsm], writes=[Bacc])
                                for j in range(1, 4):
                                    OP("dve", "scalar_tensor_tensor", acc[:, 0:T], cbs[:, j:j + T], sm[:, cw0 + j:cw0 + j + 1], acc[:, 0:T],
                                       op0=ALU.mult, op1=ALU.add, reads=[Bc, Bsm, Bacc], writes=[Bacc])
                                OP("pool", "tensor_copy", cbs[:, 0:3], cbs[:, T:T + 3], reads=[Bc], writes=[Bc])
                                dstT = (DQ, DK, DV)[typ][h, :, t0 * 128:t0 * 128 + T]
                                if typ == 2:
                                    ob, Bob = obs.next()
                                    OP("act", "activation", ob[:, 0:T], acc[:, 0:T], AF.Silu, reads=[Bacc], writes=[Bob])
                                    DMA(dstT, ob[:, 0:T], reads=[Bob])
                                else:
                                    sl, Bsl = sls.next()
                                    OP("act", "activation", sl[:, 0:T], acc[:, 0:T], AF.Silu, reads=[Bacc], writes=[Bsl])
                                    rs_norm(None, None, sl, Bsl, "ones", 128.0 if typ == 0 else 1.0, None, dstT)
                while pass_id == 0 and conv_state["i"] < len(conv_jobs):
                    convert_some(4)
                if debug and pass_id == 0:
                    DMA(BGo[:, :], bg[:], reads=[Bbg])
                P.flush()

        phase_A(0)
        if upto >= 1:
            phase_A(1)
        env = dict(locals())
        if upto >= 2:
            build_rest(env)
    return nc


def build_rest(env):
    g = env
    nc, P, NT, L, S, debug, upto = g["nc"], g["P"], g["NT"], g["L"], g["S"], g["debug"], g["upto"]
    OP, MM, DMA, sbuf, psum, rot, C, SMc = g["OP"], g["MM"], g["DMA"], g["sbuf"], g["psum"], g["rot"], g["C"], g["SMc"]
    cf, cb, sm, bg, lamt = g["cf"], g["cb"], g["sm"], g["bg"], g["lamt"]
    Bcf, Bcb, Bsm, Bbg, Blam = g["Bcf"], g["Bcb"], g["Bsm"], g["Bbg"], g["Blam"]
    QT, KT, VV, DQ, DK, DV, ZZ, OTc = g["QT"], g["KT"], g["VV"], g["DQ"], g["DK"], g["DV"], g["ZZ"], g["OTc"]
    aug = g["aug"]
    NQB = (NT - 1) // 4

    with ExitStack() as ph:
        KTa = [rot(ph, "KTa%d" % m, 2, [67, L], BF16) for m in range(2)]
        QTb = [rot(ph, "QTb%d" % m, 3, [67, 512], BF16) for m in range(2)]
        agq = rot(ph, "agq", 2, [67, 512], F32)
        Vas = rot(ph, "Va", 2, [128, NT * 129], BF16)
        APW = 2080
        augs = rot(ph, "augst", 2, [67, APW], F32)
        Sps = rot(ph, "Sps", 3, [128, 512], F32, ps=True)
        Oas = [[(psum(ph, "Oa%d%d" % (m, pr), [128, 512], F32), Buf("Oa%d%d" % (m, pr))) for pr in range(2)] for m in range(2)]
        tpo = rot(ph, "tpo", 1, [128, 1024], BF16, ps=True)
        Pts = rot(ph, "Pt", 4, [128, 512], BF16)
        rrs = rot(ph, "arr", 2, [128, 8], F32)
        t1s = rot(ph, "at1", 2, [128, 128], F32)
        dds = rot(ph, "add", 2, [128, 128], F32)
        jks = rot(ph, "ajk", 1, [128, 128], BF16)
        oas = rot(ph, "aoa", 2, [128, 128], BF16)
        oTs = rot(ph, "aoT", 2, [128, 512], BF16)

        def load_slot(s):
            tiles = {}
            for m in range(2):
                kt_, Bk = KTa[m].next()
                DMA(kt_[0:64, :], KT[s, m * 64:(m + 1) * 64, :], writes=[Bk])
                for side, (tl, Bt) in enumerate(((kt_, Bk),)):
                    for a0 in range(0, L, APW):
                        aw = min(APW, L - a0)
                        ag, Bag = augs.next()
                        DMA(ag[64:67, 0:aw], aug[s, side, :, a0:a0 + aw], writes=[Bag])
                        OP("pool", "tensor_copy", tl[64:67, a0:a0 + aw], ag[64:67, 0:aw], reads=[Bag], writes=[Bt])
                tiles["k%d" % m] = (kt_, Bk)
            va, Bva = Vas.next()
            vav = va[:, :].rearrange("p (t d) -> p t d", d=129)
            vsrc = VV[:, s * 128:(s + 1) * 128].rearrange("(t p) d -> p t d", p=128)
            for t_0 in range(0, NT, 13):
                t_1 = min(NT, t_0 + 13)
                DMA(vav[:, t_0:t_1, 0:128], vsrc[:, t_0:t_1, :], writes=[Bva])
            OP("pool", "memset", vav[:, :, 128:129], 1.0, writes=[Bva])
            OP("pool", "memset", vav[0:112, 0:1, 128:129], 0.0, writes=[Bva])
            tiles["v"] = (va, Bva)
            return tiles

        nxt = load_slot(0)
        for s in range(4):
            cur = nxt
            if s + 1 < 4:
                nxt = load_slot(s + 1)
            va, Bva = cur["v"]
            vav = va[:, :].rearrange("p (t d) -> p t d", d=129)
            slope_min = 2.0 ** -(2 * s + 2)
            for qb in range(NQB):
                q0 = 1 + 4 * qb
                kts = [0]
                for kt in range(1, q0 + 4):
                    if slope_min * (128 * (q0 - kt) - 127) > ALIBI_CUT:
                        continue
                    kts.append(kt)
                started = {}
                qcur = []
                for m in range(2):
                    qtl, Bq = QTb[m].next()
                    DMA(qtl[0:64, :], QT[s, m * 64:(m + 1) * 64, q0 * 128:(q0 + 4) * 128], writes=[Bq])
                    ag, Bag = agq.next()
                    DMA(ag[64:67, :], aug[s, 1, :, q0 * 128:(q0 + 4) * 128], writes=[Bag])
                    OP("pool", "tensor_copy", qtl[64:67, :], ag[64:67, :], reads=[Bag], writes=[Bq])
                    qcur.append((qtl, Bq))
                for kt in kts:
                    c0 = max(kt - q0, 0) * 128
                    for m in range(2):
                        ktl, Bk = cur["k%d" % m]
                        qtl, Bq = qcur[m]
                        Sp, BS = Sps.next()
                        diag = kt >= q0
                        MM(Sp[:, c0:512], ktl[0:67, kt * 128:(kt + 1) * 128], qtl[0:67, c0:512],
                           True, not diag, [Bk, Bq], [BS])
                        if diag:
                            MM(Sp[:, c0:c0 + 128], C("ident", True), C("cneg", True), False, True, [Bcb], [BS])
                        Pt, BP = Pts.next()
                        bias = C("zero") if kt == 0 else C("kbias", False, s, s + 1)
                        OP("act", "activation", Pt[:, c0:512], Sp[:, c0:512], AF.Exp, bias=bias, scale=0.125,
                           reads=[BS, Bcf], writes=[BP])
                        for qi in range(c0 // 128, 4):
                            Oa, BO = Oas[m][qi // 2]
                            col = (qi % 2) * 256
                            key = (m, qi // 2)
                            first = key not in started
                            started[key] = True
                            MM(Oa[:, col:col + 129], Pt[:, qi * 128:(qi + 1) * 128], vav[:, kt, :],
                               first, kt == q0 + qi, [BP, Bva], [BO])
                oT, BoT = oTs.next()
                for qi in range(4):
                    col = (qi % 2) * 256
                    O1, BO1 = Oas[0][qi // 2]
                    O2, BO2 = Oas[1][qi // 2]
                    rr, Brr = rrs.next()
                    OP("dve", "reciprocal", rr[:, 0:1], O1[:, col + 128:col + 129], reads=[BO1], writes=[Brr])
                    OP("dve", "reciprocal", rr[:, 1:2], O2[:, col + 128:col + 129], reads=[BO2], writes=[Brr])
                    OP("dve", "tensor_tensor", rr[:, 2:3], rr[:, 1:2], lamt[:, 5:6], ALU.mult, reads=[Brr, Blam], writes=[Brr])
                    t1, Bt1 = t1s.next()
                    OP("dve", "tensor_scalar", t1[:], O1[:, col:col + 128], rr[:, 0:1], None, op0=ALU.mult, reads=[BO1, Brr], writes=[Bt1])
                    dd, Bdd = dds.next()
                    OP("dve", "scalar_tensor_tensor", dd[:], O2[:, col:col + 128], rr[:, 2:3], t1[:], op0=ALU.mult, op1=ALU.add,
                       reads=[BO2, Brr, Bt1], writes=[Bdd])
                    jk, Bjk = jks.next()
                    OP("act", "activation", jk[:], dd[:], AF.Square, accum_out=rr[:, 3:4], reads=[Bdd], writes=[Bjk, Brr])
                    OP("act", "activation", rr[:, 4:5], rr[:, 3:4], AF.Ln, bias=C("eps"), scale=1.0 / 128, reads=[Brr, Bcf], writes=[Brr])
                    OP("act", "activation", rr[:, 5:6], rr[:, 4:5], AF.Exp, scale=-0.5, reads=[Brr], writes=[Brr])
                    oa, Boa = oas.next()
                    OP("dve", "scalar_tensor_tensor", oa[:], dd[:], rr[:, 5:6], SMc("subw"), op0=ALU.mult, op1=ALU.mult,
                       reads=[Bdd, Brr, Bsm], writes=[Boa])
                    tp, Btp = tpo.next()
                    OP("pe", "transpose", tp[:, qi * 128:(qi + 1) * 128], oa[:], C("ident", True), reads=[Boa, Bcb], writes=[Btp])
                tp, Btp = tpo.items[0]
                OP("act", "activation", oT[:, :], tp[:, 0:512], AF.Copy, scale=1.0 - LAMBDA_INIT, reads=[Btp], writes=[BoT])
                DMA(OTc(s * 128, (s + 1) * 128, (q0 - 1) * 128, (q0 + 3) * 128), oT[:, :], reads=[BoT])
        P.flush()
    if upto >= 3:
        build_dn(env)


def build_dn(env):
    g = env
    nc, P, NT, L, S, debug, upto = g["nc"], g["P"], g["NT"], g["L"], g["S"], g["debug"], g["upto"]
    OP, MM, DMA, sbuf, psum, rot, C, SMc = g["OP"], g["MM"], g["DMA"], g["sbuf"], g["psum"], g["rot"], g["C"], g["SMc"]
    cf, cb, sm, bg = g["cf"], g["cb"], g["sm"], g["bg"]
    Bcf, Bcb, Bsm, Bbg = g["Bcf"], g["Bcb"], g["Bsm"], g["Bbg"]
    DQ, DK, DV, ZZ, OTc = g["DQ"], g["DK"], g["DV"], g["ZZ"], g["OTc"]

    with ExitStack() as ph:
        pf = rot(ph, "pf", 6, [128, 512], F32, ps=True)
        pb = rot(ph, "pb", 2, [128, 1024], BF16, ps=True)
        R = {}

        def T(name, dt=BF16, n=2, w=512):
            if name not in R:
                R[name] = rot(ph, "c" + name, n, [128, w], dt)
            return R[name].next()

        Sf = sbuf(ph, "Sf", [128, 512], F32)
        BSf = Buf("Sf")
        Sb = sbuf(ph, "Sb", [128, 512], BF16)
        BSb = Buf("Sb")
        OP("pool", "memset", Sf[:], 0.0, writes=[BSf])
        OP("pool", "memset", Sb[:], 0.0, writes=[BSb])
        hs = [slice(h * 128, (h + 1) * 128) for h in range(4)]

        def mm4(ps, Bps, lhs, Blhs, rhs, Brhs, extra_reads=()):
            for h in range(4):
                MM(ps[:, hs[h]], lhs[:, hs[h]], rhs[:, hs[h]], True, True, [Blhs, Brhs] + list(extra_reads), [Bps])

        def tr4(src, Bsrc, col0=0):
            tp, Btp = pb.next()
            for h in range(4):
                OP("pe", "transpose", tp[:, col0 + h * 128:col0 + (h + 1) * 128], src[:, hs[h]], C("ident", True),
                   reads=[Bsrc, Bcb], writes=[Btp])
            return tp, Btp

        for i in range(NT):
            kT, BkT = T("kT", n=3)
            qT, BqT = T("qT", n=3)
            vT, BvT = T("vT", n=3)
            tsl = slice(i * 128, (i + 1) * 128)
            DMA(kT[:, :].rearrange("p (h t) -> p h t", h=4), DK[:, :, tsl].rearrange("h p t -> p h t"), writes=[BkT])
            DMA(qT[:, :].rearrange("p (h t) -> p h t", h=4), DQ[:, :, tsl].rearrange("h p t -> p h t"), writes=[BqT])
            DMA(vT[:, :].rearrange("p (h t) -> p h t", h=4), DV[:, :, tsl].rearrange("h p t -> p h t"), writes=[BvT])
            if i >= 1:
                z4, Bz4 = T("z4", n=2)
                DMA(z4[:, :], ZZ[tsl, :], writes=[Bz4])
            g4 = bg[:, i * 8 + 4:i * 8 + 8]
            b4 = bg[:, i * 8:i * 8 + 4]
            gp, Bgp = pf.next()
            MM(gp[:, 0:4], C("utf"), g4, True, True, [Bcf, Bbg], [Bgp])
            B4, BB4 = T("B4", F32)
            for h in range(4):
                OP("pool", "tensor_scalar", B4[:, hs[h]], C("utf"), g4[:, h:h + 1], None, op0=ALU.mult, reads=[Bcf, Bbg], writes=[BB4])
            Gr, BGr = pf.next()
            MM(Gr[:, :], C("ones"), B4[:, :], True, True, [Bcf, BB4], [BGr])
            sc, Bsc = T("sc", F32, 2, 32)
            gcs, dl, edl, edlb, egl = sc[:, 0:4], sc[:, 4:8], sc[:, 8:12], sc[:, 12:16], sc[:, 16:20]
            OP("act", "activation", gcs, gp[:, 0:4], AF.Copy, reads=[Bgp], writes=[Bsc])
            Grl = Gr[:, :].rearrange("p (h t) -> p h t", h=4)[:, :, 127:128]
            OP("dve", "tensor_tensor", dl.rearrange("p (h o) -> p h o", o=1), Grl, gcs.rearrange("p (h o) -> p h o", o=1),
               ALU.subtract, reads=[BGr, Bsc], writes=[Bsc])
            OP("act", "activation", edl, dl, AF.Exp, reads=[Bsc], writes=[Bsc])
            OP("act", "activation", egl.rearrange("p (h o) -> p h o", o=1), Grl, AF.Exp, reads=[BGr], writes=[Bsc])
            OP("dve", "tensor_tensor", edlb, edl, b4, ALU.mult, reads=[Bsc, Bbg], writes=[Bsc])
            Er, BEr = T("Er")
            OP("act", "activation", Er[:, :], Gr[:, :], AF.Exp, reads=[BGr], writes=[BEr])
            KgT, BKgT = T("KgT")
            QgT, BQgT = T("QgT")
            OP("pool", "tensor_tensor", KgT[:, :], kT[:, :], Er[:, :], ALU.mult, reads=[BkT, BEr], writes=[BKgT])
            OP("pool", "tensor_tensor", QgT[:, :], qT[:, :], Er[:, :], ALU.mult, reads=[BqT, BEr], writes=[BQgT])
            Dm, BDm = T("Dm", F32)
            for h in range(4):
                OP("dve", "tensor_scalar", Dm[:, hs[h]], Gr[:, hs[h]], gcs[:, h:h + 1], 0.0, op0=ALU.subtract, op1=ALU.min,
                   reads=[BGr, Bsc], writes=[BDm])
            OP("act", "activation", Dm[:, :], Dm[:, :], AF.Exp, reads=[BDm], writes=[BDm])
            DTs, BDTs = T("DTs", F32)
            DTi, BDTi = T("DTi", F32)
            OP("pool", "tensor_tensor", DTs[:, :], Dm[:, :], C("su4"), ALU.mult, reads=[BDm, Bcf], writes=[BDTs])
            OP("pool", "tensor_tensor", DTi[:, :], Dm[:, :], C("iu4"), ALU.mult, reads=[BDm, Bcf], writes=[BDTi])
            tkv, Btkv = pb.next()
            for h in range(4):
                OP("pe", "transpose", tkv[:, hs[h]], kT[:, hs[h]], C("ident", True), reads=[BkT, Bcb], writes=[Btkv])
            for h in range(4):
                OP("pe", "transpose", tkv[:, 512 + h * 128:512 + (h + 1) * 128], vT[:, hs[h]], C("ident", True), reads=[BvT, Bcb], writes=[Btkv])
            V4, BV4 = T("V4")
            OP("act", "activation", V4[:, :], tkv[:, 512:1024], AF.Copy, reads=[Btkv], writes=[BV4])
            Kd, BKd = T("Kd")
            for h in range(4):
                OP("dve", "tensor_scalar", Kd[:, hs[h]], tkv[:, hs[h]], edlb[:, h:h + 1], None, op0=ALU.mult, reads=[Btkv, Bsc], writes=[BKd])
            Ap, BAp = pf.next()
            mm4(Ap, BAp, kT, BkT, kT, BkT)
            Qp, BQp = pf.next()
            mm4(Qp, BQp, kT, BkT, qT, BqT)
            Y, BY = T("Y")
            QKd, BQKd = T("QKd")
            for h in range(4):
                OP("dve", "scalar_tensor_tensor", Y[:, hs[h]], Ap[:, hs[h]], b4[:, h:h + 1], DTs[:, hs[h]], op0=ALU.mult, op1=ALU.mult,
                   reads=[BAp, Bbg, BDTs], writes=[BY])
            for h in range(4):
                OP("dve", "scalar_tensor_tensor", QKd[:, hs[h]], Qp[:, hs[h]], b4[:, h:h + 1], DTi[:, hs[h]], op0=ALU.mult, op1=ALU.mult,
                   reads=[BQp, Bbg, BDTi], writes=[BQKd])
            tn, Btn = tr4(Y, BY)
            Nk, BNk = T("Nk", n=3)
            O1T, BO1T = T("O1T")
            O2T, BO2T = T("O2T")
            OP("dve", "tensor_tensor", Nk[:, :], tn[:, 0:512], C("d32", True), ALU.mult, reads=[Btn, Bcb], writes=[BNk])
            OP("dve", "tensor_tensor", O1T[:, :], tn[:, 0:512], C("o1t", True), ALU.mult, reads=[Btn, Bcb], writes=[BO1T])
            OP("dve", "tensor_tensor", O2T[:, :], tn[:, 0:512], C("o2t", True), ALU.mult, reads=[Btn, Bcb], writes=[BO2T])
            Yk, BYk = T("Yk", n=3)
            OP("pool", "tensor_tensor", Yk[:, :], Y[:, :], C("d32", True), ALU.mult, reads=[BY, Bcb], writes=[BYk])
            Pk, BPk = T("Pk", n=3)
            OP("pool", "tensor_tensor", Pk[:, :], C("i4", True), Yk[:, :], ALU.subtract, reads=[BYk, Bcb], writes=[BPk])
            for lvl in range(4):
                last = lvl == 3
                if not last:
                    py, Bpy = pf.next()
                    mm4(py, Bpy, Nk, BNk, Yk, BYk)
                pn, Bpn = pf.next()
                mm4(pn, Bpn, Yk, BYk, Nk, BNk)
                Nn, BNn = T("Nk", n=3)
                OP("act", "activation", Nn[:, :], pn[:, :], AF.Copy, reads=[Bpn], writes=[BNn])
                if not last:
                    Yn, BYn = T("Yk", n=3)
                    OP("act", "activation", Yn[:, :], py[:, :], AF.Copy, reads=[Bpy], writes=[BYn])
                    Yk, BYk = Yn, BYn
                Nk, BNk = Nn, BNn
                pp, Bpp = pf.next()
                mm4(pp, Bpp, Nk, BNk, Pk, BPk)
                Pn, BPn = T("Pk", n=3)
                OP("dve", "tensor_tensor", Pn[:, :], pp[:, :], Pk[:, :], ALU.add, reads=[Bpp, BPk], writes=[BPn])
                Pk, BPk = Pn, BPn
            Z, BZ = Pk, BPk
            for (OxT, BOxT) in ((O1T, BO1T), (O2T, BO2T)):
                tz, Btz = tr4(Z, BZ)
                ZT, BZT = T("ZT")
                OP("act", "activation", ZT[:, :], tz[:, 0:512], AF.Copy, reads=[Btz], writes=[BZT])
                pm, Bpm = pf.next()
                mm4(pm, Bpm, OxT, BOxT, Z, BZ)
                M1, BM1 = T("M1")
                OP("act", "activation", M1[:, :], pm[:, :], AF.Copy, reads=[Bpm], writes=[BM1])
                pz, Bpz = pf.next()
                mm4(pz, Bpz, ZT, BZT, M1, BM1)
                Zn, BZn = T("Pk", n=3)
                OP("dve", "tensor_tensor", Zn[:, :], Z[:, :], pz[:, :], ALU.subtract, reads=[BZ, Bpz], writes=[BZn])
                Z, BZ = Zn, BZn
            XT, BXT = Z, BZ
            pk, Bpk = pf.next()
            mm4(pk, Bpk, KgT, BKgT, Sb, BSb)
            R4, BR4 = T("R4")
            OP("dve", "tensor_tensor", R4[:, :], V4[:, :], pk[:, :], ALU.subtract, reads=[BV4, Bpk], writes=[BR4])
            px, Bpx = pf.next()
            mm4(px, Bpx, XT, BXT, R4, BR4)
            vn, Bvn = T("vn")
            OP("act", "activation", vn[:, :], px[:, :], AF.Copy, reads=[Bpx], writes=[Bvn])
            if i >= 1:
                po, Bpo = pf.next()
                for h in range(4):
                    MM(po[:, hs[h]], QgT[:, hs[h]], Sb[:, hs[h]], True, False, [BQgT, BSb], [Bpo])
                    MM(po[:, hs[h]], QKd[:, hs[h]], vn[:, hs[h]], False, True, [BQKd, Bvn], [Bpo])
            if i + 1 < NT:
                pd_, Bpd = pf.next()
                mm4(pd_, Bpd, Kd, BKd, vn, Bvn)
                for h in range(4):
                    OP("dve", "scalar_tensor_tensor", Sf[:, hs[h]], Sf[:, hs[h]], egl[:, h:h + 1], pd_[:, hs[h]], op0=ALU.mult, op1=ALU.add,
                       reads=[BSf, Bsc, Bpd], writes=[BSf])
                OP("act", "activation", Sb[:, :], Sf[:, :], AF.Copy, reads=[BSf], writes=[BSb])
            if i >= 1:
                so, Bso = T("so", F32, 2, 16)
                jk, Bjk = T("jk", BF16, 1, 128)
                for h in range(4):
                    OP("act", "activation", jk[:, :], po[:, hs[h]], AF.Square, accum_out=so[:, h:h + 1], reads=[Bpo], writes=[Bjk, Bso])
                OP("act", "activation", so[:, 4:8], so[:, 0:4], AF.Ln, bias=C("eps"), scale=1.0 / 128, reads=[Bso, Bcf], writes=[Bso])
                OP("act", "activation", so[:, 8:12], so[:, 4:8], AF.Exp, scale=-0.5, reads=[Bso], writes=[Bso])
                on, Bon = T("on", F32)
                for h in range(4):
                    OP("dve", "scalar_tensor_tensor", on[:, hs[h]], po[:, hs[h]], so[:, 8 + h:9 + h], SMc("onw"), op0=ALU.mult, op1=ALU.mult,
                       reads=[Bpo, Bso, Bsm], writes=[Bon])
                sz, Bsz = T("sz", F32)
                OP("act", "activation", sz[:, :], z4[:, :], AF.Silu, reads=[Bz4], writes=[Bsz])
                og, Bog = T("og")
                OP("pool", "tensor_tensor", og[:, :], on[:, :], sz[:, :], ALU.mult, reads=[Bon, Bsz], writes=[Bog])
                tg, Btg = tr4(og, Bog)
                odT, BodT = T("odT")
                OP("act", "activation", odT[:, :], tg[:, 0:512], AF.Copy, reads=[Btg], writes=[BodT])
                DMA(OTc(512, 1024, (i - 1) * 128, i * 128).rearrange("(h p) t -> p h t", p=128),
                    odT[:, :].rearrange("p (h t) -> p h t", h=4), reads=[BodT])
        P.flush()
    if upto >= 4:
        build_ffn(env)


def build_ffn(env):
    g = env
    nc, P, NT, L, S, debug, upto = g["nc"], g["P"], g["NT"], g["L"], g["S"], g["debug"], g["upto"]
    OP, MM, DMA, sbuf, psum, rot, C, SMc = g["OP"], g["MM"], g["DMA"], g["sbuf"], g["psum"], g["rot"], g["C"], g["SMc"]
    cf, cb, sm = g["cf"], g["cb"], g["sm"]
    Bcf, Bcb, Bsm = g["Bcf"], g["Bcb"], g["Bsm"]
    OTs, OTFs, OTFc, WG2, WU2, WD2, WO2 = g["OTs"], g["OTFs"], g["OTFc"], g["WG2"], g["WU2"], g["WD2"], g["WO2"]
    xh, sel, fnw, out, SH, NTH = g["xh"], g["sel"], g["fnw"], g["out"], g["SH"], g["NTH"]

    def CC(a, b):
        o = P.op("pool", lambda e: e.collective_compute("AllGather", ALU.bypass, replica_groups=[[0, 1], [2, 3], [4, 5], [6, 7]],
                                                         ins=[a], outs=[b]), (), ())
        o.sig = True
    for k in range(len(OTs)):
        CC(OTs[k], OTFs[k])
    P.flush()

    with ExitStack() as ph:
        pf = rot(ph, "epf", 6, [128, 512], F32, ps=True)
        pb = rot(ph, "epb", 2, [128, 1024], BF16, ps=True)
        oTa = sbuf(ph, "oTa", [128, 16 * 512], BF16)
        oTb = sbuf(ph, "oTb", [128, 16 * 512], BF16)
        BoTa, BoTb = Buf("oTa"), Buf("oTb")
        wobs = rot(ph, "wob", 2, [128, 16 * 256], BF16)
        haccs = [(sbuf(ph, "hacc%d" % i, [128, D], F32), Buf("hacc%d" % i)) for i in range(4)]
        u = sbuf(ph, "eu", [128, D], BF16)
        Bu = Buf("eu")
        uT = sbuf(ph, "euT", [128, 16 * 512], BF16)
        BuT = Buf("euT")
        uTv = uT[:, :].rearrange("p (c t) -> p c t", c=16)
        fnwt = sbuf(ph, "fnwt", [128, D], F32)
        Bfnw = Buf("fnwt")
        selt = sbuf(ph, "selt", [128, 2], F32)
        Bsel = Buf("selt")
        DMA(fnwt[:], fnw[:, :], writes=[Bfnw])
        DMA(selt[:], sel[:, :], writes=[Bsel])
        wgs = rot(ph, "wgb", 2, [128, 16 * 256], BF16)
        wus = rot(ph, "wub", 2, [128, 16 * 256], BF16)
        wds = rot(ph, "wdb", 2, [128, 2 * 2048], BF16)
        sgs = rot(ph, "sg", 2, [128, 512], F32)
        acts = rot(ph, "actT", 4, [128, 512], BF16)
        sts = rot(ph, "est", 2, [128, 4], F32)

        for tb in range(NTH // 4):
            c0 = tb * 512
            for hf in range(2):
                DMA(oTa[:, hf * 4096:(hf + 1) * 4096].rearrange("p (c t) -> p c t", c=8),
                    OTFc(hf * 1024, (hf + 1) * 1024, c0, c0 + 512).rearrange("(c p) t -> p c t", p=128), writes=[BoTa])
                DMA(oTb[:, hf * 4096:(hf + 1) * 4096].rearrange("p (c t) -> p c t", c=8),
                    OTFc(hf * 1024, (hf + 1) * 1024, SH + c0, SH + c0 + 512).rearrange("(c p) t -> p c t", p=128), writes=[BoTb])
            OP("pool", "tensor_scalar", oTa[:, :], oTa[:, :], selt[:, 0:1], None, op0=ALU.mult, reads=[BoTa, Bsel], writes=[BoTa])
            OP("dve", "scalar_tensor_tensor", oTa[:, :], oTb[:, :], selt[:, 1:2], oTa[:, :], op0=ALU.mult, op1=ALU.add,
               reads=[BoTa, BoTb, Bsel], writes=[BoTa])
            for i in range(4):
                ha, Bha = haccs[i]
                DMA(ha[:], xh[c0 + i * 128:c0 + (i + 1) * 128, :], writes=[Bha])
            for n in range(8):
                wob, Bwob = wobs.next()
                DMA(wob[:, :].rearrange("p (c n) -> p c n", c=16), WO2[:, n * 256:(n + 1) * 256].rearrange("(c p) n -> p c n", p=128), writes=[Bwob])
                for i in range(4):
                    ha, Bha = haccs[i]
                    ps, Bps = pf.next()
                    for c in range(16):
                        MM(ps[:, 0:256], oTa[:, c * 512 + i * 128:c * 512 + (i + 1) * 128], wob[:, c * 256:(c + 1) * 256],
                           c == 0, c == 15, [BoTa, Bwob], [Bps])
                    OP("dve", "tensor_tensor", ha[:, n * 256:(n + 1) * 256], ps[:, 0:256], ha[:, n * 256:(n + 1) * 256], ALU.add,
                       reads=[Bps, Bha], writes=[Bha])
            for i in range(4):
                ha, Bha = haccs[i]
                st, Bst = sts.next()
                OP("act", "activation", u[:], ha[:], AF.Square, accum_out=st[:, 0:1], reads=[Bha], writes=[Bu, Bst])
                OP("act", "activation", st[:, 1:2], st[:, 0:1], AF.Ln, bias=C("eps"), scale=1.0 / D, reads=[Bst, Bcf], writes=[Bst])
                OP("act", "activation", st[:, 2:3], st[:, 1:2], AF.Exp, scale=-0.5, reads=[Bst], writes=[Bst])
                OP("dve", "scalar_tensor_tensor", u[:], ha[:], st[:, 2:3], fnwt[:], op0=ALU.mult, op1=ALU.mult,
                   reads=[Bha, Bst, Bfnw], writes=[Bu])
                for half in range(2):
                    tp, Btp = pb.next()
                    for c8 in range(8):
                        c = half * 8 + c8
                        OP("pe", "transpose", tp[:, c8 * 128:(c8 + 1) * 128], u[:, c * 128:(c + 1) * 128], C("ident", True),
                           reads=[Bu, Bcb], writes=[Btp])
                    dstv = uTv[:, half * 8:half * 8 + 8, i * 128:(i + 1) * 128]
                    srcv = tp[:, :].rearrange("p (c t) -> p c t", c=8)
                    if half == 0:
                        OP("act", "activation", dstv, srcv, AF.Copy, reads=[Btp], writes=[BuT])
                    else:
                        OP("dve", "tensor_copy", dstv, srcv, reads=[Btp], writes=[BuT])
            for gi in range(22):
                wgb, Bwg = wgs.next()
                wub, Bwu = wus.next()
                wdb, Bwd = wds.next()
                DMA(wgb[:, :], WG2[gi, :, :], writes=[Bwg])
                DMA(wub[:, :], WU2[gi, :, :], writes=[Bwu])
                DMA(wdb[:, :].rearrange("p (k n) -> p k n", k=2), WD2[gi * 256:(gi + 1) * 256, :].rearrange("(k p) n -> p k n", p=128), writes=[Bwd])
                ak = []
                for k in range(2):
                    pg, Bpg = pf.next()
                    for c in range(16):
                        MM(pg[:, :], wgb[:, c * 256 + k * 128:c * 256 + (k + 1) * 128], uT[:, c * 512:(c + 1) * 512], c == 0, c == 15, [Bwg, BuT], [Bpg])
                    pu, Bpu = pf.next()
                    for c in range(16):
                        MM(pu[:, :], wub[:, c * 256 + k * 128:c * 256 + (k + 1) * 128], uT[:, c * 512:(c + 1) * 512], c == 0, c == 15, [Bwu, BuT], [Bpu])
                    sg, Bsg = sgs.next()
                    OP("act", "activation", sg[:, :], pg[:, :], AF.Silu, reads=[Bpg], writes=[Bsg])
                    at, Bat = acts.next()
                    OP("dve", "tensor_tensor", at[:, :], sg[:, :], pu[:, :], ALU.mult, reads=[Bsg, Bpu], writes=[Bat])
                    ak.append((at, Bat))
                for i in range(4):
                    ha, Bha = haccs[i]
                    for n in range(4):
                        ps, Bps = pf.next()
                        for k in range(2):
                            at, Bat = ak[k]
                            MM(ps[:, :], at[:, i * 128:(i + 1) * 128], wdb[:, k * 2048 + n * 512:k * 2048 + (n + 1) * 512], k == 0, k == 1, [Bat, Bwd], [Bps])
                        OP("dve", "tensor_tensor", ha[:, n * 512:(n + 1) * 512], ps[:, :], ha[:, n * 512:(n + 1) * 512], ALU.add,
                           reads=[Bps, Bha], writes=[Bha])
            for i in range(4):
                ha, Bha = haccs[i]
                DMA(out[c0 + i * 128:c0 + (i + 1) * 128, :], ha[:], reads=[Bha])
        P.flush()


OFF = {"aq": 0, "ak": 1024, "av": 2048, "dq": 3072, "dk": 4096, "dv": 5120, "dz": 6144, "db": 7168, "da": 7176}


def core_inputs(inp, c, NT):
    b, j = c // 2, c % 2
    L = NT * 128
    f = np.float32
    ah = [2 * s + j for s in range(4)]
    dh = [4 * j + k for k in range(4)]
    slopes = [2.0 ** -(h + 1) for h in ah]
    w_in = inp["w_in"][0]

    def cols(base, heads, w=128):
        return np.concatenate([w_in[:, base + h * w:base + (h + 1) * w] for h in heads], axis=1)

    wA = np.concatenate([cols(OFF["aq"], ah), cols(OFF["ak"], ah), cols(OFF["av"], ah), cols(OFF["dz"], dh),
                         cols(OFF["db"], dh, 1), cols(OFF["da"], dh, 1)], axis=1)
    wD = np.concatenate([cols(OFF["dq"], dh), cols(OFF["dk"], dh), cols(OFF["dv"], dh)], axis=1)
    w_out = inp["w_out"][0]
    rows = []
    for r in range(2):
        rows += [w_out[(2 * s + r) * 128:(2 * s + r + 1) * 128] for s in range(4)]
        rows += [w_out[1024 + (4 * r + k) * 128:1024 + (4 * r + k + 1) * 128] for k in range(4)]
    wo = np.concatenate(rows, axis=0)
    small = np.zeros((128, 572), f)
    small[:, 0] = np.tile(inp["q_norm_w"][0], 2)
    small[:, 1] = np.tile(inp["k_norm_w"][0], 2)
    small[:, 2:258] = np.concatenate([inp["lambda_q1"][0], inp["lambda_k1"][0], inp["lambda_q2"][0], inp["lambda_k2"][0]])[None, :]
    small[:, 258:386] = inp["subln_w"][0][None, :]
    small[:, 386:514] = inp["o_norm_w"][0][None, :]
    cw = inp["conv_w"][0]
    for typ in range(3):
        for k, h in enumerate(dh):
            s = typ * 4 + k
            small[:, 514 + s * 4:514 + s * 4 + 4] = cw[typ * 1024 + h * 128:typ * 1024 + (h + 1) * 128, :]
    small[:, 562:566] = inp["a_log"][0][dh][None, :]
    small[:, 566:570] = inp["dt_bias"][0][dh][None, :]
    aug = np.zeros((4, 2, 3, L), f)
    tok = np.arange(L)
    tt, ti = tok // 128, tok % 128
    real = (tok >= 128).astype(f)
    for s, sl in enumerate(slopes):
        aug[s, 0, 0] = real
        aug[s, 0, 1] = real
        aug[s, 0, 2] = real * 8.0 * sl * 128.0 * tt
        aug[s, 1, 0] = -8.0 * sl * 128.0 * tt
        aug[s, 1, 1] = -8.0 * sl * ti
        aug[s, 1, 2] = 1.0
    S = L - 128
    return {
        "x": np.ascontiguousarray(inp["x"][b][:S]),
        "meta": np.ascontiguousarray(inp["meta_tokens"]),
        "anw": np.ascontiguousarray(np.broadcast_to(inp["attn_norm_w"][0][None, :], (128, D))),
        "fnw": np.ascontiguousarray(np.broadcast_to(inp["ffn_norm_w"][0][None, :], (128, D))),
        "wA": np.ascontiguousarray(wA), "wD": np.ascontiguousarray(wD), "wo": np.ascontiguousarray(wo),
        "wg": np.ascontiguousarray(inp["w_gate"][0]), "wu": np.ascontiguousarray(inp["w_up"][0]),
        "wd": np.ascontiguousarray(inp["w_down"][0]),
        "cst": make_consts(slopes), "small": small, "aug": aug,
        "xh": np.ascontiguousarray(inp["x"][b][j * (S // 2):(j + 1) * (S // 2)]),
        "sel": np.ascontiguousarray(np.broadcast_to(np.array([1.0 - j, float(j)], f)[None, :], (128, 2))),
    }


_NC_CACHE = {}


def kernel(**inputs):
    inp = {k: np.asarray(v) for k, v in inputs.items()}
    B, S, _ = inp["x"].shape
    NT = S // 128 + 1
    if NT not in _NC_CACHE:
        _NC_CACHE[NT] = build_program(NT)
    nc = _NC_CACHE[NT]
    in_maps = [core_inputs(inp, c, NT) for c in range(8)]
    res = run_bass_kernel_spmd(nc, in_maps, core_ids=list(range(8)))
    SH = S // 2
    out = np.empty((B, S, D), np.float32)
    for c in range(8):
        b, j = c // 2, c % 2
        out[b, j * SH:(j + 1) * SH] = res.results[c]["out"]
    return out
```

```python
import math
import numpy as np
from contextlib import ExitStack
import concourse.bass as bass
import concourse.mybir as mybir
from concourse.bass_utils import run_bass_kernel_spmd

F32 = mybir.dt.float32
BF16 = mybir.dt.bfloat16
AF = mybir.ActivationFunctionType
ALU = mybir.AluOpType

D = 2048
HID = 5632
EPS = 1e-6
LAMBDA_INIT = 0.8 - 0.6 * math.exp(0.0)
ENGS = ("pe", "act", "dve", "pool", "sp")
ALIBI_CUT = 60.0


class Buf:
    __slots__ = ("name", "w", "r", "dsem", "dcnt")

    def __init__(self, name):
        self.name = name
        self.w = None
        self.r = []
        self.dsem = None
        self.dcnt = 0


class Op:
    __slots__ = ("eng", "fn", "raw", "oth", "sig", "cnt", "isdma", "sem", "ndma")

    def __init__(self, eng, fn, isdma=False):
        self.eng = eng
        self.fn = fn
        self.raw = set()
        self.oth = set()
        self.sig = False
        self.cnt = 0
        self.isdma = isdma
        self.sem = None
        self.ndma = 0


class Prog:
    def __init__(self, nc, stack):
        self.nc = nc
        self.stack = stack
        self.esem = {e: stack.enter_context(nc.semaphore("es_" + e)) for e in ENGS}
        self.ecnt = {e: 0 for e in ENGS}
        self.waited = {e: {} for e in ENGS}
        self.ops = {e: [] for e in ENGS}
        self.dma_bufs = []
        self.nsem = 0

    def _deps(self, o, reads, writes):
        for b in reads:
            if b.w is not None:
                o.raw.add(b.w)
        for b in writes:
            if b.w is not None:
                o.oth.add(b.w)
            for r in b.r:
                o.oth.add(r)
        o.raw.discard(o)
        o.oth.discard(o)
        for b in reads:
            b.r.append(o)
        for b in writes:
            b.w = o
            b.r = []

    def op(self, eng, fn, reads=(), writes=()):
        o = Op(eng, fn)
        self._deps(o, reads, writes)
        self.ops[eng].append(o)
        return o

    def dma(self, fn, reads=(), writes=(), n=1, key=None, eng="sp"):
        o = Op(eng, fn, isdma=True)
        o.ndma = n
        self._deps(o, reads, writes)
        if key is None:
            key = writes[0] if writes else reads[0]
        if key.dsem is None:
            key.dsem = self.stack.enter_context(self.nc.semaphore("ds%d" % self.nsem))
            self.nsem += 1
            self.dma_bufs.append(key)
        key.dcnt += 16 * n
        o.sem = key.dsem
        o.cnt = key.dcnt
        self.ops[eng].append(o)
        return o

    def flush(self):
        nc = self.nc
        ops = self.ops
        for e in ENGS:
            for o in ops[e]:
                for d in o.raw | o.oth:
                    if d.isdma:
                        continue
                    if d.eng == e and (e in ("pe", "sp") or d not in o.raw):
                        continue
                    d.sig = True
        for e in ENGS:
            for o in reversed(ops[e]):
                if not o.isdma:
                    o.sig = True
                    break
        for e in ENGS:
            c = self.ecnt[e]
            for o in ops[e]:
                if o.isdma:
                    continue
                if o.sig:
                    c += 1
                o.cnt = c
            self.ecnt[e] = c
        final_e = dict(self.ecnt)
        final_d = [(b.dsem, b.dcnt) for b in self.dma_bufs]
        esem = self.esem
        waited = self.waited

        def emit(e, eng):
            wd = waited[e]

            def wait(sem, val):
                k = id(sem)
                if wd.get(k, 0) < val:
                    eng.wait_ge(sem, val)
                    wd[k] = val

            for o in ops[e]:
                need = {}
                for d in o.raw | o.oth:
                    if d.isdma:
                        s, v = d.sem, d.cnt
                    else:
                        if d.eng == e and (e in ("pe", "sp") or d not in o.raw):
                            continue
                        s, v = esem[d.eng], d.cnt
                    k = id(s)
                    if k not in need or need[k][1] < v:
                        need[k] = (s, v)
                for s, v in need.values():
                    wait(s, v)
                r = o.fn(eng)
                if o.isdma:
                    assert len(r) == o.ndma, (len(r), o.ndma)
                    for ins in r:
                        ins.then_inc(o.sem, 16)
                elif o.sig:
                    r.then_inc(esem[e], 1)
            if e == "sp":
                for s, v in final_d:
                    wait(s, v)
                eng.sem_inc(esem["sp"], 1)
            for e2 in ENGS:
                v = final_e[e2] + (1 if e2 == "sp" else 0)
                if e2 == e:
                    wd[id(esem[e2])] = v
                    continue
                wait(esem[e2], v)

        with nc.Block() as block:
            @block.tensor
            def _(eng):
                emit("pe", eng)

            @block.scalar
            def _(eng):
                emit("act", eng)

            @block.vector
            def _(eng):
                emit("dve", eng)

            @block.gpsimd
            def _(eng):
                emit("pool", eng)

            @block.sync
            def _(eng):
                emit("sp", eng)
        self.ecnt["sp"] += 1
        self.ops = {e: [] for e in ENGS}


class Rot:
    def __init__(self, items):
        self.items = items
        self.i = 0

    def next(self):
        it = self.items[self.i % len(self.items)]
        self.i += 1
        return it


def _cst_layout():
    off = {}
    o = 0
    for name, w in (("ident", 128), ("ones", 128), ("utf", 128), ("su4", 512), ("iu4", 512), ("valid", 1),
                    ("kbias", 4), ("eps", 1), ("zero", 1), ("one", 1), ("F32END", 0),
                    ("blk64", 128), ("d32", 512), ("o1t", 512), ("o2t", 512), ("i4", 512), ("cneg", 128)):
        off[name] = (o, w)
        o += w
    return off, o


CST, NCST = _cst_layout()
NF32 = CST["F32END"][0]


def make_consts(slopes4):
    c = np.zeros((128, NCST), np.float32)

    def put(name, arr):
        o, w = CST[name]
        c[:, o:o + w] = arr

    p = np.arange(128)
    put("ident", np.eye(128))
    put("blk64", ((p[:, None] // 64) == (p[None, :] // 64)) / 64.0)
    put("ones", np.ones((128, 128)))
    put("utf", (p[:, None] <= p[None, :]))
    su = (p[None, :] > p[:, None]).astype(np.float32)
    iu = (p[None, :] >= p[:, None]).astype(np.float32)
    put("su4", np.tile(su, (1, 4)))
    put("iu4", np.tile(iu, (1, 4)))
    blk = p // 32
    d32 = (blk[:, None] == blk[None, :]).astype(np.float32)
    put("d32", np.tile(d32, (1, 4)))
    o1t = ((blk[:, None] == blk[None, :] + 1) & (blk[:, None] % 2 == 1)).astype(np.float32)
    o2t = ((blk[:, None] >= 2) & (blk[None, :] < 2)).astype(np.float32)
    put("o1t", np.tile(o1t, (1, 4)))
    put("o2t", np.tile(o2t, (1, 4)))
    put("i4", np.tile(np.eye(128), (1, 4)))
    put("cneg", np.where(p[:, None] > p[None, :], -30000.0, 0.0))
    put("valid", (p >= 112).astype(np.float32)[:, None])
    put("kbias", p[:, None] * np.asarray(slopes4, np.float32)[None, :])
    put("eps", np.full((128, 1), EPS))
    put("one", np.ones((128, 1)))
    return c


def build_program(NT, debug=False, upto=99):
    L = NT * 128
    S = L - 128
    NTH = (NT - 1) // 2
    SH = NTH * 128
    nc = bass.Bass("TRN2", target_bir_lowering=False)

    def din(name, shape, dt=F32):
        return nc.dram_tensor(name, list(shape), dt, kind="ExternalInput").ap()

    def dscr(name, shape, dt, out=False):
        return nc.dram_tensor(name, list(shape), dt, kind="ExternalOutput" if (out and debug) else "Internal").ap()

    x = din("x", [S, D])
    meta = din("meta", [16, D])
    anw = din("anw", [128, D])
    fnw = din("fnw", [128, D])
    wA = din("wA", [D, 2056])
    wDn = din("wD", [D, 1536])
    wo = din("wo", [D, D])
    wg = din("wg", [D, HID])
    wu = din("wu", [D, HID])
    wd = din("wd", [HID, D])
    cst = din("cst", [128, NCST])
    small = din("small", [128, 572])
    aug = din("aug", [4, 2, 3, L])
    xh = din("xh", [SH, D])
    sel = din("sel", [128, 2])
    out = nc.dram_tensor("out", [SH, D], F32, kind="ExternalOutput").ap()

    QT = dscr("QT", [4, 128, L], BF16, True)
    KT = dscr("KT", [4, 128, L], BF16, True)
    VV = dscr("VV", [L, 512], BF16, True)
    DQ = dscr("DQ", [4, 128, L], BF16, True)
    DK = dscr("DK", [4, 128, L], BF16, True)
    DV = dscr("DV", [4, 128, L], BF16, True)
    ZZ = dscr("ZZ", [L, 512], BF16, True)
    PW = min(1024, S)
    NPC = S // PW
    OTs = [dscr("OT%d" % k, [1024, PW], BF16, debug and upto < 4) for k in range(NPC)]
    OTFs = [dscr("OTF%d" % k, [2048, PW], BF16) for k in range(NPC)]

    def OTc(r0, r1, c0, c1):
        k = c0 // PW
        assert (c1 - 1) // PW == k
        return OTs[k][r0:r1, c0 - k * PW:c1 - k * PW]

    def OTFc(r0, r1, c0, c1):
        k = c0 // PW
        assert (c1 - 1) // PW == k
        return OTFs[k][r0:r1, c0 - k * PW:c1 - k * PW]
    WG2 = dscr("WG2", [22, 128, 16 * 256], BF16)
    WU2 = dscr("WU2", [22, 128, 16 * 256], BF16)
    WD2 = dscr("WD2", [HID, D], BF16)
    WO2 = dscr("WO2", [D, D], BF16)
    BGo = nc.dram_tensor("BGo", [128, NT * 8], F32, kind="ExternalOutput").ap() if debug else None

    SM = {"qkw": (0, 2), "lam": (2, 256), "subw": (258, 128), "onw": (386, 128), "convw": (514, 48),
          "alog": (562, 4), "dtb": (566, 4)}

    with ExitStack() as outer:
        P = Prog(nc, outer)

        def OP(eng, method, *args, reads=(), writes=(), **kw):
            return P.op(eng, lambda e: getattr(e, method)(*args, **kw), reads, writes)

        def MM(out_, lhsT, rhs, start, stop, reads, writes):
            return P.op("pe", lambda e: e.matmul(out_, lhsT, rhs, start=start, stop=stop, skip_group_check=True), reads, writes)

        def DMA(out_, in_, reads=(), writes=(), eng="sp", key=None):
            return P.dma(lambda e: [e.dma_start(out=out_, in_=in_)], reads, writes, 1, key, eng)

        uniq = [0]

        def sbuf(stack, name, shape, dt):
            uniq[0] += 1
            return stack.enter_context(nc.sbuf_tensor("%s_%d" % (name, uniq[0]), list(shape), dt))

        def psum(stack, name, shape, dt):
            uniq[0] += 1
            return stack.enter_context(nc.psum_tensor("%s_%d" % (name, uniq[0]), list(shape), dt))

        def rot(stack, name, n, shape, dt, ps=False):
            mk = psum if ps else sbuf
            return Rot([(mk(stack, "%s%d" % (name, i), shape, dt), Buf("%s%d" % (name, i))) for i in range(n)])

        cf = sbuf(outer, "cf", [128, NF32], F32)
        cb = sbuf(outer, "cb", [128, NCST], BF16)
        sm = sbuf(outer, "sm", [128, 572], F32)
        bg = sbuf(outer, "bg", [128, NT * 8], F32)
        lamt = sbuf(outer, "lamt", [128, 8], F32)
        negA = sbuf(outer, "negA", [128, 4], F32)
        Bcf, Bcb, Bsm, Bbg, Blam, BnegA = [Buf(n) for n in ("cf", "cb", "sm", "bg", "lam", "negA")]

        def C(name, bf=False, lo=0, hi=None):
            o, w = CST[name]
            t = cb if bf else cf
            if not bf:
                assert o + w <= NF32, name
            return t[:, o + lo:o + (w if hi is None else hi)]

        def SMc(name, lo=0, hi=None):
            o, w = SM[name]
            return sm[:, o + lo:o + (w if hi is None else hi)]

        conv_jobs = []
        for c in range(16):
            for hf in range(2):
                for (wsrc_, wdst_) in ((wg, WG2), (wu, WU2)):
                    conv_jobs.append((wsrc_[c * 128:(c + 1) * 128, hf * 2816:(hf + 1) * 2816],
                                      wdst_[hf * 11:(hf + 1) * 11, :, c * 256:(c + 1) * 256].rearrange("g p k -> p g k"), 2816, 11))
        for r in range(HID // 128):
            conv_jobs.append((wd[r * 128:(r + 1) * 128, :], WD2[r * 128:(r + 1) * 128, :], 2048, 0))
        for r in range(16):
            conv_jobs.append((wo[r * 128:(r + 1) * 128, :], WO2[r * 128:(r + 1) * 128, :], 2048, 0))
        conv_state = {"i": 0}

        with ExitStack() as ph:
            ctmp = sbuf(ph, "ctmp", [128, NCST], F32)
            Bct = Buf("ctmp")
            DMA(ctmp[:], cst[:, :], writes=[Bct])
            DMA(cf[:], cst[:, 0:NF32], writes=[Bcf])
            DMA(sm[:], small[:, :], writes=[Bsm])
            OP("pool", "tensor_copy", cb[:], ctmp[:], reads=[Bct], writes=[Bcb])
            lo = SM["lam"][0]
            tmp = sbuf(ph, "ltmp", [128, 64], F32)
            Bt = Buf("ltmp")
            for k in range(2):
                OP("dve", "tensor_tensor", tmp[:], sm[:, lo + k * 128:lo + k * 128 + 64],
                   sm[:, lo + k * 128 + 64:lo + k * 128 + 128], ALU.mult, reads=[Bsm], writes=[Bt])
                OP("dve", "tensor_reduce", lamt[:, k:k + 1], tmp[:], mybir.AxisListType.X, ALU.add, reads=[Bt], writes=[Blam])
            OP("act", "activation", lamt[:, 2:4], lamt[:, 0:2], AF.Exp, reads=[Blam], writes=[Blam])
            OP("dve", "tensor_tensor", lamt[:, 4:5], lamt[:, 3:4], lamt[:, 2:3], ALU.subtract, reads=[Blam], writes=[Blam])
            OP("dve", "tensor_scalar", lamt[:, 5:6], lamt[:, 4:5], -LAMBDA_INIT, None, op0=ALU.add, reads=[Blam], writes=[Blam])
            OP("act", "activation", negA[:], SMc("alog"), AF.Exp, reads=[Bsm], writes=[BnegA])
            OP("dve", "tensor_scalar", negA[:], negA[:], -1.0, None, op0=ALU.mult, reads=[BnegA], writes=[BnegA])
            P.flush()

        blocks = [(0, 1)] + [(1 + 4 * i, 4) for i in range((NT - 1) // 4)]
        assert (NT - 1) % 4 == 0

        def phase_A(pass_id):
            CW = 2056 if pass_id == 0 else 1536
            wsrc = wA if pass_id == 0 else wDn
            with ExitStack() as ph:
                wres = sbuf(ph, "wres", [128, 16 * CW], BF16)
                Bw = Buf("wres")
                xts = rot(ph, "xt", 2, [128, D], F32)
                if pass_id == 0:
                    stR = rot(ph, "stg", 2, [128, 2816], F32)
                    stbR = rot(ph, "stgb", 2, [128, 2816], BF16)
                else:
                    stR = xts
                for c in range(16):
                    st, Bs = stR.next()
                    DMA(st[:, 0:CW], wsrc[c * 128:(c + 1) * 128, :], writes=[Bs])
                    OP("pool", "tensor_copy", wres[:, c * CW:(c + 1) * CW], st[:, 0:CW], reads=[Bs], writes=[Bw])

                def convert_some(n):
                    for _ in range(n):
                        if conv_state["i"] >= len(conv_jobs):
                            return
                        src, dst, w, g = conv_jobs[conv_state["i"]]
                        conv_state["i"] += 1
                        st, Bs = stR.next()
                        sb_, Bb = stbR.next()
                        DMA(st[:, 0:w], src, writes=[Bs])
                        OP("pool", "tensor_copy", sb_[:, 0:w], st[:, 0:w], reads=[Bs], writes=[Bb])
                        if g:
                            DMA(dst, sb_[:, 0:w].rearrange("p (g k) -> p g k", g=g), reads=[Bb])
                        else:
                            DMA(dst, sb_[:, 0:w], reads=[Bb])

                us = rot(ph, "u", 5, [128, D], BF16)
                uTs = rot(ph, "uT", 2, [128, 16 * 512], BF16)
                sts = rot(ph, "st", 2, [128, 4], F32)
                anwt = sbuf(ph, "anwt", [128, D], F32)
                Banw = Buf("anwt")
                DMA(anwt[:], anw[:, :], writes=[Banw])
                tps = rot(ph, "tp", 2, [128, 1024], BF16, ps=True)
                mms = rot(ph, "mm", 4, [128, 512], F32, ps=True)
                aux = rot(ph, "aux", 2, [128, 512], F32, ps=True)
                sqs = rot(ph, "sq", 3, [128, 512], BF16)
                rrs = rot(ph, "rr", 2, [128, 512], F32)
                obs = rot(ph, "ob", 3, [128, 512], BF16)
                if pass_id == 0:
                    bat = rot(ph, "bat", 2, [128, 16], F32)
                else:
                    cbuf = sbuf(ph, "cbuf", [128, 12 * 515], F32)
                    Bcbuf = [Buf("cbuf%d" % i) for i in range(12)]
                    OP("pool", "memset", cbuf[:], 0.0, writes=Bcbuf)
                    accs = rot(ph, "acc", 2, [128, 512], F32)
                    sls = rot(ph, "sl", 3, [128, 512], F32)

                def stage1(bi):
                    t0, nt = blocks[bi]
                    ul = []
                    for i in range(nt):
                        ti = t0 + i
                        xt, Bx = xts.next()
                        if ti == 0:
                            OP("pool", "memset", xt[:], 0.0, writes=[Bx])
                            DMA(xt[112:128, :], meta[:, :], writes=[Bx])
                        else:
                            DMA(xt[:], x[(ti - 1) * 128:ti * 128, :], writes=[Bx])
                        st, Bst = sts.next()
                        u, Bu = us.next()
                        OP("act", "activation", u[:], xt[:], AF.Square, accum_out=st[:, 0:1], reads=[Bx], writes=[Bu, Bst])
                        OP("act", "activation", st[:, 1:2], st[:, 0:1], AF.Ln, bias=C("eps"), scale=1.0 / D, reads=[Bst, Bcf], writes=[Bst])
                        OP("act", "activation", st[:, 2:3], st[:, 1:2], AF.Exp, scale=-0.5, reads=[Bst], writes=[Bst])
                        OP("dve", "scalar_tensor_tensor", u[:], xt[:], st[:, 2:3], anwt[:], op0=ALU.mult, op1=ALU.mult,
                           reads=[Bx, Bst, Banw], writes=[Bu])
                        ul.append((u, Bu))
                    return ul

                def stage2(bi, ul):
                    uT, BuT = uTs.next()
                    uTv = uT[:, :].rearrange("p (c t) -> p c t", c=16)
                    for i, (u, Bu) in enumerate(ul):
                        for half in range(2):
                            tp, Btp = tps.next()
                            for c8 in range(8):
                                c = half * 8 + c8
                                OP("pe", "transpose", tp[:, c8 * 128:(c8 + 1) * 128], u[:, c * 128:(c + 1) * 128], C("ident", True),
                                   reads=[Bu, Bcb], writes=[Btp])
                            dstv = uTv[:, half * 8:half * 8 + 8, i * 128:(i + 1) * 128]
                            srcv = tp[:, :].rearrange("p (c t) -> p c t", c=8)
                            if half == 0:
                                OP("act", "activation", dstv, srcv, AF.Copy, reads=[Btp], writes=[BuT])
                            else:
                                OP("dve", "tensor_copy", dstv, srcv, reads=[Btp], writes=[BuT])
                    return uT, BuT

                cur_uT = stage2(0, stage1(0))
                for bi, (t0, nt) in enumerate(blocks):
                    T = nt * 128
                    uT, BuT = cur_uT

                    def fm_block(col0):
                        mm, Bmm = mms.next()
                        for c in range(16):
                            MM(mm[:, 0:T], wres[:, c * CW + col0:c * CW + col0 + 128], uT[:, c * 512:c * 512 + T],
                               c == 0, c == 15, [Bw, BuT], [Bmm])
                        return mm, Bmm

                    def tm_block(i, col0, cw):
                        mm, Bmm = mms.next()
                        for c in range(16):
                            MM(mm[:, 0:cw], uT[:, c * 512 + i * 128:c * 512 + (i + 1) * 128], wres[:, c * CW + col0:c * CW + col0 + cw],
                               c == 0, c == 15, [Bw, BuT], [Bmm])
                        return mm, Bmm

                    def norm1(mm, Bmm, src_sb, Bsrc):
                        sq, Bsq = sqs.next()
                        if src_sb is None:
                            OP("act", "activation", sq[:, 0:T], mm[:, 0:T], AF.Square, reads=[Bmm], writes=[Bsq])
                        else:
                            OP("pool", "tensor_tensor", sq[:, 0:T], src_sb[:, 0:T], src_sb[:, 0:T], ALU.mult, reads=[Bsrc], writes=[Bsq])
                        return sq, Bsq

                    def norm2(sq, Bsq, mm, Bmm, src_sb, Bsrc, onesname, lnscale, wcol, dst):
                        ax, Bax = aux.next()
                        MM(ax[:, 0:T], C(onesname, True), sq[:, 0:T], True, True, [Bsq, Bcb], [Bax])
                        rr, Brr = rrs.next()
                        OP("act", "activation", rr[:, 0:T], ax[:, 0:T], AF.Ln, bias=C("eps"), scale=lnscale, reads=[Bax, Bcf], writes=[Brr])
                        OP("act", "activation", rr[:, 0:T], rr[:, 0:T], AF.Exp, scale=-0.5, reads=[Brr], writes=[Brr])
                        ob, Bob = obs.next()
                        if src_sb is None:
                            OP("dve", "scalar_tensor_tensor", ob[:, 0:T], mm[:, 0:T], wcol, rr[:, 0:T], op0=ALU.mult, op1=ALU.mult,
                               reads=[Bmm, Brr, Bsm], writes=[Bob])
                        else:
                            OP("dve", "tensor_tensor", ob[:, 0:T], src_sb[:, 0:T], rr[:, 0:T], ALU.mult, reads=[Bsrc, Brr], writes=[Bob])
                        DMA(dst, ob[:, 0:T], reads=[Bob])

                    units = []
                    if pass_id == 0:
                        for qk in range(2):
                            for h in range(4):
                                def unit(qk=qk, h=h):
                                    mm, Bmm = fm_block(qk * 512 + h * 128)
                                    dstT = (QT if qk == 0 else KT)[h, :, t0 * 128:t0 * 128 + T]
                                    sq, Bsq = norm1(mm, Bmm, None, None)
                                    return lambda: norm2(sq, Bsq, mm, Bmm, None, None, "blk64", 1.0, SMc("qkw", qk, qk + 1), dstT)
                                units.append(unit)
                        for i in range(nt):
                            for (col0, dstD) in ((1024, VV), (1536, ZZ)):
                                def unit(i=i, col0=col0, dstD=dstD):
                                    ti = t0 + i
                                    mm, Bmm = tm_block(i, col0, 512)
                                    ob, Bob = obs.next()
                                    OP("act", "activation", ob[:, :], mm[:, :], AF.Copy, reads=[Bmm], writes=[Bob])
                                    DMA(dstD[ti * 128:(ti + 1) * 128, :], ob[:, :], reads=[Bob])
                                    return None
                                units.append(unit)

                            def unit(i=i):
                                ti = t0 + i
                                mm, Bmm = tm_block(i, 2048, 8)
                                bt, Bbt = bat.next()
                                OP("act", "activation", bt[:, 0:4], mm[:, 0:4], AF.Exp, scale=-1.0, reads=[Bmm], writes=[Bbt])
                                OP("dve", "tensor_scalar", bt[:, 0:4], bt[:, 0:4], 1.0, None, op0=ALU.add, reads=[Bbt], writes=[Bbt])
                                OP("dve", "reciprocal", bg[:, ti * 8:ti * 8 + 4], bt[:, 0:4], reads=[Bbt], writes=[Bbg])
                                OP("dve", "tensor_tensor", bt[:, 4:8], mm[:, 4:8], SMc("dtb"), ALU.add, reads=[Bmm, Bsm], writes=[Bbt])
                                OP("act", "activation", bt[:, 8:12], bt[:, 4:8], AF.Exp, reads=[Bbt], writes=[Bbt])
                                OP("act", "activation", bt[:, 12:16], bt[:, 8:12], AF.Ln, bias=C("one"), reads=[Bbt, Bcf], writes=[Bbt])
                                OP("dve", "tensor_tensor", bg[:, ti * 8 + 4:ti * 8 + 8], bt[:, 12:16], negA[:], ALU.mult,
                                   reads=[Bbt, BnegA], writes=[Bbg])
                                if ti == 0:
                                    OP("dve", "tensor_scalar", bg[:, 0:8], bg[:, 0:8], C("valid"), None, op0=ALU.mult, reads=[Bbg, Bcf], writes=[Bbg])
                                return None
                            units.append(unit)
                    else:
                        for typ in range(3):
                            for h in range(4):
                                def unit(typ=typ, h=h):
                                    s = typ * 4 + h
                                    mm, Bmm = fm_block(typ * 512 + h * 128)
                                    cbs = cbuf[:, s * 515:(s + 1) * 515]
                                    Bc = Bcbuf[s]
                                    OP("act", "activation", cbs[:, 3:3 + T], mm[:, 0:T], AF.Copy, reads=[Bmm], writes=[Bc])
                                    acc, Bacc = accs.next()
                                    cw0 = SM["convw"][0] + s * 4
                                    OP("dve", "tensor_scalar", acc[:, 0:T], cbs[:, 0:T], sm[:, cw0:cw0 + 1], None, op0=ALU.mult,
                                       reads=[Bc, Bsm], writes=[Bacc])
                                    for j in range(1, 4):
                                        OP("dve", "scalar_tensor_tensor", acc[:, 0:T], cbs[:, j:j + T], sm[:, cw0 + j:cw0 + j + 1], acc[:, 0:T],
                                           op0=ALU.mult, op1=ALU.add, reads=[Bc, Bsm, Bacc], writes=[Bacc])
                                    OP("pool", "tensor_copy", cbs[:, 0:3], cbs[:, T:T + 3], reads=[Bc], writes=[Bc])
                                    dstT = (DQ, DK, DV)[typ][h, :, t0 * 128:t0 * 128 + T]
                                    if typ == 2:
                                        ob, Bob = obs.next()
                                        OP("act", "activation", ob[:, 0:T], acc[:, 0:T], AF.Silu, reads=[Bacc], writes=[Bob])
                                        DMA(dstT, ob[:, 0:T], reads=[Bob])
                                        return None
                                    sl, Bsl = sls.next()
                                    OP("act", "activation", sl[:, 0:T], acc[:, 0:T], AF.Silu, reads=[Bacc], writes=[Bsl])
                                    sq, Bsq = norm1(None, None, sl, Bsl)
                                    return lambda: norm2(sq, Bsq, None, None, sl, Bsl, "ones", 128.0 if typ == 0 else 1.0, None, dstT)
                                units.append(unit)

                    prevpost = None
                    nxt_ul = None
                    for idx, unit in enumerate(units):
                        post2 = unit()
                        if prevpost is not None:
                            prevpost()
                        prevpost = post2
                        if idx == 2 and bi + 1 < len(blocks):
                            nxt_ul = stage1(bi + 1)
                    if bi + 1 < len(blocks):
                        if nxt_ul is None:
                            nxt_ul = stage1(bi + 1)
                        cur_uT = stage2(bi + 1, nxt_ul)
                    if prevpost is not None:
                        prevpost()
                    if pass_id == 0:
                        convert_some(7)
                while pass_id == 0 and conv_state["i"] < len(conv_jobs):
                    convert_some(4)
                if debug and pass_id == 0:
                    DMA(BGo[:, :], bg[:], reads=[Bbg])
                P.flush()

        phase_A(0)
        if upto >= 1:
            phase_A(1)
        env = dict(locals())
        if upto >= 2:
            build_rest(env)
    return nc


def build_rest(env):
    g = env
    nc, P, NT, L, S, debug, upto = g["nc"], g["P"], g["NT"], g["L"], g["S"], g["debug"], g["upto"]
    OP, MM, DMA, sbuf, psum, rot, C, SMc = g["OP"], g["MM"], g["DMA"], g["sbuf"], g["psum"], g["rot"], g["C"], g["SMc"]
    cf, cb, sm, bg, lamt = g["cf"], g["cb"], g["sm"], g["bg"], g["lamt"]
    Bcf, Bcb, Bsm, Bbg, Blam = g["Bcf"], g["Bcb"], g["Bsm"], g["Bbg"], g["Blam"]
    QT, KT, VV, DQ, DK, DV, ZZ, OTc = g["QT"], g["KT"], g["VV"], g["DQ"], g["DK"], g["DV"], g["ZZ"], g["OTc"]
    aug = g["aug"]
    NQB = (NT - 1) // 4

    with ExitStack() as ph:
        KTa = [rot(ph, "KTa%d" % m, 2, [67, L], BF16) for m in range(2)]
        QTb = [rot(ph, "QTb%d" % m, 3, [67, 512], BF16) for m in range(2)]
        agq = rot(ph, "agq", 2, [67, 512], F32)
        Vas = rot(ph, "Va", 2, [128, NT * 129], BF16)
        APW = 2080
        augs = rot(ph, "augst", 2, [67, APW], F32)
        Sps = rot(ph, "Sps", 3, [128, 512], F32, ps=True)
        Oas = [[(psum(ph, "Oa%d%d" % (m, pr), [128, 512], F32), Buf("Oa%d%d" % (m, pr))) for pr in range(2)] for m in range(2)]
        tpo = rot(ph, "tpo", 1, [128, 1024], BF16, ps=True)
        Pts = rot(ph, "Pt", 4, [128, 512], BF16)
        rrs = rot(ph, "arr", 4, [128, 8], F32)
        t1s = rot(ph, "at1", 2, [128, 128], F32)
        dds = rot(ph, "add", 2, [128, 128], F32)
        jks = rot(ph, "ajk", 1, [128, 128], BF16)
        oas = rot(ph, "aoa", 8, [128, 128], BF16)
        Osbs = rot(ph, "Osb", 4, [128, 385], F32)
        pending = []
        oTs = rot(ph, "aoT", 2, [128, 512], BF16)

        def load_slot(s):
            tiles = {}
            for m in range(2):
                kt_, Bk = KTa[m].next()
                DMA(kt_[0:64, :], KT[s, m * 64:(m + 1) * 64, :], writes=[Bk])
                for side, (tl, Bt) in enumerate(((kt_, Bk),)):
                    for a0 in range(0, L, APW):
                        aw = min(APW, L - a0)
                        ag, Bag = augs.next()
                        DMA(ag[64:67, 0:aw], aug[s, side, :, a0:a0 + aw], writes=[Bag])
                        OP("pool", "tensor_copy", tl[64:67, a0:a0 + aw], ag[64:67, 0:aw], reads=[Bag], writes=[Bt])
                tiles["k%d" % m] = (kt_, Bk)
            va, Bva = Vas.next()
            vav = va[:, :].rearrange("p (t d) -> p t d", d=129)
            vsrc = VV[:, s * 128:(s + 1) * 128].rearrange("(t p) d -> p t d", p=128)
            for t_0 in range(0, NT, 13):
                t_1 = min(NT, t_0 + 13)
                DMA(vav[:, t_0:t_1, 0:128], vsrc[:, t_0:t_1, :], writes=[Bva])
            OP("pool", "memset", vav[:, :, 128:129], 1.0, writes=[Bva])
            OP("pool", "memset", vav[0:112, 0:1, 128:129], 0.0, writes=[Bva])
            tiles["v"] = (va, Bva)
            return tiles

        nxt = load_slot(0)
        for s in range(4):
            cur = nxt
            if s + 1 < 4:
                nxt = load_slot(s + 1)
            va, Bva = cur["v"]
            vav = va[:, :].rearrange("p (t d) -> p t d", d=129)
            slope_min = 2.0 ** -(2 * s + 2)
            for qb in range(NQB):
                q0 = 1 + 4 * qb
                kts = [0]
                for kt in range(1, q0 + 4):
                    if slope_min * (128 * (q0 - kt) - 127) > ALIBI_CUT:
                        continue
                    kts.append(kt)
                started = {}
                qcur = []
                for m in range(2):
                    qtl, Bq = QTb[m].next()
                    DMA(qtl[0:64, :], QT[s, m * 64:(m + 1) * 64, q0 * 128:(q0 + 4) * 128], writes=[Bq])
                    ag, Bag = agq.next()
                    DMA(ag[64:67, :], aug[s, 1, :, q0 * 128:(q0 + 4) * 128], writes=[Bag])
                    OP("pool", "tensor_copy", qtl[64:67, :], ag[64:67, :], reads=[Bag], writes=[Bq])
                    qcur.append((qtl, Bq))
                steps = [(kt, m) for kt in kts for m in range(2)]

                def emit_qk(kt, m):
                    c0 = max(kt - q0, 0) * 128
                    ktl, Bk = cur["k%d" % m]
                    qtl, Bq = qcur[m]
                    Sp, BS = Sps.next()
                    diag = kt >= q0
                    MM(Sp[:, c0:512], ktl[0:67, kt * 128:(kt + 1) * 128], qtl[0:67, c0:512], True, not diag, [Bk, Bq], [BS])
                    if diag:
                        MM(Sp[:, c0:c0 + 128], C("ident", True), C("cneg", True), False, True, [Bcb], [BS])
                    return Sp, BS, c0

                def emit_exp(kt, m, Sp, BS, c0):
                    Pt, BP = Pts.next()
                    bias = C("zero") if kt == 0 else C("kbias", False, s, s + 1)
                    OP("act", "activation", Pt[:, c0:512], Sp[:, c0:512], AF.Exp, bias=bias, scale=0.125, reads=[BS, Bcf], writes=[BP])
                    return Pt, BP

                def emit_pv(kt, m, Pt, BP, c0):
                    for qi in range(c0 // 128, 4):
                        Oa, BO = Oas[m][qi // 2]
                        col = (qi % 2) * 256
                        key = (m, qi // 2)
                        first = key not in started
                        started[key] = True
                        MM(Oa[:, col:col + 129], Pt[:, qi * 128:(qi + 1) * 128], vav[:, kt, :], first, kt == q0 + qi, [BP, Bva], [BO])

                LA = 2
                qk = {}
                for n in range(min(LA, len(steps))):
                    qk[n] = emit_qk(*steps[n])
                for n in range(len(steps)):
                    Sp, BS, c0 = qk.pop(n)
                    Pt, BP = emit_exp(steps[n][0], steps[n][1], Sp, BS, c0)
                    if n + LA < len(steps):
                        qk[n + LA] = emit_qk(*steps[n + LA])
                    emit_pv(steps[n][0], steps[n][1], Pt, BP, c0)
                    if n == 3 and pending:
                        pending.pop()()
                Osb = []
                for m in range(2):
                    for pr in range(2):
                        Oa, BO = Oas[m][pr]
                        ob_, Bob_ = Osbs.next()
                        if pr == 0:
                            OP("act", "activation", ob_[:, 0:385], Oa[:, 0:385], AF.Copy, reads=[BO], writes=[Bob_])
                        else:
                            OP("dve", "tensor_copy", ob_[:, 0:385], Oa[:, 0:385], reads=[BO], writes=[Bob_])
                        Osb.append((ob_, Bob_))
                oT, BoT = oTs.next()
                oa4 = []
                for qi in range(4):
                    col = (qi % 2) * 256
                    O1, BO1 = Osb[0 * 2 + qi // 2]
                    O2, BO2 = Osb[1 * 2 + qi // 2]
                    rr, Brr = rrs.next()
                    OP("dve", "reciprocal", rr[:, 0:1], O1[:, col + 128:col + 129], reads=[BO1], writes=[Brr])
                    OP("dve", "reciprocal", rr[:, 1:2], O2[:, col + 128:col + 129], reads=[BO2], writes=[Brr])
                    OP("dve", "tensor_tensor", rr[:, 2:3], rr[:, 1:2], lamt[:, 5:6], ALU.mult, reads=[Brr, Blam], writes=[Brr])
                    t1, Bt1 = t1s.next()
                    OP("dve", "tensor_scalar", t1[:], O1[:, col:col + 128], rr[:, 0:1], None, op0=ALU.mult, reads=[BO1, Brr], writes=[Bt1])
                    dd, Bdd = dds.next()
                    OP("dve", "scalar_tensor_tensor", dd[:], O2[:, col:col + 128], rr[:, 2:3], t1[:], op0=ALU.mult, op1=ALU.add,
                       reads=[BO2, Brr, Bt1], writes=[Bdd])
                    jk, Bjk = jks.next()
                    OP("act", "activation", jk[:], dd[:], AF.Square, accum_out=rr[:, 3:4], reads=[Bdd], writes=[Bjk, Brr])
                    OP("act", "activation", rr[:, 4:5], rr[:, 3:4], AF.Ln, bias=C("eps"), scale=1.0 / 128, reads=[Brr, Bcf], writes=[Brr])
                    OP("act", "activation", rr[:, 5:6], rr[:, 4:5], AF.Exp, scale=-0.5, reads=[Brr], writes=[Brr])
                    oa, Boa = oas.next()
                    OP("dve", "scalar_tensor_tensor", oa[:], dd[:], rr[:, 5:6], SMc("subw"), op0=ALU.mult, op1=ALU.mult,
                       reads=[Bdd, Brr, Bsm], writes=[Boa])
                    oa4.append((oa, Boa))

                def finish(oa4=oa4, oT=oT, BoT=BoT, s=s, q0=q0):
                    tp, Btp = tpo.next()
                    for qi in range(4):
                        oa, Boa = oa4[qi]
                        OP("pe", "transpose", tp[:, qi * 128:(qi + 1) * 128], oa[:], C("ident", True), reads=[Boa, Bcb], writes=[Btp])
                    OP("act", "activation", oT[:, :], tp[:, 0:512], AF.Copy, scale=1.0 - LAMBDA_INIT, reads=[Btp], writes=[BoT])
                    DMA(OTc(s * 128, (s + 1) * 128, (q0 - 1) * 128, (q0 + 3) * 128), oT[:, :], reads=[BoT])
                pending.append(finish)
        while pending:
            pending.pop()()
        P.flush()
    if upto >= 3:
        build_dn(env)


def build_dn(env):
    g = env
    nc, P, NT, L, S, debug, upto = g["nc"], g["P"], g["NT"], g["L"], g["S"], g["debug"], g["upto"]
    OP, MM, DMA, sbuf, psum, rot, C, SMc = g["OP"], g["MM"], g["DMA"], g["sbuf"], g["psum"], g["rot"], g["C"], g["SMc"]
    cf, cb, sm, bg = g["cf"], g["cb"], g["sm"], g["bg"]
    Bcf, Bcb, Bsm, Bbg = g["Bcf"], g["Bcb"], g["Bsm"], g["Bbg"]
    DQ, DK, DV, ZZ, OTc = g["DQ"], g["DK"], g["DV"], g["ZZ"], g["OTc"]

    with ExitStack() as ph:
        pf = rot(ph, "pf", 6, [128, 512], F32, ps=True)
        pb = rot(ph, "pb", 2, [128, 1024], BF16, ps=True)
        R = {}

        def T(name, dt=BF16, n=2, w=512):
            if name not in R:
                R[name] = rot(ph, "c" + name, n, [128, w], dt)
            return R[name].next()

        Sf = sbuf(ph, "Sf", [128, 512], F32)
        BSf = Buf("Sf")
        Sb = sbuf(ph, "Sb", [128, 512], BF16)
        BSb = Buf("Sb")
        OP("pool", "memset", Sf[:], 0.0, writes=[BSf])
        OP("pool", "memset", Sb[:], 0.0, writes=[BSb])
        hs = [slice(h * 128, (h + 1) * 128) for h in range(4)]

        def mm4(ps, Bps, lhs, Blhs, rhs, Brhs, extra_reads=()):
            for h in range(4):
                MM(ps[:, hs[h]], lhs[:, hs[h]], rhs[:, hs[h]], True, True, [Blhs, Brhs] + list(extra_reads), [Bps])

        def tr4(src, Bsrc, col0=0):
            tp, Btp = pb.next()
            for h in range(4):
                OP("pe", "transpose", tp[:, col0 + h * 128:col0 + (h + 1) * 128], src[:, hs[h]], C("ident", True),
                   reads=[Bsrc, Bcb], writes=[Btp])
            return tp, Btp

        for i in range(NT):
            kT, BkT = T("kT", n=3)
            qT, BqT = T("qT", n=3)
            vT, BvT = T("vT", n=3)
            tsl = slice(i * 128, (i + 1) * 128)
            DMA(kT[:, :].rearrange("p (h t) -> p h t", h=4), DK[:, :, tsl].rearrange("h p t -> p h t"), writes=[BkT])
            DMA(qT[:, :].rearrange("p (h t) -> p h t", h=4), DQ[:, :, tsl].rearrange("h p t -> p h t"), writes=[BqT])
            DMA(vT[:, :].rearrange("p (h t) -> p h t", h=4), DV[:, :, tsl].rearrange("h p t -> p h t"), writes=[BvT])
            if i >= 1:
                z4, Bz4 = T("z4", n=2)
                DMA(z4[:, :], ZZ[tsl, :], writes=[Bz4])
            g4 = bg[:, i * 8 + 4:i * 8 + 8]
            b4 = bg[:, i * 8:i * 8 + 4]
            gp, Bgp = pf.next()
            MM(gp[:, 0:4], C("utf"), g4, True, True, [Bcf, Bbg], [Bgp])
            B4, BB4 = T("B4", F32)
            for h in range(4):
                OP("pool", "tensor_scalar", B4[:, hs[h]], C("utf"), g4[:, h:h + 1], None, op0=ALU.mult, reads=[Bcf, Bbg], writes=[BB4])
            Gr, BGr = pf.next()
            MM(Gr[:, :], C("ones"), B4[:, :], True, True, [Bcf, BB4], [BGr])
            sc, Bsc = T("sc", F32, 2, 32)
            gcs, dl, edl, edlb, egl = sc[:, 0:4], sc[:, 4:8], sc[:, 8:12], sc[:, 12:16], sc[:, 16:20]
            OP("act", "activation", gcs, gp[:, 0:4], AF.Copy, reads=[Bgp], writes=[Bsc])
            Grl = Gr[:, :].rearrange("p (h t) -> p h t", h=4)[:, :, 127:128]
            OP("dve", "tensor_tensor", dl.rearrange("p (h o) -> p h o", o=1), Grl, gcs.rearrange("p (h o) -> p h o", o=1),
               ALU.subtract, reads=[BGr, Bsc], writes=[Bsc])
            OP("act", "activation", edl, dl, AF.Exp, reads=[Bsc], writes=[Bsc])
            OP("act", "activation", egl.rearrange("p (h o) -> p h o", o=1), Grl, AF.Exp, reads=[BGr], writes=[Bsc])
            OP("dve", "tensor_tensor", edlb, edl, b4, ALU.mult, reads=[Bsc, Bbg], writes=[Bsc])
            Er, BEr = T("Er")
            OP("act", "activation", Er[:, :], Gr[:, :], AF.Exp, reads=[BGr], writes=[BEr])
            KgT, BKgT = T("KgT")
            QgT, BQgT = T("QgT")
            OP("pool", "tensor_tensor", KgT[:, :], kT[:, :], Er[:, :], ALU.mult, reads=[BkT, BEr], writes=[BKgT])
            OP("pool", "tensor_tensor", QgT[:, :], qT[:, :], Er[:, :], ALU.mult, reads=[BqT, BEr], writes=[BQgT])
            Dm, BDm = T("Dm", F32)
            for h in range(4):
                OP("dve", "tensor_scalar", Dm[:, hs[h]], Gr[:, hs[h]], gcs[:, h:h + 1], 0.0, op0=ALU.subtract, op1=ALU.min,
                   reads=[BGr, Bsc], writes=[BDm])
            OP("act", "activation", Dm[:, :], Dm[:, :], AF.Exp, reads=[BDm], writes=[BDm])
            DTs, BDTs = T("DTs", F32)
            DTi, BDTi = T("DTi", F32)
            OP("pool", "tensor_tensor", DTs[:, :], Dm[:, :], C("su4"), ALU.mult, reads=[BDm, Bcf], writes=[BDTs])
            OP("pool", "tensor_tensor", DTi[:, :], Dm[:, :], C("iu4"), ALU.mult, reads=[BDm, Bcf], writes=[BDTi])
            tkv, Btkv = pb.next()
            for h in range(4):
                OP("pe", "transpose", tkv[:, hs[h]], kT[:, hs[h]], C("ident", True), reads=[BkT, Bcb], writes=[Btkv])
            for h in range(4):
                OP("pe", "transpose", tkv[:, 512 + h * 128:512 + (h + 1) * 128], vT[:, hs[h]], C("ident", True), reads=[BvT, Bcb], writes=[Btkv])
            V4, BV4 = T("V4")
            OP("act", "activation", V4[:, :], tkv[:, 512:1024], AF.Copy, reads=[Btkv], writes=[BV4])
            Kd, BKd = T("Kd")
            for h in range(4):
                OP("dve", "tensor_scalar", Kd[:, hs[h]], tkv[:, hs[h]], edlb[:, h:h + 1], None, op0=ALU.mult, reads=[Btkv, Bsc], writes=[BKd])
            Ap, BAp = pf.next()
            mm4(Ap, BAp, kT, BkT, kT, BkT)
            Qp, BQp = pf.next()
            mm4(Qp, BQp, kT, BkT, qT, BqT)
            Y, BY = T("Y")
            QKd, BQKd = T("QKd")
            for h in range(4):
                OP("dve", "scalar_tensor_tensor", Y[:, hs[h]], Ap[:, hs[h]], b4[:, h:h + 1], DTs[:, hs[h]], op0=ALU.mult, op1=ALU.mult,
                   reads=[BAp, Bbg, BDTs], writes=[BY])
            for h in range(4):
                OP("dve", "scalar_tensor_tensor", QKd[:, hs[h]], Qp[:, hs[h]], b4[:, h:h + 1], DTi[:, hs[h]], op0=ALU.mult, op1=ALU.mult,
                   reads=[BQp, Bbg, BDTi], writes=[BQKd])
            tn, Btn = tr4(Y, BY)
            Nk, BNk = T("Nk", n=3)
            O1T, BO1T = T("O1T")
            O2T, BO2T = T("O2T")
            OP("dve", "tensor_tensor", Nk[:, :], tn[:, 0:512], C("d32", True), ALU.mult, reads=[Btn, Bcb], writes=[BNk])
            OP("dve", "tensor_tensor", O1T[:, :], tn[:, 0:512], C("o1t", True), ALU.mult, reads=[Btn, Bcb], writes=[BO1T])
            OP("dve", "tensor_tensor", O2T[:, :], tn[:, 0:512], C("o2t", True), ALU.mult, reads=[Btn, Bcb], writes=[BO2T])
            Yk, BYk = T("Yk", n=3)
            OP("pool", "tensor_tensor", Yk[:, :], Y[:, :], C("d32", True), ALU.mult, reads=[BY, Bcb], writes=[BYk])
            Pk, BPk = T("Pk", n=3)
            OP("pool", "tensor_tensor", Pk[:, :], C("i4", True), Yk[:, :], ALU.subtract, reads=[BYk, Bcb], writes=[BPk])
            for lvl in range(4):
                last = lvl == 3
                if not last:
                    py, Bpy = pf.next()
                    mm4(py, Bpy, Nk, BNk, Yk, BYk)
                pn, Bpn = pf.next()
                mm4(pn, Bpn, Yk, BYk, Nk, BNk)
                Nn, BNn = T("Nk", n=3)
                OP("act", "activation", Nn[:, :], pn[:, :], AF.Copy, reads=[Bpn], writes=[BNn])
                if not last:
                    Yn, BYn = T("Yk", n=3)
                    OP("act", "activation", Yn[:, :], py[:, :], AF.Copy, reads=[Bpy], writes=[BYn])
                    Yk, BYk = Yn, BYn
                Nk, BNk = Nn, BNn
                pp, Bpp = pf.next()
                mm4(pp, Bpp, Nk, BNk, Pk, BPk)
                Pn, BPn = T("Pk", n=3)
                OP("dve", "tensor_tensor", Pn[:, :], pp[:, :], Pk[:, :], ALU.add, reads=[Bpp, BPk], writes=[BPn])
                Pk, BPk = Pn, BPn
            Z, BZ = Pk, BPk
            for (OxT, BOxT) in ((O1T, BO1T), (O2T, BO2T)):
                tz, Btz = tr4(Z, BZ)
                ZT, BZT = T("ZT")
                OP("act", "activation", ZT[:, :], tz[:, 0:512], AF.Copy, reads=[Btz], writes=[BZT])
                pm, Bpm = pf.next()
                mm4(pm, Bpm, OxT, BOxT, Z, BZ)
                M1, BM1 = T("M1")
                OP("act", "activation", M1[:, :], pm[:, :], AF.Copy, reads=[Bpm], writes=[BM1])
                pz, Bpz = pf.next()
                mm4(pz, Bpz, ZT, BZT, M1, BM1)
                Zn, BZn = T("Pk", n=3)
                OP("dve", "tensor_tensor", Zn[:, :], Z[:, :], pz[:, :], ALU.subtract, reads=[BZ, Bpz], writes=[BZn])
                Z, BZ = Zn, BZn
            XT, BXT = Z, BZ
            pk, Bpk = pf.next()
            mm4(pk, Bpk, KgT, BKgT, Sb, BSb)
            R4, BR4 = T("R4")
            OP("dve", "tensor_tensor", R4[:, :], V4[:, :], pk[:, :], ALU.subtract, reads=[BV4, Bpk], writes=[BR4])
            px, Bpx = pf.next()
            mm4(px, Bpx, XT, BXT, R4, BR4)
            vn, Bvn = T("vn")
            OP("act", "activation", vn[:, :], px[:, :], AF.Copy, reads=[Bpx], writes=[Bvn])
            if i >= 1:
                po, Bpo = pf.next()
                for h in range(4):
                    MM(po[:, hs[h]], QgT[:, hs[h]], Sb[:, hs[h]], True, False, [BQgT, BSb], [Bpo])
                    MM(po[:, hs[h]], QKd[:, hs[h]], vn[:, hs[h]], False, True, [BQKd, Bvn], [Bpo])
            if i + 1 < NT:
                pd_, Bpd = pf.next()
                mm4(pd_, Bpd, Kd, BKd, vn, Bvn)
                for h in range(4):
                    OP("dve", "scalar_tensor_tensor", Sf[:, hs[h]], Sf[:, hs[h]], egl[:, h:h + 1], pd_[:, hs[h]], op0=ALU.mult, op1=ALU.add,
                       reads=[BSf, Bsc, Bpd], writes=[BSf])
                OP("act", "activation", Sb[:, :], Sf[:, :], AF.Copy, reads=[BSf], writes=[BSb])
            if i >= 1:
                so, Bso = T("so", F32, 2, 16)
                jk, Bjk = T("jk", BF16, 1, 128)
                for h in range(4):
                    OP("act", "activation", jk[:, :], po[:, hs[h]], AF.Square, accum_out=so[:, h:h + 1], reads=[Bpo], writes=[Bjk, Bso])
                OP("act", "activation", so[:, 4:8], so[:, 0:4], AF.Ln, bias=C("eps"), scale=1.0 / 128, reads=[Bso, Bcf], writes=[Bso])
                OP("act", "activation", so[:, 8:12], so[:, 4:8], AF.Exp, scale=-0.5, reads=[Bso], writes=[Bso])
                on, Bon = T("on", F32)
                for h in range(4):
                    OP("dve", "scalar_tensor_tensor", on[:, hs[h]], po[:, hs[h]], so[:, 8 + h:9 + h], SMc("onw"), op0=ALU.mult, op1=ALU.mult,
                       reads=[Bpo, Bso, Bsm], writes=[Bon])
                sz, Bsz = T("sz", F32)
                OP("act", "activation", sz[:, :], z4[:, :], AF.Silu, reads=[Bz4], writes=[Bsz])
                og, Bog = T("og")
                OP("pool", "tensor_tensor", og[:, :], on[:, :], sz[:, :], ALU.mult, reads=[Bon, Bsz], writes=[Bog])
                tg, Btg = tr4(og, Bog)
                odT, BodT = T("odT")
                OP("act", "activation", odT[:, :], tg[:, 0:512], AF.Copy, reads=[Btg], writes=[BodT])
                DMA(OTc(512, 1024, (i - 1) * 128, i * 128).rearrange("(h p) t -> p h t", p=128),
                    odT[:, :].rearrange("p (h t) -> p h t", h=4), reads=[BodT])
        P.flush()
    if upto >= 4:
        build_ffn(env)


def build_ffn(env):
    g = env
    nc, P, NT, L, S, debug, upto = g["nc"], g["P"], g["NT"], g["L"], g["S"], g["debug"], g["upto"]
    OP, MM, DMA, sbuf, psum, rot, C, SMc = g["OP"], g["MM"], g["DMA"], g["sbuf"], g["psum"], g["rot"], g["C"], g["SMc"]
    cf, cb, sm = g["cf"], g["cb"], g["sm"]
    Bcf, Bcb, Bsm = g["Bcf"], g["Bcb"], g["Bsm"]
    OTs, OTFs, OTFc, WG2, WU2, WD2, WO2 = g["OTs"], g["OTFs"], g["OTFc"], g["WG2"], g["WU2"], g["WD2"], g["WO2"]
    xh, sel, fnw, out, SH, NTH = g["xh"], g["sel"], g["fnw"], g["out"], g["SH"], g["NTH"]

    def CC(a, b):
        o = P.op("pool", lambda e: e.collective_compute("AllGather", ALU.bypass, replica_groups=[[0, 1], [2, 3], [4, 5], [6, 7]],
                                                         ins=[a], outs=[b]), (), ())
        o.sig = True
    for k in range(len(OTs)):
        CC(OTs[k], OTFs[k])
    P.flush()

    with ExitStack() as ph:
        pf = rot(ph, "epf", 6, [128, 512], F32, ps=True)
        pb = rot(ph, "epb", 2, [128, 1024], BF16, ps=True)
        oTa = sbuf(ph, "oTa", [128, 16 * 512], BF16)
        oTb = sbuf(ph, "oTb", [128, 16 * 512], BF16)
        BoTa, BoTb = Buf("oTa"), Buf("oTb")
        wobs = rot(ph, "wob", 2, [128, 16 * 256], BF16)
        haccs = [(sbuf(ph, "hacc%d" % i, [128, D], F32), Buf("hacc%d" % i)) for i in range(4)]
        u = sbuf(ph, "eu", [128, D], BF16)
        Bu = Buf("eu")
        uT = sbuf(ph, "euT", [128, 16 * 512], BF16)
        BuT = Buf("euT")
        uTv = uT[:, :].rearrange("p (c t) -> p c t", c=16)
        fnwt = sbuf(ph, "fnwt", [128, D], F32)
        Bfnw = Buf("fnwt")
        selt = sbuf(ph, "selt", [128, 2], F32)
        Bsel = Buf("selt")
        DMA(fnwt[:], fnw[:, :], writes=[Bfnw])
        DMA(selt[:], sel[:, :], writes=[Bsel])
        wgs = rot(ph, "wgb", 2, [128, 16 * 256], BF16)
        wus = rot(ph, "wub", 2, [128, 16 * 256], BF16)
        wds = rot(ph, "wdb", 2, [128, 2 * 2048], BF16)
        sgs = rot(ph, "sg", 2, [128, 512], F32)
        acts = rot(ph, "actT", 4, [128, 512], BF16)
        sts = rot(ph, "est", 2, [128, 4], F32)

        for tb in range(NTH // 4):
            c0 = tb * 512
            for hf in range(2):
                DMA(oTa[:, hf * 4096:(hf + 1) * 4096].rearrange("p (c t) -> p c t", c=8),
                    OTFc(hf * 1024, (hf + 1) * 1024, c0, c0 + 512).rearrange("(c p) t -> p c t", p=128), writes=[BoTa])
                DMA(oTb[:, hf * 4096:(hf + 1) * 4096].rearrange("p (c t) -> p c t", c=8),
                    OTFc(hf * 1024, (hf + 1) * 1024, SH + c0, SH + c0 + 512).rearrange("(c p) t -> p c t", p=128), writes=[BoTb])
            OP("pool", "tensor_scalar", oTa[:, :], oTa[:, :], selt[:, 0:1], None, op0=ALU.mult, reads=[BoTa, Bsel], writes=[BoTa])
            OP("dve", "scalar_tensor_tensor", oTa[:, :], oTb[:, :], selt[:, 1:2], oTa[:, :], op0=ALU.mult, op1=ALU.add,
               reads=[BoTa, BoTb, Bsel], writes=[BoTa])
            for i in range(4):
                ha, Bha = haccs[i]
                DMA(ha[:], xh[c0 + i * 128:c0 + (i + 1) * 128, :], writes=[Bha])
            for n in range(8):
                wob, Bwob = wobs.next()
                DMA(wob[:, :].rearrange("p (c n) -> p c n", c=16), WO2[:, n * 256:(n + 1) * 256].rearrange("(c p) n -> p c n", p=128), writes=[Bwob])
                for i in range(4):
                    ha, Bha = haccs[i]
                    ps, Bps = pf.next()
                    for c in range(16):
                        MM(ps[:, 0:256], oTa[:, c * 512 + i * 128:c * 512 + (i + 1) * 128], wob[:, c * 256:(c + 1) * 256],
                           c == 0, c == 15, [BoTa, Bwob], [Bps])
                    OP("dve", "tensor_tensor", ha[:, n * 256:(n + 1) * 256], ps[:, 0:256], ha[:, n * 256:(n + 1) * 256], ALU.add,
                       reads=[Bps, Bha], writes=[Bha])
            for i in range(4):
                ha, Bha = haccs[i]
                st, Bst = sts.next()
                OP("act", "activation", u[:], ha[:], AF.Square, accum_out=st[:, 0:1], reads=[Bha], writes=[Bu, Bst])
                OP("act", "activation", st[:, 1:2], st[:, 0:1], AF.Ln, bias=C("eps"), scale=1.0 / D, reads=[Bst, Bcf], writes=[Bst])
                OP("act", "activation", st[:, 2:3], st[:, 1:2], AF.Exp, scale=-0.5, reads=[Bst], writes=[Bst])
                OP("dve", "scalar_tensor_tensor", u[:], ha[:], st[:, 2:3], fnwt[:], op0=ALU.mult, op1=ALU.mult,
                   reads=[Bha, Bst, Bfnw], writes=[Bu])
                for half in range(2):
                    tp, Btp = pb.next()
                    for c8 in range(8):
                        c = half * 8 + c8
                        OP("pe", "transpose", tp[:, c8 * 128:(c8 + 1) * 128], u[:, c * 128:(c + 1) * 128], C("ident", True),
                           reads=[Bu, Bcb], writes=[Btp])
                    dstv = uTv[:, half * 8:half * 8 + 8, i * 128:(i + 1) * 128]
                    srcv = tp[:, :].rearrange("p (c t) -> p c t", c=8)
                    if half == 0:
                        OP("act", "activation", dstv, srcv, AF.Copy, reads=[Btp], writes=[BuT])
                    else:
                        OP("dve", "tensor_copy", dstv, srcv, reads=[Btp], writes=[BuT])
            for gi in range(22):
                wgb, Bwg = wgs.next()
                wub, Bwu = wus.next()
                wdb, Bwd = wds.next()
                DMA(wgb[:, :], WG2[gi, :, :], writes=[Bwg])
                DMA(wub[:, :], WU2[gi, :, :], writes=[Bwu])
                DMA(wdb[:, :].rearrange("p (k n) -> p k n", k=2), WD2[gi * 256:(gi + 1) * 256, :].rearrange("(k p) n -> p k n", p=128), writes=[Bwd])
                ak = []
                for k in range(2):
                    pg, Bpg = pf.next()
                    for c in range(16):
                        MM(pg[:, :], wgb[:, c * 256 + k * 128:c * 256 + (k + 1) * 128], uT[:, c * 512:(c + 1) * 512], c == 0, c == 15, [Bwg, BuT], [Bpg])
                    pu, Bpu = pf.next()
                    for c in range(16):
                        MM(pu[:, :], wub[:, c * 256 + k * 128:c * 256 + (k + 1) * 128], uT[:, c * 512:(c + 1) * 512], c == 0, c == 15, [Bwu, BuT], [Bpu])
                    sg, Bsg = sgs.next()
                    OP("act", "activation", sg[:, :], pg[:, :], AF.Silu, reads=[Bpg], writes=[Bsg])
                    at, Bat = acts.next()
                    OP("dve", "tensor_tensor", at[:, :], sg[:, :], pu[:, :], ALU.mult, reads=[Bsg, Bpu], writes=[Bat])
                    ak.append((at, Bat))
                for i in range(4):
                    ha, Bha = haccs[i]
                    for n in range(4):
                        ps, Bps = pf.next()
                        for k in range(2):
                            at, Bat = ak[k]
                            MM(ps[:, :], at[:, i * 128:(i + 1) * 128], wdb[:, k * 2048 + n * 512:k * 2048 + (n + 1) * 512], k == 0, k == 1, [Bat, Bwd], [Bps])
                        OP("dve", "tensor_tensor", ha[:, n * 512:(n + 1) * 512], ps[:, :], ha[:, n * 512:(n + 1) * 512], ALU.add,
                           reads=[Bps, Bha], writes=[Bha])
            for i in range(4):
                ha, Bha = haccs[i]
                DMA(out[c0 + i * 128:c0 + (i + 1) * 128, :], ha[:], reads=[Bha])
        P.flush()


OFF = {"aq": 0, "ak": 1024, "av": 2048, "dq": 3072, "dk": 4096, "dv": 5120, "dz": 6144, "db": 7168, "da": 7176}


def core_inputs(inp, c, NT):
    b, j = c // 2, c % 2
    L = NT * 128
    f = np.float32
    ah = [2 * s + j for s in range(4)]
    dh = [4 * j + k for k in range(4)]
    slopes = [2.0 ** -(h + 1) for h in ah]
    w_in = inp["w_in"][0]

    def cols(base, heads, w=128):
        return np.concatenate([w_in[:, base + h * w:base + (h + 1) * w] for h in heads], axis=1)

    wA = np.concatenate([cols(OFF["aq"], ah), cols(OFF["ak"], ah), cols(OFF["av"], ah), cols(OFF["dz"], dh),
                         cols(OFF["db"], dh, 1), cols(OFF["da"], dh, 1)], axis=1)
    wD = np.concatenate([cols(OFF["dq"], dh), cols(OFF["dk"], dh), cols(OFF["dv"], dh)], axis=1)
    w_out = inp["w_out"][0]
    rows = []
    for r in range(2):
        rows += [w_out[(2 * s + r) * 128:(2 * s + r + 1) * 128] for s in range(4)]
        rows += [w_out[1024 + (4 * r + k) * 128:1024 + (4 * r + k + 1) * 128] for k in range(4)]
    wo = np.concatenate(rows, axis=0)
    small = np.zeros((128, 572), f)
    small[:, 0] = np.tile(inp["q_norm_w"][0], 2)
    small[:, 1] = np.tile(inp["k_norm_w"][0], 2)
    small[:, 2:258] = np.concatenate([inp["lambda_q1"][0], inp["lambda_k1"][0], inp["lambda_q2"][0], inp["lambda_k2"][0]])[None, :]
    small[:, 258:386] = inp["subln_w"][0][None, :]
    small[:, 386:514] = inp["o_norm_w"][0][None, :]
    cw = inp["conv_w"][0]
    for typ in range(3):
        for k, h in enumerate(dh):
            s = typ * 4 + k
            small[:, 514 + s * 4:514 + s * 4 + 4] = cw[typ * 1024 + h * 128:typ * 1024 + (h + 1) * 128, :]
    small[:, 562:566] = inp["a_log"][0][dh][None, :]
    small[:, 566:570] = inp["dt_bias"][0][dh][None, :]
    aug = np.zeros((4, 2, 3, L), f)
    tok = np.arange(L)
    tt, ti = tok // 128, tok % 128
    real = (tok >= 128).astype(f)
    for s, sl in enumerate(slopes):
        aug[s, 0, 0] = real
        aug[s, 0, 1] = real
        aug[s, 0, 2] = real * 8.0 * sl * 128.0 * tt
        aug[s, 1, 0] = -8.0 * sl * 128.0 * tt
        aug[s, 1, 1] = -8.0 * sl * ti
        aug[s, 1, 2] = 1.0
    S = L - 128
    return {
        "x": np.ascontiguousarray(inp["x"][b][:S]),
        "meta": np.ascontiguousarray(inp["meta_tokens"]),
        "anw": np.ascontiguousarray(np.broadcast_to(inp["attn_norm_w"][0][None, :], (128, D))),
        "fnw": np.ascontiguousarray(np.broadcast_to(inp["ffn_norm_w"][0][None, :], (128, D))),
        "wA": np.ascontiguousarray(wA), "wD": np.ascontiguousarray(wD), "wo": np.ascontiguousarray(wo),
        "wg": np.ascontiguousarray(inp["w_gate"][0]), "wu": np.ascontiguousarray(inp["w_up"][0]),
        "wd": np.ascontiguousarray(inp["w_down"][0]),
        "cst": make_consts(slopes), "small": small, "aug": aug,
        "xh": np.ascontiguousarray(inp["x"][b][j * (S // 2):(j + 1) * (S // 2)]),
        "sel": np.ascontiguousarray(np.broadcast_to(np.array([1.0 - j, float(j)], f)[None, :], (128, 2))),
    }


_NC_CACHE = {}


def kernel(**inputs):
    inp = {k: np.asarray(v) for k, v in inputs.items()}
    B, S, _ = inp["x"].shape
    NT = S // 128 + 1
    if NT not in _NC_CACHE:
        _NC_CACHE[NT] = build_program(NT)
    nc = _NC_CACHE[NT]
    in_maps = [core_inputs(inp, c, NT) for c in range(8)]
    res = run_bass_kernel_spmd(nc, in_maps, core_ids=list(range(8)))
    SH = S // 2
    out = np.empty((B, S, D), np.float32)
    for c in range(8):
        b, j = c // 2, c % 2
        out[b, j * SH:(j + 1) * SH] = res.results[c]["out"]
    return out
```

```python
import math
import numpy as np
from contextlib import ExitStack
import concourse.bass as bass
import concourse.mybir as mybir
from concourse.bass_utils import run_bass_kernel_spmd

F32 = mybir.dt.float32
BF16 = mybir.dt.bfloat16
AF = mybir.ActivationFunctionType
ALU = mybir.AluOpType

D = 2048
HID = 5632
EPS = 1e-6
LAMBDA_INIT = 0.8 - 0.6 * math.exp(0.0)
ENGS = ("pe", "act", "dve", "pool", "sp")
ALIBI_CUT = 60.0


class Buf:
    __slots__ = ("name", "w", "r", "dsem", "dcnt")

    def __init__(self, name):
        self.name = name
        self.w = None
        self.r = []
        self.dsem = None
        self.dcnt = 0


class Op:
    __slots__ = ("eng", "fn", "raw", "oth", "sig", "cnt", "isdma", "sem", "ndma")

    def __init__(self, eng, fn, isdma=False):
        self.eng = eng
        self.fn = fn
        self.raw = set()
        self.oth = set()
        self.sig = False
        self.cnt = 0
        self.isdma = isdma
        self.sem = None
        self.ndma = 0


class Prog:
    def __init__(self, nc, stack):
        self.nc = nc
        self.stack = stack
        self.esem = {e: stack.enter_context(nc.semaphore("es_" + e)) for e in ENGS}
        self.ecnt = {e: 0 for e in ENGS}
        self.waited = {e: {} for e in ENGS}
        self.ops = {e: [] for e in ENGS}
        self.dma_bufs = []
        self.nsem = 0

    def _deps(self, o, reads, writes):
        for b in reads:
            if b.w is not None:
                o.raw.add(b.w)
        for b in writes:
            if b.w is not None:
                o.oth.add(b.w)
            for r in b.r:
                o.oth.add(r)
        o.raw.discard(o)
        o.oth.discard(o)
        for b in reads:
            b.r.append(o)
        for b in writes:
            b.w = o
            b.r = []

    def op(self, eng, fn, reads=(), writes=()):
        o = Op(eng, fn)
        self._deps(o, reads, writes)
        self.ops[eng].append(o)
        return o

    def dma(self, fn, reads=(), writes=(), n=1, key=None, eng="sp"):
        o = Op(eng, fn, isdma=True)
        o.ndma = n
        self._deps(o, reads, writes)
        if key is None:
            key = writes[0] if writes else reads[0]
        if key.dsem is None:
            key.dsem = self.stack.enter_context(self.nc.semaphore("ds%d" % self.nsem))
            self.nsem += 1
            self.dma_bufs.append(key)
        key.dcnt += 16 * n
        o.sem = key.dsem
        o.cnt = key.dcnt
        self.ops[eng].append(o)
        return o

    def flush(self):
        nc = self.nc
        ops = self.ops
        for e in ENGS:
            for o in ops[e]:
                for d in o.raw | o.oth:
                    if d.isdma:
                        continue
                    if d.eng == e and (e in ("pe", "sp") or d not in o.raw):
                        continue
                    d.sig = True
        for e in ENGS:
            for o in reversed(ops[e]):
                if not o.isdma:
                    o.sig = True
                    break
        for e in ENGS:
            c = self.ecnt[e]
            for o in ops[e]:
                if o.isdma:
                    continue
                if o.sig:
                    c += 1
                o.cnt = c
            self.ecnt[e] = c
        final_e = dict(self.ecnt)
        final_d = [(b.dsem, b.dcnt) for b in self.dma_bufs]
        esem = self.esem
        waited = self.waited

        def emit(e, eng):
            wd = waited[e]

            def wait(sem, val):
                k = id(sem)
                if wd.get(k, 0) < val:
                    eng.wait_ge(sem, val)
                    wd[k] = val

            for o in ops[e]:
                need = {}
                for d in o.raw | o.oth:
                    if d.isdma:
                        s, v = d.sem, d.cnt
                    else:
                        if d.eng == e and (e in ("pe", "sp") or d not in o.raw):
                            continue
                        s, v = esem[d.eng], d.cnt
                    k = id(s)
                    if k not in need or need[k][1] < v:
                        need[k] = (s, v)
                for s, v in need.values():
                    wait(s, v)
                r = o.fn(eng)
                if o.isdma:
                    assert len(r) == o.ndma, (len(r), o.ndma)
                    for ins in r:
                        ins.then_inc(o.sem, 16)
                elif o.sig:
                    r.then_inc(esem[e], 1)
            if e == "sp":
                for s, v in final_d:
                    wait(s, v)
                eng.sem_inc(esem["sp"], 1)
            for e2 in ENGS:
                v = final_e[e2] + (1 if e2 == "sp" else 0)
                if e2 == e:
                    wd[id(esem[e2])] = v
                    continue
                wait(esem[e2], v)

        with nc.Block() as block:
            @block.tensor
            def _(eng):
                emit("pe", eng)

            @block.scalar
            def _(eng):
                emit("act", eng)

            @block.vector
            def _(eng):
                emit("dve", eng)

            @block.gpsimd
            def _(eng):
                emit("pool", eng)

            @block.sync
            def _(eng):
                emit("sp", eng)
        self.ecnt["sp"] += 1
        self.ops = {e: [] for e in ENGS}


class Rot:
    def __init__(self, items):
        self.items = items
        self.i = 0

    def next(self):
        it = self.items[self.i % len(self.items)]
        self.i += 1
        return it


def _cst_layout():
    off = {}
    o = 0
    for name, w in (("ident", 128), ("ones", 128), ("utf", 128), ("su4", 512), ("iu4", 512), ("valid", 1),
                    ("kbias", 4), ("eps", 1), ("zero", 1), ("one", 1), ("F32END", 0),
                    ("blk64", 128), ("d32", 512), ("o1t", 512), ("o2t", 512), ("i4", 512), ("cneg", 128)):
        off[name] = (o, w)
        o += w
    return off, o


CST, NCST = _cst_layout()
NF32 = CST["F32END"][0]


def make_consts(slopes4):
    c = np.zeros((128, NCST), np.float32)

    def put(name, arr):
        o, w = CST[name]
        c[:, o:o + w] = arr

    p = np.arange(128)
    put("ident", np.eye(128))
    put("blk64", ((p[:, None] // 64) == (p[None, :] // 64)) / 64.0)
    put("ones", np.ones((128, 128)))
    put("utf", (p[:, None] <= p[None, :]))
    su = (p[None, :] > p[:, None]).astype(np.float32)
    iu = (p[None, :] >= p[:, None]).astype(np.float32)
    put("su4", np.tile(su, (1, 4)))
    put("iu4", np.tile(iu, (1, 4)))
    blk = p // 32
    d32 = (blk[:, None] == blk[None, :]).astype(np.float32)
    put("d32", np.tile(d32, (1, 4)))
    o1t = ((blk[:, None] == blk[None, :] + 1) & (blk[:, None] % 2 == 1)).astype(np.float32)
    o2t = ((blk[:, None] >= 2) & (blk[None, :] < 2)).astype(np.float32)
    put("o1t", np.tile(o1t, (1, 4)))
    put("o2t", np.tile(o2t, (1, 4)))
    put("i4", np.tile(np.eye(128), (1, 4)))
    put("cneg", np.where(p[:, None] > p[None, :], -30000.0, 0.0))
    put("valid", (p >= 112).astype(np.float32)[:, None])
    put("kbias", p[:, None] * np.asarray(slopes4, np.float32)[None, :])
    put("eps", np.full((128, 1), EPS))
    put("one", np.ones((128, 1)))
    return c


def build_program(NT, debug=False, upto=99):
    L = NT * 128
    S = L - 128
    NTH = (NT - 1) // 2
    SH = NTH * 128
    nc = bass.Bass("TRN2", target_bir_lowering=False)

    def din(name, shape, dt=F32):
        return nc.dram_tensor(name, list(shape), dt, kind="ExternalInput").ap()

    def dscr(name, shape, dt, out=False):
        return nc.dram_tensor(name, list(shape), dt, kind="ExternalOutput" if (out and debug) else "Internal").ap()

    x = din("x", [S, D])
    meta = din("meta", [16, D])
    anw = din("anw", [128, D])
    fnw = din("fnw", [128, D])
    wA = din("wA", [D, 2056])
    wDn = din("wD", [D, 1536])
    wo = din("wo", [D, D])
    wg = din("wg", [D, HID])
    wu = din("wu", [D, HID])
    wd = din("wd", [HID, D])
    cst = din("cst", [128, NCST])
    small = din("small", [128, 572])
    aug = din("aug", [4, 2, 3, L])
    xh = din("xh", [SH, D])
    sel = din("sel", [128, 2])
    out = nc.dram_tensor("out", [SH, D], F32, kind="ExternalOutput").ap()

    QT = dscr("QT", [4, 128, L], BF16, True)
    KT = dscr("KT", [4, 128, L], BF16, True)
    VV = dscr("VV", [L, 512], BF16, True)
    DQ = dscr("DQ", [4, 128, L], BF16, True)
    DK = dscr("DK", [4, 128, L], BF16, True)
    DV = dscr("DV", [4, 128, L], BF16, True)
    ZZ = dscr("ZZ", [L, 512], BF16, True)
    PW = min(1024, S)
    NPC = S // PW
    OTs = [dscr("OT%d" % k, [1024, PW], BF16, debug and upto < 4) for k in range(NPC)]
    OTFs = [dscr("OTF%d" % k, [2048, PW], BF16) for k in range(NPC)]

    def OTc(r0, r1, c0, c1):
        k = c0 // PW
        assert (c1 - 1) // PW == k
        return OTs[k][r0:r1, c0 - k * PW:c1 - k * PW]

    def OTFc(r0, r1, c0, c1):
        k = c0 // PW
        assert (c1 - 1) // PW == k
        return OTFs[k][r0:r1, c0 - k * PW:c1 - k * PW]
    WG2 = dscr("WG2", [22, 128, 16 * 256], BF16)
    WU2 = dscr("WU2", [22, 128, 16 * 256], BF16)
    WD2 = dscr("WD2", [HID, D], BF16)
    WO2 = dscr("WO2", [D, D], BF16)
    BGo = nc.dram_tensor("BGo", [128, NT * 8], F32, kind="ExternalOutput").ap() if debug else None

    SM = {"qkw": (0, 2), "lam": (2, 256), "subw": (258, 128), "onw": (386, 128), "convw": (514, 48),
          "alog": (562, 4), "dtb": (566, 4)}

    with ExitStack() as outer:
        P = Prog(nc, outer)

        def OP(eng, method, *args, reads=(), writes=(), **kw):
            return P.op(eng, lambda e: getattr(e, method)(*args, **kw), reads, writes)

        def MM(out_, lhsT, rhs, start, stop, reads, writes):
            return P.op("pe", lambda e: e.matmul(out_, lhsT, rhs, start=start, stop=stop, skip_group_check=True), reads, writes)

        def DMA(out_, in_, reads=(), writes=(), eng="sp", key=None):
            return P.dma(lambda e: [e.dma_start(out=out_, in_=in_)], reads, writes, 1, key, eng)

        uniq = [0]

        def sbuf(stack, name, shape, dt):
            uniq[0] += 1
            return stack.enter_context(nc.sbuf_tensor("%s_%d" % (name, uniq[0]), list(shape), dt))

        def psum(stack, name, shape, dt):
            uniq[0] += 1
            return stack.enter_context(nc.psum_tensor("%s_%d" % (name, uniq[0]), list(shape), dt))

        def rot(stack, name, n, shape, dt, ps=False):
            mk = psum if ps else sbuf
            return Rot([(mk(stack, "%s%d" % (name, i), shape, dt), Buf("%s%d" % (name, i))) for i in range(n)])

        cf = sbuf(outer, "cf", [128, NF32], F32)
        cb = sbuf(outer, "cb", [128, NCST], BF16)
        sm = sbuf(outer, "sm", [128, 572], F32)
        bg = sbuf(outer, "bg", [128, NT * 8], F32)
        lamt = sbuf(outer, "lamt", [128, 8], F32)
        negA = sbuf(outer, "negA", [128, 4], F32)
        Bcf, Bcb, Bsm, Bbg, Blam, BnegA = [Buf(n) for n in ("cf", "cb", "sm", "bg", "lam", "negA")]

        def C(name, bf=False, lo=0, hi=None):
            o, w = CST[name]
            t = cb if bf else cf
            if not bf:
                assert o + w <= NF32, name
            return t[:, o + lo:o + (w if hi is None else hi)]

        def SMc(name, lo=0, hi=None):
            o, w = SM[name]
            return sm[:, o + lo:o + (w if hi is None else hi)]

        conv_jobs = []
        for c in range(16):
            for hf in range(2):
                for (wsrc_, wdst_) in ((wg, WG2), (wu, WU2)):
                    conv_jobs.append((wsrc_[c * 128:(c + 1) * 128, hf * 2816:(hf + 1) * 2816],
                                      wdst_[hf * 11:(hf + 1) * 11, :, c * 256:(c + 1) * 256].rearrange("g p k -> p g k"), 2816, 11))
        for r in range(HID // 128):
            conv_jobs.append((wd[r * 128:(r + 1) * 128, :], WD2[r * 128:(r + 1) * 128, :], 2048, 0))
        for r in range(16):
            conv_jobs.append((wo[r * 128:(r + 1) * 128, :], WO2[r * 128:(r + 1) * 128, :], 2048, 0))
        conv_state = {"i": 0}

        with ExitStack() as ph:
            ctmp = sbuf(ph, "ctmp", [128, NCST], F32)
            Bct = Buf("ctmp")
            DMA(ctmp[:], cst[:, :], writes=[Bct])
            DMA(cf[:], cst[:, 0:NF32], writes=[Bcf])
            DMA(sm[:], small[:, :], writes=[Bsm])
            OP("pool", "tensor_copy", cb[:], ctmp[:], reads=[Bct], writes=[Bcb])
            lo = SM["lam"][0]
            tmp = sbuf(ph, "ltmp", [128, 64], F32)
            Bt = Buf("ltmp")
            for k in range(2):
                OP("dve", "tensor_tensor", tmp[:], sm[:, lo + k * 128:lo + k * 128 + 64],
                   sm[:, lo + k * 128 + 64:lo + k * 128 + 128], ALU.mult, reads=[Bsm], writes=[Bt])
                OP("dve", "tensor_reduce", lamt[:, k:k + 1], tmp[:], mybir.AxisListType.X, ALU.add, reads=[Bt], writes=[Blam])
            OP("act", "activation", lamt[:, 2:4], lamt[:, 0:2], AF.Exp, reads=[Blam], writes=[Blam])
            OP("dve", "tensor_tensor", lamt[:, 4:5], lamt[:, 3:4], lamt[:, 2:3], ALU.subtract, reads=[Blam], writes=[Blam])
            OP("dve", "tensor_scalar", lamt[:, 5:6], lamt[:, 4:5], -LAMBDA_INIT, None, op0=ALU.add, reads=[Blam], writes=[Blam])
            OP("act", "activation", negA[:], SMc("alog"), AF.Exp, reads=[Bsm], writes=[BnegA])
            OP("dve", "tensor_scalar", negA[:], negA[:], -1.0, None, op0=ALU.mult, reads=[BnegA], writes=[BnegA])
            P.flush()

        blocks = [(0, 1)] + [(1 + 4 * i, 4) for i in range((NT - 1) // 4)]
        assert (NT - 1) % 4 == 0

        def phase_A(pass_id):
            CW = 2056 if pass_id == 0 else 1536
            wsrc = wA if pass_id == 0 else wDn
            with ExitStack() as ph:
                wres = sbuf(ph, "wres", [128, 16 * CW], BF16)
                Bw = Buf("wres")
                xts = rot(ph, "xt", 2, [128, D], F32)
                if pass_id == 0:
                    stR = rot(ph, "stg", 2, [128, 2816], F32)
                    stbR = rot(ph, "stgb", 2, [128, 2816], BF16)
                else:
                    stR = xts
                for c in range(16):
                    st, Bs = stR.next()
                    DMA(st[:, 0:CW], wsrc[c * 128:(c + 1) * 128, :], writes=[Bs])
                    OP("pool", "tensor_copy", wres[:, c * CW:(c + 1) * CW], st[:, 0:CW], reads=[Bs], writes=[Bw])

                def convert_some(n):
                    for _ in range(n):
                        if conv_state["i"] >= len(conv_jobs):
                            return
                        src, dst, w, g = conv_jobs[conv_state["i"]]
                        conv_state["i"] += 1
                        st, Bs = stR.next()
                        sb_, Bb = stbR.next()
                        DMA(st[:, 0:w], src, writes=[Bs])
                        OP("pool", "tensor_copy", sb_[:, 0:w], st[:, 0:w], reads=[Bs], writes=[Bb])
                        if g:
                            DMA(dst, sb_[:, 0:w].rearrange("p (g k) -> p g k", g=g), reads=[Bb])
                        else:
                            DMA(dst, sb_[:, 0:w], reads=[Bb])

                us = rot(ph, "u", 5, [128, D], BF16)
                uTs = rot(ph, "uT", 2, [128, 16 * 512], BF16)
                sts = rot(ph, "st", 2, [128, 4], F32)
                anwt = sbuf(ph, "anwt", [128, D], F32)
                Banw = Buf("anwt")
                DMA(anwt[:], anw[:, :], writes=[Banw])
                tps = rot(ph, "tp", 2, [128, 1024], BF16, ps=True)
                mms = rot(ph, "mm", 4, [128, 512], F32, ps=True)
                aux = rot(ph, "aux", 2, [128, 512], F32, ps=True)
                sqs = rot(ph, "sq", 3, [128, 512], BF16)
                rrs = rot(ph, "rr", 2, [128, 512], F32)
                obs = rot(ph, "ob", 3, [128, 512], BF16)
                if pass_id == 0:
                    bat = rot(ph, "bat", 2, [128, 16], F32)
                else:
                    cbuf = sbuf(ph, "cbuf", [128, 12 * 515], F32)
                    Bcbuf = [Buf("cbuf%d" % i) for i in range(12)]
                    OP("pool", "memset", cbuf[:], 0.0, writes=Bcbuf)
                    accs = rot(ph, "acc", 2, [128, 512], F32)
                    sls = rot(ph, "sl", 3, [128, 512], F32)

                def stage1(bi):
                    t0, nt = blocks[bi]
                    ul = []
                    for i in range(nt):
                        ti = t0 + i
                        xt, Bx = xts.next()
                        if ti == 0:
                            OP("pool", "memset", xt[:], 0.0, writes=[Bx])
                            DMA(xt[112:128, :], meta[:, :], writes=[Bx])
                        else:
                            DMA(xt[:], x[(ti - 1) * 128:ti * 128, :], writes=[Bx])
                        st, Bst = sts.next()
                        u, Bu = us.next()
                        OP("act", "activation", u[:], xt[:], AF.Square, accum_out=st[:, 0:1], reads=[Bx], writes=[Bu, Bst])
                        OP("act", "activation", st[:, 1:2], st[:, 0:1], AF.Ln, bias=C("eps"), scale=1.0 / D, reads=[Bst, Bcf], writes=[Bst])
                        OP("act", "activation", st[:, 2:3], st[:, 1:2], AF.Exp, scale=-0.5, reads=[Bst], writes=[Bst])
                        OP("dve", "scalar_tensor_tensor", u[:], xt[:], st[:, 2:3], anwt[:], op0=ALU.mult, op1=ALU.mult,
                           reads=[Bx, Bst, Banw], writes=[Bu])
                        ul.append((u, Bu))
                    return ul

                def stage2(bi, ul):
                    uT, BuT = uTs.next()
                    uTv = uT[:, :].rearrange("p (c t) -> p c t", c=16)
                    for i, (u, Bu) in enumerate(ul):
                        for half in range(2):
                            tp, Btp = tps.next()
                            for c8 in range(8):
                                c = half * 8 + c8
                                OP("pe", "transpose", tp[:, c8 * 128:(c8 + 1) * 128], u[:, c * 128:(c + 1) * 128], C("ident", True),
                                   reads=[Bu, Bcb], writes=[Btp])
                            dstv = uTv[:, half * 8:half * 8 + 8, i * 128:(i + 1) * 128]
                            srcv = tp[:, :].rearrange("p (c t) -> p c t", c=8)
                            if half == 0:
                                OP("act", "activation", dstv, srcv, AF.Copy, reads=[Btp], writes=[BuT])
                            else:
                                OP("dve", "tensor_copy", dstv, srcv, reads=[Btp], writes=[BuT])
                    return uT, BuT

                cur_uT = stage2(0, stage1(0))
                for bi, (t0, nt) in enumerate(blocks):
                    T = nt * 128
                    uT, BuT = cur_uT

                    def fm_block(col0):
                        mm, Bmm = mms.next()
                        for c in range(16):
                            MM(mm[:, 0:T], wres[:, c * CW + col0:c * CW + col0 + 128], uT[:, c * 512:c * 512 + T],
                               c == 0, c == 15, [Bw, BuT], [Bmm])
                        return mm, Bmm

                    def tm_block(i, col0, cw):
                        mm, Bmm = mms.next()
                        for c in range(16):
                            MM(mm[:, 0:cw], uT[:, c * 512 + i * 128:c * 512 + (i + 1) * 128], wres[:, c * CW + col0:c * CW + col0 + cw],
                               c == 0, c == 15, [Bw, BuT], [Bmm])
                        return mm, Bmm

                    def norm1(mm, Bmm, src_sb, Bsrc):
                        sq, Bsq = sqs.next()
                        if src_sb is None:
                            OP("act", "activation", sq[:, 0:T], mm[:, 0:T], AF.Square, reads=[Bmm], writes=[Bsq])
                        else:
                            OP("pool", "tensor_tensor", sq[:, 0:T], src_sb[:, 0:T], src_sb[:, 0:T], ALU.mult, reads=[Bsrc], writes=[Bsq])
                        return sq, Bsq

                    def norm2(sq, Bsq, mm, Bmm, src_sb, Bsrc, onesname, lnscale, wcol, dst):
                        ax, Bax = aux.next()
                        MM(ax[:, 0:T], C(onesname, True), sq[:, 0:T], True, True, [Bsq, Bcb], [Bax])
                        rr, Brr = rrs.next()
                        OP("act", "activation", rr[:, 0:T], ax[:, 0:T], AF.Ln, bias=C("eps"), scale=lnscale, reads=[Bax, Bcf], writes=[Brr])
                        OP("act", "activation", rr[:, 0:T], rr[:, 0:T], AF.Exp, scale=-0.5, reads=[Brr], writes=[Brr])
                        ob, Bob = obs.next()
                        if src_sb is None:
                            OP("dve", "scalar_tensor_tensor", ob[:, 0:T], mm[:, 0:T], wcol, rr[:, 0:T], op0=ALU.mult, op1=ALU.mult,
                               reads=[Bmm, Brr, Bsm], writes=[Bob])
                        else:
                            OP("dve", "tensor_tensor", ob[:, 0:T], src_sb[:, 0:T], rr[:, 0:T], ALU.mult, reads=[Bsrc, Brr], writes=[Bob])
                        DMA(dst, ob[:, 0:T], reads=[Bob])

                    units = []
                    if pass_id == 0:
                        for qk in range(2):
                            for h in range(4):
                                def unit(qk=qk, h=h):
                                    mm, Bmm = fm_block(qk * 512 + h * 128)
                                    dstT = (QT if qk == 0 else KT)[h, :, t0 * 128:t0 * 128 + T]
                                    sq, Bsq = norm1(mm, Bmm, None, None)
                                    return lambda: norm2(sq, Bsq, mm, Bmm, None, None, "blk64", 1.0, SMc("qkw", qk, qk + 1), dstT)
                                units.append(unit)
                        for i in range(nt):
                            for (col0, dstD) in ((1024, VV), (1536, ZZ)):
                                def unit(i=i, col0=col0, dstD=dstD):
                                    ti = t0 + i
                                    mm, Bmm = tm_block(i, col0, 512)
                                    ob, Bob = obs.next()
                                    OP("act", "activation", ob[:, :], mm[:, :], AF.Copy, reads=[Bmm], writes=[Bob])
                                    DMA(dstD[ti * 128:(ti + 1) * 128, :], ob[:, :], reads=[Bob])
                                    return None
                                units.append(unit)

                            def unit(i=i):
                                ti = t0 + i
                                mm, Bmm = tm_block(i, 2048, 8)
                                bt, Bbt = bat.next()
                                OP("act", "activation", bt[:, 0:4], mm[:, 0:4], AF.Exp, scale=-1.0, reads=[Bmm], writes=[Bbt])
                                OP("dve", "tensor_scalar", bt[:, 0:4], bt[:, 0:4], 1.0, None, op0=ALU.add, reads=[Bbt], writes=[Bbt])
                                OP("dve", "reciprocal", bg[:, ti * 8:ti * 8 + 4], bt[:, 0:4], reads=[Bbt], writes=[Bbg])
                                OP("dve", "tensor_tensor", bt[:, 4:8], mm[:, 4:8], SMc("dtb"), ALU.add, reads=[Bmm, Bsm], writes=[Bbt])
                                OP("act", "activation", bt[:, 8:12], bt[:, 4:8], AF.Exp, reads=[Bbt], writes=[Bbt])
                                OP("act", "activation", bt[:, 12:16], bt[:, 8:12], AF.Ln, bias=C("one"), reads=[Bbt, Bcf], writes=[Bbt])
                                OP("dve", "tensor_tensor", bg[:, ti * 8 + 4:ti * 8 + 8], bt[:, 12:16], negA[:], ALU.mult,
                                   reads=[Bbt, BnegA], writes=[Bbg])
                                if ti == 0:
                                    OP("dve", "tensor_scalar", bg[:, 0:8], bg[:, 0:8], C("valid"), None, op0=ALU.mult, reads=[Bbg, Bcf], writes=[Bbg])
                                return None
                            units.append(unit)
                    else:
                        for typ in range(3):
                            for h in range(4):
                                def unit(typ=typ, h=h):
                                    s = typ * 4 + h
                                    mm, Bmm = fm_block(typ * 512 + h * 128)
                                    cbs = cbuf[:, s * 515:(s + 1) * 515]
                                    Bc = Bcbuf[s]
                                    OP("act", "activation", cbs[:, 3:3 + T], mm[:, 0:T], AF.Copy, reads=[Bmm], writes=[Bc])
                                    acc, Bacc = accs.next()
                                    cw0 = SM["convw"][0] + s * 4
                                    OP("dve", "tensor_scalar", acc[:, 0:T], cbs[:, 0:T], sm[:, cw0:cw0 + 1], None, op0=ALU.mult,
                                       reads=[Bc, Bsm], writes=[Bacc])
                                    for j in range(1, 4):
                                        OP("dve", "scalar_tensor_tensor", acc[:, 0:T], cbs[:, j:j + T], sm[:, cw0 + j:cw0 + j + 1], acc[:, 0:T],
                                           op0=ALU.mult, op1=ALU.add, reads=[Bc, Bsm, Bacc], writes=[Bacc])
                                    OP("pool", "tensor_copy", cbs[:, 0:3], cbs[:, T:T + 3], reads=[Bc], writes=[Bc])
                                    dstT = (DQ, DK, DV)[typ][h, :, t0 * 128:t0 * 128 + T]
                                    if typ == 2:
                                        ob, Bob = obs.next()
                                        OP("act", "activation", ob[:, 0:T], acc[:, 0:T], AF.Silu, reads=[Bacc], writes=[Bob])
                                        DMA(dstT, ob[:, 0:T], reads=[Bob])
                                        return None
                                    sl, Bsl = sls.next()
                                    OP("act", "activation", sl[:, 0:T], acc[:, 0:T], AF.Silu, reads=[Bacc], writes=[Bsl])
                                    sq, Bsq = norm1(None, None, sl, Bsl)
                                    return lambda: norm2(sq, Bsq, None, None, sl, Bsl, "ones", 128.0 if typ == 0 else 1.0, None, dstT)
                                units.append(unit)

                    prevpost = None
                    nxt_ul = None
                    for idx, unit in enumerate(units):
                        post2 = unit()
                        if prevpost is not None:
                            prevpost()
                        prevpost = post2
                        if idx == 2 and bi + 1 < len(blocks):
                            nxt_ul = stage1(bi + 1)
                    if bi + 1 < len(blocks):
                        if nxt_ul is None:
                            nxt_ul = stage1(bi + 1)
                        cur_uT = stage2(bi + 1, nxt_ul)
                    if prevpost is not None:
                        prevpost()
                    if pass_id == 0:
                        convert_some(7)
                while pass_id == 0 and conv_state["i"] < len(conv_jobs):
                    convert_some(4)
                if debug and pass_id == 0:
                    DMA(BGo[:, :], bg[:], reads=[Bbg])
                P.flush()

        phase_A(0)
        if upto >= 1:
            phase_A(1)
        env = dict(locals())
        if upto >= 2:
            build_rest(env)
    return nc


def build_rest(env):
    g = env
    nc, P, NT, L, S, debug, upto = g["nc"], g["P"], g["NT"], g["L"], g["S"], g["debug"], g["upto"]
    OP, MM, DMA, sbuf, psum, rot, C, SMc = g["OP"], g["MM"], g["DMA"], g["sbuf"], g["psum"], g["rot"], g["C"], g["SMc"]
    cf, cb, sm, bg, lamt = g["cf"], g["cb"], g["sm"], g["bg"], g["lamt"]
    Bcf, Bcb, Bsm, Bbg, Blam = g["Bcf"], g["Bcb"], g["Bsm"], g["Bbg"], g["Blam"]
    QT, KT, VV, DQ, DK, DV, ZZ, OTc = g["QT"], g["KT"], g["VV"], g["DQ"], g["DK"], g["DV"], g["ZZ"], g["OTc"]
    aug = g["aug"]
    NQB = (NT - 1) // 4

    with ExitStack() as ph:
        KTa = [rot(ph, "KTa%d" % m, 2, [67, L], BF16) for m in range(2)]
        QTb = [rot(ph, "QTb%d" % m, 3, [67, 512], BF16) for m in range(2)]
        agq = rot(ph, "agq", 2, [67, 512], F32)
        Vas = rot(ph, "Va", 2, [128, NT * 129], BF16)
        APW = 2080
        augs = rot(ph, "augst", 2, [67, APW], F32)
        Sps = rot(ph, "Sps", 3, [128, 512], F32, ps=True)
        Oas = [[(psum(ph, "Oa%d%d" % (m, pr), [128, 512], F32), Buf("Oa%d%d" % (m, pr))) for pr in range(2)] for m in range(2)]
        tpo = rot(ph, "tpo", 1, [128, 1024], BF16, ps=True)
        Pts = rot(ph, "Pt", 4, [128, 512], BF16)
        rrs = rot(ph, "arr", 4, [128, 8], F32)
        t1s = rot(ph, "at1", 2, [128, 128], F32)
        dds = rot(ph, "add", 2, [128, 128], F32)
        jks = rot(ph, "ajk", 1, [128, 128], BF16)
        oas = rot(ph, "aoa", 8, [128, 128], BF16)
        Osbs = rot(ph, "Osb", 4, [128, 385], F32)
        pending = []
        oTs = rot(ph, "aoT", 2, [128, 512], BF16)

        def load_slot(s):
            tiles = {}
            for m in range(2):
                kt_, Bk = KTa[m].next()
                DMA(kt_[0:64, :], KT[s, m * 64:(m + 1) * 64, :], writes=[Bk])
                for side, (tl, Bt) in enumerate(((kt_, Bk),)):
                    for a0 in range(0, L, APW):
                        aw = min(APW, L - a0)
                        ag, Bag = augs.next()
                        DMA(ag[64:67, 0:aw], aug[s, side, :, a0:a0 + aw], writes=[Bag])
                        OP("pool", "tensor_copy", tl[64:67, a0:a0 + aw], ag[64:67, 0:aw], reads=[Bag], writes=[Bt])
                tiles["k%d" % m] = (kt_, Bk)
            va, Bva = Vas.next()
            vav = va[:, :].rearrange("p (t d) -> p t d", d=129)
            vsrc = VV[:, s * 128:(s + 1) * 128].rearrange("(t p) d -> p t d", p=128)
            for t_0 in range(0, NT, 13):
                t_1 = min(NT, t_0 + 13)
                DMA(vav[:, t_0:t_1, 0:128], vsrc[:, t_0:t_1, :], writes=[Bva])
            OP("pool", "memset", vav[:, :, 128:129], 1.0, writes=[Bva])
            OP("pool", "memset", vav[0:112, 0:1, 128:129], 0.0, writes=[Bva])
            tiles["v"] = (va, Bva)
            return tiles

        nxt = load_slot(0)
        for s in range(4):
            cur = nxt
            if s + 1 < 4:
                nxt = load_slot(s + 1)
            va, Bva = cur["v"]
            vav = va[:, :].rearrange("p (t d) -> p t d", d=129)
            slope_min = 2.0 ** -(2 * s + 2)
            for qb in range(NQB):
                q0 = 1 + 4 * qb
                kts = [0]
                for kt in range(1, q0 + 4):
                    if slope_min * (128 * (q0 - kt) - 127) > ALIBI_CUT:
                        continue
                    kts.append(kt)
                started = {}
                qcur = []
                for m in range(2):
                    qtl, Bq = QTb[m].next()
                    DMA(qtl[0:64, :], QT[s, m * 64:(m + 1) * 64, q0 * 128:(q0 + 4) * 128], writes=[Bq])
                    ag, Bag = agq.next()
                    DMA(ag[64:67, :], aug[s, 1, :, q0 * 128:(q0 + 4) * 128], writes=[Bag])
                    OP("pool", "tensor_copy", qtl[64:67, :], ag[64:67, :], reads=[Bag], writes=[Bq])
                    qcur.append((qtl, Bq))
                steps = [(kt, m) for kt in kts for m in range(2)]

                def emit_qk(kt, m):
                    c0 = max(kt - q0, 0) * 128
                    ktl, Bk = cur["k%d" % m]
                    qtl, Bq = qcur[m]
                    Sp, BS = Sps.next()
                    diag = kt >= q0
                    MM(Sp[:, c0:512], ktl[0:67, kt * 128:(kt + 1) * 128], qtl[0:67, c0:512], True, not diag, [Bk, Bq], [BS])
                    if diag:
                        MM(Sp[:, c0:c0 + 128], C("ident", True), C("cneg", True), False, True, [Bcb], [BS])
                    return Sp, BS, c0

                def emit_exp(kt, m, Sp, BS, c0):
                    Pt, BP = Pts.next()
                    bias = C("zero") if kt == 0 else C("kbias", False, s, s + 1)
                    OP("act", "activation", Pt[:, c0:512], Sp[:, c0:512], AF.Exp, bias=bias, scale=0.125, reads=[BS, Bcf], writes=[BP])
                    return Pt, BP

                def emit_pv(kt, m, Pt, BP, c0):
                    for qi in range(c0 // 128, 4):
                        Oa, BO = Oas[m][qi // 2]
                        col = (qi % 2) * 256
                        key = (m, qi // 2)
                        first = key not in started
                        started[key] = True
                        MM(Oa[:, col:col + 129], Pt[:, qi * 128:(qi + 1) * 128], vav[:, kt, :], first, kt == q0 + qi, [BP, Bva], [BO])

                LA = 2
                qk = {}
                for n in range(min(LA, len(steps))):
                    qk[n] = emit_qk(*steps[n])
                for n in range(len(steps)):
                    Sp, BS, c0 = qk.pop(n)
                    Pt, BP = emit_exp(steps[n][0], steps[n][1], Sp, BS, c0)
                    if n + LA < len(steps):
                        qk[n + LA] = emit_qk(*steps[n + LA])
                    emit_pv(steps[n][0], steps[n][1], Pt, BP, c0)
                    if n == 3 and pending:
                        pending.pop()()
                Osb = []
                for m in range(2):
                    for pr in range(2):
                        Oa, BO = Oas[m][pr]
                        ob_, Bob_ = Osbs.next()
                        if pr == 0:
                            OP("act", "activation", ob_[:, 0:385], Oa[:, 0:385], AF.Copy, reads=[BO], writes=[Bob_])
                        else:
                            OP("dve", "tensor_copy", ob_[:, 0:385], Oa[:, 0:385], reads=[BO], writes=[Bob_])
                        Osb.append((ob_, Bob_))
                oT, BoT = oTs.next()
                oa4 = []
                for qi in range(4):
                    col = (qi % 2) * 256
                    O1, BO1 = Osb[0 * 2 + qi // 2]
                    O2, BO2 = Osb[1 * 2 + qi // 2]
                    rr, Brr = rrs.next()
                    OP("dve", "reciprocal", rr[:, 0:1], O1[:, col + 128:col + 129], reads=[BO1], writes=[Brr])
                    OP("dve", "reciprocal", rr[:, 1:2], O2[:, col + 128:col + 129], reads=[BO2], writes=[Brr])
                    OP("dve", "tensor_tensor", rr[:, 2:3], rr[:, 1:2], lamt[:, 5:6], ALU.mult, reads=[Brr, Blam], writes=[Brr])
                    t1, Bt1 = t1s.next()
                    OP("dve", "tensor_scalar", t1[:], O1[:, col:col + 128], rr[:, 0:1], None, op0=ALU.mult, reads=[BO1, Brr], writes=[Bt1])
                    dd, Bdd = dds.next()
                    OP("dve", "scalar_tensor_tensor", dd[:], O2[:, col:col + 128], rr[:, 2:3], t1[:], op0=ALU.mult, op1=ALU.add,
                       reads=[BO2, Brr, Bt1], writes=[Bdd])
                    jk, Bjk = jks.next()
                    OP("act", "activation", jk[:], dd[:], AF.Square, accum_out=rr[:, 3:4], reads=[Bdd], writes=[Bjk, Brr])
                    OP("act", "activation", rr[:, 4:5], rr[:, 3:4], AF.Ln, bias=C("eps"), scale=1.0 / 128, reads=[Brr, Bcf], writes=[Brr])
                    OP("act", "activation", rr[:, 5:6], rr[:, 4:5], AF.Exp, scale=-0.5, reads=[Brr], writes=[Brr])
                    oa, Boa = oas.next()
                    OP("dve", "scalar_tensor_tensor", oa[:], dd[:], rr[:, 5:6], SMc("subw"), op0=ALU.mult, op1=ALU.mult,
                       reads=[Bdd, Brr, Bsm], writes=[Boa])
                    oa4.append((oa, Boa))

                def finish(oa4=oa4, oT=oT, BoT=BoT, s=s, q0=q0):
                    tp, Btp = tpo.next()
                    for qi in range(4):
                        oa, Boa = oa4[qi]
                        OP("pe", "transpose", tp[:, qi * 128:(qi + 1) * 128], oa[:], C("ident", True), reads=[Boa, Bcb], writes=[Btp])
                    OP("act", "activation", oT[:, :], tp[:, 0:512], AF.Copy, scale=1.0 - LAMBDA_INIT, reads=[Btp], writes=[BoT])
                    DMA(OTc(s * 128, (s + 1) * 128, (q0 - 1) * 128, (q0 + 3) * 128), oT[:, :], reads=[BoT])
                pending.append(finish)
        while pending:
            pending.pop()()
        P.flush()
    if upto >= 3:
        build_dn(env)


def build_dn(env):
    g = env
    nc, P, NT, L, S, debug, upto = g["nc"], g["P"], g["NT"], g["L"], g["S"], g["debug"], g["upto"]
    OP, MM, DMA, sbuf, psum, rot, C, SMc = g["OP"], g["MM"], g["DMA"], g["sbuf"], g["psum"], g["rot"], g["C"], g["SMc"]
    cf, cb, sm, bg = g["cf"], g["cb"], g["sm"], g["bg"]
    Bcf, Bcb, Bsm, Bbg = g["Bcf"], g["Bcb"], g["Bsm"], g["Bbg"]
    DQ, DK, DV, ZZ, OTc = g["DQ"], g["DK"], g["DV"], g["ZZ"], g["OTc"]

    with ExitStack() as ph:
        pf = rot(ph, "pf", 6, [128, 512], F32, ps=True)
        pb = rot(ph, "pb", 2, [128, 1024], BF16, ps=True)
        R = {}

        def T(name, dt=BF16, n=2, w=512):
            if name not in R:
                R[name] = rot(ph, "c" + name, n, [128, w], dt)
            return R[name].next()

        Sf = sbuf(ph, "Sf", [128, 512], F32)
        BSf = Buf("Sf")
        Sb = sbuf(ph, "Sb", [128, 512], BF16)
        BSb = Buf("Sb")
        OP("pool", "memset", Sf[:], 0.0, writes=[BSf])
        OP("pool", "memset", Sb[:], 0.0, writes=[BSb])
        hs = [slice(h * 128, (h + 1) * 128) for h in range(4)]

        def mm4(ps, Bps, lhs, Blhs, rhs, Brhs, extra_reads=()):
            for h in range(4):
                MM(ps[:, hs[h]], lhs[:, hs[h]], rhs[:, hs[h]], True, True, [Blhs, Brhs] + list(extra_reads), [Bps])

        def tr4(src, Bsrc, col0=0):
            tp, Btp = pb.next()
            for h in range(4):
                OP("pe", "transpose", tp[:, col0 + h * 128:col0 + (h + 1) * 128], src[:, hs[h]], C("ident", True),
                   reads=[Bsrc, Bcb], writes=[Btp])
            return tp, Btp

        for i in range(NT):
            kT, BkT = T("kT", n=3)
            qT, BqT = T("qT", n=3)
            vT, BvT = T("vT", n=3)
            tsl = slice(i * 128, (i + 1) * 128)
            DMA(kT[:, :].rearrange("p (h t) -> p h t", h=4), DK[:, :, tsl].rearrange("h p t -> p h t"), writes=[BkT])
            DMA(qT[:, :].rearrange("p (h t) -> p h t", h=4), DQ[:, :, tsl].rearrange("h p t -> p h t"), writes=[BqT])
            DMA(vT[:, :].rearrange("p (h t) -> p h t", h=4), DV[:, :, tsl].rearrange("h p t -> p h t"), writes=[BvT])
            if i >= 1:
                z4, Bz4 = T("z4", n=2)
                DMA(z4[:, :], ZZ[tsl, :], writes=[Bz4])
            g4 = bg[:, i * 8 + 4:i * 8 + 8]
            b4 = bg[:, i * 8:i * 8 + 4]
            gp, Bgp = pf.next()
            MM(gp[:, 0:4], C("utf"), g4, True, True, [Bcf, Bbg], [Bgp])
            B4, BB4 = T("B4", F32)
            for h in range(4):
                OP("pool", "tensor_scalar", B4[:, hs[h]], C("utf"), g4[:, h:h + 1], None, op0=ALU.mult, reads=[Bcf, Bbg], writes=[BB4])
            Gr, BGr = pf.next()
            MM(Gr[:, :], C("ones"), B4[:, :], True, True, [Bcf, BB4], [BGr])
            sc, Bsc = T("sc", F32, 2, 32)
            gcs, dl, edl, edlb, egl = sc[:, 0:4], sc[:, 4:8], sc[:, 8:12], sc[:, 12:16], sc[:, 16:20]
            OP("act", "activation", gcs, gp[:, 0:4], AF.Copy, reads=[Bgp], writes=[Bsc])
            Grl = Gr[:, :].rearrange("p (h t) -> p h t", h=4)[:, :, 127:128]
            OP("dve", "tensor_tensor", dl.rearrange("p (h o) -> p h o", o=1), Grl, gcs.rearrange("p (h o) -> p h o", o=1),
               ALU.subtract, reads=[BGr, Bsc], writes=[Bsc])
            OP("act", "activation", edl, dl, AF.Exp, reads=[Bsc], writes=[Bsc])
            OP("act", "activation", egl.rearrange("p (h o) -> p h o", o=1), Grl, AF.Exp, reads=[BGr], writes=[Bsc])
            OP("dve", "tensor_tensor", edlb, edl, b4, ALU.mult, reads=[Bsc, Bbg], writes=[Bsc])
            Er, BEr = T("Er")
            OP("act", "activation", Er[:, :], Gr[:, :], AF.Exp, reads=[BGr], writes=[BEr])
            KgT, BKgT = T("KgT")
            QgT, BQgT = T("QgT")
            OP("pool", "tensor_tensor", KgT[:, :], kT[:, :], Er[:, :], ALU.mult, reads=[BkT, BEr], writes=[BKgT])
            OP("pool", "tensor_tensor", QgT[:, :], qT[:, :], Er[:, :], ALU.mult, reads=[BqT, BEr], writes=[BQgT])
            Dm, BDm = T("Dm", F32)
            for h in range(4):
                OP("dve", "tensor_scalar", Dm[:, hs[h]], Gr[:, hs[h]], gcs[:, h:h + 1], 0.0, op0=ALU.subtract, op1=ALU.min,
                   reads=[BGr, Bsc], writes=[BDm])
            OP("act", "activation", Dm[:, :], Dm[:, :], AF.Exp, reads=[BDm], writes=[BDm])
            DTs, BDTs = T("DTs", F32)
            DTi, BDTi = T("DTi", F32)
            OP("pool", "tensor_tensor", DTs[:, :], Dm[:, :], C("su4"), ALU.mult, reads=[BDm, Bcf], writes=[BDTs])
            OP("pool", "tensor_tensor", DTi[:, :], Dm[:, :], C("iu4"), ALU.mult, reads=[BDm, Bcf], writes=[BDTi])
            tkv, Btkv = pb.next()
            for h in range(4):
                OP("pe", "transpose", tkv[:, hs[h]], kT[:, hs[h]], C("ident", True), reads=[BkT, Bcb], writes=[Btkv])
            for h in range(4):
                OP("pe", "transpose", tkv[:, 512 + h * 128:512 + (h + 1) * 128], vT[:, hs[h]], C("ident", True), reads=[BvT, Bcb], writes=[Btkv])
            V4, BV4 = T("V4")
            OP("act", "activation", V4[:, :], tkv[:, 512:1024], AF.Copy, reads=[Btkv], writes=[BV4])
            Kd, BKd = T("Kd")
            for h in range(4):
                OP("dve", "tensor_scalar", Kd[:, hs[h]], tkv[:, hs[h]], edlb[:, h:h + 1], None, op0=ALU.mult, reads=[Btkv, Bsc], writes=[BKd])
            Ap, BAp = pf.next()
            mm4(Ap, BAp, kT, BkT, kT, BkT)
            Qp, BQp = pf.next()
            mm4(Qp, BQp, kT, BkT, qT, BqT)
            Y, BY = T("Y")
            QKd, BQKd = T("QKd")
            for h in range(4):
                OP("dve", "scalar_tensor_tensor", Y[:, hs[h]], Ap[:, hs[h]], b4[:, h:h + 1], DTs[:, hs[h]], op0=ALU.mult, op1=ALU.mult,
                   reads=[BAp, Bbg, BDTs], writes=[BY])
            for h in range(4):
                OP("dve", "scalar_tensor_tensor", QKd[:, hs[h]], Qp[:, hs[h]], b4[:, h:h + 1], DTi[:, hs[h]], op0=ALU.mult, op1=ALU.mult,
                   reads=[BQp, Bbg, BDTi], writes=[BQKd])
            tn, Btn = tr4(Y, BY)
            Nk, BNk = T("Nk", n=3)
            O1T, BO1T = T("O1T")
            O2T, BO2T = T("O2T")
            OP("dve", "tensor_tensor", Nk[:, :], tn[:, 0:512], C("d32", True), ALU.mult, reads=[Btn, Bcb], writes=[BNk])
            OP("dve", "tensor_tensor", O1T[:, :], tn[:, 0:512], C("o1t", True), ALU.mult, reads=[Btn, Bcb], writes=[BO1T])
            OP("dve", "tensor_tensor", O2T[:, :], tn[:, 0:512], C("o2t", True), ALU.mult, reads=[Btn, Bcb], writes=[BO2T])
            Yk, BYk = T("Yk", n=3)
            OP("pool", "tensor_tensor", Yk[:, :], Y[:, :], C("d32", True), ALU.mult, reads=[BY, Bcb], writes=[BYk])
            Pk, BPk = T("Pk", n=3)
            OP("pool", "tensor_tensor", Pk[:, :], C("i4", True), Yk[:, :], ALU.subtract, reads=[BYk, Bcb], writes=[BPk])
            for lvl in range(4):
                last = lvl == 3
                if not last:
                    py, Bpy = pf.next()
                    mm4(py, Bpy, Nk, BNk, Yk, BYk)
                pn, Bpn = pf.next()
                mm4(pn, Bpn, Yk, BYk, Nk, BNk)
                Nn, BNn = T("Nk", n=3)
                OP("act", "activation", Nn[:, :], pn[:, :], AF.Copy, reads=[Bpn], writes=[BNn])
                if not last:
                    Yn, BYn = T("Yk", n=3)
                    OP("act", "activation", Yn[:, :], py[:, :], AF.Copy, reads=[Bpy], writes=[BYn])
                    Yk, BYk = Yn, BYn
                Nk, BNk = Nn, BNn
                pp, Bpp = pf.next()
                mm4(pp, Bpp, Nk, BNk, Pk, BPk)
                Pn, BPn = T("Pk", n=3)
                OP("dve", "tensor_tensor", Pn[:, :], pp[:, :], Pk[:, :], ALU.add, reads=[Bpp, BPk], writes=[BPn])
                Pk, BPk = Pn, BPn
            Z, BZ = Pk, BPk
            for (OxT, BOxT) in ((O1T, BO1T), (O2T, BO2T)):
                tz, Btz = tr4(Z, BZ)
                ZT, BZT = T("ZT")
                OP("act", "activation", ZT[:, :], tz[:, 0:512], AF.Copy, reads=[Btz], writes=[BZT])
                pm, Bpm = pf.next()
                mm4(pm, Bpm, OxT, BOxT, Z, BZ)
                M1, BM1 = T("M1")
                OP("act", "activation", M1[:, :], pm[:, :], AF.Copy, reads=[Bpm], writes=[BM1])
                pz, Bpz = pf.next()
                mm4(pz, Bpz, ZT, BZT, M1, BM1)
                Zn, BZn = T("Pk", n=3)
                OP("dve", "tensor_tensor", Zn[:, :], Z[:, :], pz[:, :], ALU.subtract, reads=[BZ, Bpz], writes=[BZn])
                Z, BZ = Zn, BZn
            XT, BXT = Z, BZ
            pk, Bpk = pf.next()
            mm4(pk, Bpk, KgT, BKgT, Sb, BSb)
            R4, BR4 = T("R4")
            OP("dve", "tensor_tensor", R4[:, :], V4[:, :], pk[:, :], ALU.subtract, reads=[BV4, Bpk], writes=[BR4])
            px, Bpx = pf.next()
            mm4(px, Bpx, XT, BXT, R4, BR4)
            vn, Bvn = T("vn")
            OP("act", "activation", vn[:, :], px[:, :], AF.Copy, reads=[Bpx], writes=[Bvn])
            if i >= 1:
                po, Bpo = pf.next()
                for h in range(4):
                    MM(po[:, hs[h]], QgT[:, hs[h]], Sb[:, hs[h]], True, False, [BQgT, BSb], [Bpo])
                    MM(po[:, hs[h]], QKd[:, hs[h]], vn[:, hs[h]], False, True, [BQKd, Bvn], [Bpo])
            if i + 1 < NT:
                pd_, Bpd = pf.next()
                mm4(pd_, Bpd, Kd, BKd, vn, Bvn)
                for h in range(4):
                    OP("dve", "scalar_tensor_tensor", Sf[:, hs[h]], Sf[:, hs[h]], egl[:, h:h + 1], pd_[:, hs[h]], op0=ALU.mult, op1=ALU.add,
                       reads=[BSf, Bsc, Bpd], writes=[BSf])
                OP("act", "activation", Sb[:, :], Sf[:, :], AF.Copy, reads=[BSf], writes=[BSb])
            if i >= 1:
                so, Bso = T("so", F32, 2, 16)
                jk, Bjk = T("jk", BF16, 1, 128)
                for h in range(4):
                    OP("act", "activation", jk[:, :], po[:, hs[h]], AF.Square, accum_out=so[:, h:h + 1], reads=[Bpo], writes=[Bjk, Bso])
                OP("act", "activation", so[:, 4:8], so[:, 0:4], AF.Ln, bias=C("eps"), scale=1.0 / 128, reads=[Bso, Bcf], writes=[Bso])
                OP("act", "activation", so[:, 8:12], so[:, 4:8], AF.Exp, scale=-0.5, reads=[Bso], writes=[Bso])
                on, Bon = T("on", F32)
                for h in range(4):
                    OP("dve", "scalar_tensor_tensor", on[:, hs[h]], po[:, hs[h]], so[:, 8 + h:9 + h], SMc("onw"), op0=ALU.mult, op1=ALU.mult,
                       reads=[Bpo, Bso, Bsm], writes=[Bon])
                sz, Bsz = T("sz", F32)
                OP("act", "activation", sz[:, :], z4[:, :], AF.Silu, reads=[Bz4], writes=[Bsz])
                og, Bog = T("og")
                OP("pool", "tensor_tensor", og[:, :], on[:, :], sz[:, :], ALU.mult, reads=[Bon, Bsz], writes=[Bog])
                tg, Btg = tr4(og, Bog)
                odT, BodT = T("odT")
                OP("act", "activation", odT[:, :], tg[:, 0:512], AF.Copy, reads=[Btg], writes=[BodT])
                DMA(OTc(512, 1024, (i - 1) * 128, i * 128).rearrange("(h p) t -> p h t", p=128),
                    odT[:, :].rearrange("p (h t) -> p h t", h=4), reads=[BodT])
        P.flush()
    if upto >= 4:
        build_ffn(env)


def build_ffn(env):
    g = env
    nc, P, NT, L, S, debug, upto = g["nc"], g["P"], g["NT"], g["L"], g["S"], g["debug"], g["upto"]
    OP, MM, DMA, sbuf, psum, rot, C, SMc = g["OP"], g["MM"], g["DMA"], g["sbuf"], g["psum"], g["rot"], g["C"], g["SMc"]
    cf, cb, sm = g["cf"], g["cb"], g["sm"]
    Bcf, Bcb, Bsm = g["Bcf"], g["Bcb"], g["Bsm"]
    OTs, OTFs, OTFc, WG2, WU2, WD2, WO2 = g["OTs"], g["OTFs"], g["OTFc"], g["WG2"], g["WU2"], g["WD2"], g["WO2"]
    xh, sel, fnw, out, SH, NTH = g["xh"], g["sel"], g["fnw"], g["out"], g["SH"], g["NTH"]

    def CC(a, b):
        o = P.op("pool", lambda e: e.collective_compute("AllGather", ALU.bypass, replica_groups=[[0, 1], [2, 3], [4, 5], [6, 7]],
                                                         ins=[a], outs=[b]), (), ())
        o.sig = True
    for k in range(len(OTs)):
        CC(OTs[k], OTFs[k])
    P.flush()

    with ExitStack() as ph:
        pf = rot(ph, "epf", 3, [128, 512], F32, ps=True)
        pgu = rot(ph, "epgu", 4, [128, 512], F32, ps=True)
        pb = rot(ph, "epb", 1, [128, 1024], BF16, ps=True)
        oTa = sbuf(ph, "oTa", [128, 16 * 512], BF16)
        oTb = sbuf(ph, "oTb", [128, 16 * 512], BF16)
        BoTa, BoTb = Buf("oTa"), Buf("oTb")
        wobs = rot(ph, "wob", 2, [128, 16 * 256], BF16)
        haccs = [(sbuf(ph, "hacc%d" % i, [128, D], F32), Buf("hacc%d" % i)) for i in range(4)]
        u = sbuf(ph, "eu", [128, D], BF16)
        Bu = Buf("eu")
        uT = sbuf(ph, "euT", [128, 16 * 512], BF16)
        BuT = Buf("euT")
        uTv = uT[:, :].rearrange("p (c t) -> p c t", c=16)
        fnwt = sbuf(ph, "fnwt", [128, D], F32)
        Bfnw = Buf("fnwt")
        selt = sbuf(ph, "selt", [128, 2], F32)
        Bsel = Buf("selt")
        DMA(fnwt[:], fnw[:, :], writes=[Bfnw])
        DMA(selt[:], sel[:, :], writes=[Bsel])
        wgs = rot(ph, "wgb", 2, [128, 16 * 256], BF16)
        wus = rot(ph, "wub", 2, [128, 16 * 256], BF16)
        wds = rot(ph, "wdb", 3, [128, 2 * 2048], BF16)
        sgs = rot(ph, "sg", 2, [128, 512], F32)
        acts = rot(ph, "actT", 4, [128, 512], BF16)
        sts = rot(ph, "est", 2, [128, 4], F32)

        for tb in range(NTH // 4):
            c0 = tb * 512
            for hf in range(2):
                DMA(oTa[:, hf * 4096:(hf + 1) * 4096].rearrange("p (c t) -> p c t", c=8),
                    OTFc(hf * 1024, (hf + 1) * 1024, c0, c0 + 512).rearrange("(c p) t -> p c t", p=128), writes=[BoTa])
                DMA(oTb[:, hf * 4096:(hf + 1) * 4096].rearrange("p (c t) -> p c t", c=8),
                    OTFc(hf * 1024, (hf + 1) * 1024, SH + c0, SH + c0 + 512).rearrange("(c p) t -> p c t", p=128), writes=[BoTb])
            OP("pool", "tensor_scalar", oTa[:, :], oTa[:, :], selt[:, 0:1], None, op0=ALU.mult, reads=[BoTa, Bsel], writes=[BoTa])
            OP("dve", "scalar_tensor_tensor", oTa[:, :], oTb[:, :], selt[:, 1:2], oTa[:, :], op0=ALU.mult, op1=ALU.add,
               reads=[BoTa, BoTb, Bsel], writes=[BoTa])
            for i in range(4):
                ha, Bha = haccs[i]
                DMA(ha[:], xh[c0 + i * 128:c0 + (i + 1) * 128, :], writes=[Bha])
            for n in range(8):
                wob, Bwob = wobs.next()
                DMA(wob[:, :].rearrange("p (c n) -> p c n", c=16), WO2[:, n * 256:(n + 1) * 256].rearrange("(c p) n -> p c n", p=128), writes=[Bwob])
                for i in range(4):
                    ha, Bha = haccs[i]
                    ps, Bps = pf.next()
                    for c in range(16):
                        MM(ps[:, 0:256], oTa[:, c * 512 + i * 128:c * 512 + (i + 1) * 128], wob[:, c * 256:(c + 1) * 256],
                           c == 0, c == 15, [BoTa, Bwob], [Bps])
                    OP("dve", "tensor_tensor", ha[:, n * 256:(n + 1) * 256], ps[:, 0:256], ha[:, n * 256:(n + 1) * 256], ALU.add,
                       reads=[Bps, Bha], writes=[Bha])
            for i in range(4):
                ha, Bha = haccs[i]
                st, Bst = sts.next()
                OP("act", "activation", u[:], ha[:], AF.Square, accum_out=st[:, 0:1], reads=[Bha], writes=[Bu, Bst])
                OP("act", "activation", st[:, 1:2], st[:, 0:1], AF.Ln, bias=C("eps"), scale=1.0 / D, reads=[Bst, Bcf], writes=[Bst])
                OP("act", "activation", st[:, 2:3], st[:, 1:2], AF.Exp, scale=-0.5, reads=[Bst], writes=[Bst])
                OP("dve", "scalar_tensor_tensor", u[:], ha[:], st[:, 2:3], fnwt[:], op0=ALU.mult, op1=ALU.mult,
                   reads=[Bha, Bst, Bfnw], writes=[Bu])
                for half in range(2):
                    tp, Btp = pb.next()
                    for c8 in range(8):
                        c = half * 8 + c8
                        OP("pe", "transpose", tp[:, c8 * 128:(c8 + 1) * 128], u[:, c * 128:(c + 1) * 128], C("ident", True),
                           reads=[Bu, Bcb], writes=[Btp])
                    dstv = uTv[:, half * 8:half * 8 + 8, i * 128:(i + 1) * 128]
                    srcv = tp[:, :].rearrange("p (c t) -> p c t", c=8)
                    if half == 0:
                        OP("act", "activation", dstv, srcv, AF.Copy, reads=[Btp], writes=[BuT])
                    else:
                        OP("dve", "tensor_copy", dstv, srcv, reads=[Btp], writes=[BuT])
            def down_jobs(ak, wdb, Bwd):
                jobs = []
                for i in range(4):
                    for n in range(4):
                        def job(i=i, n=n):
                            ha, Bha = haccs[i]
                            ps, Bps = pf.next()
                            for k in range(2):
                                at, Bat = ak[k]
                                MM(ps[:, :], at[:, i * 128:(i + 1) * 128], wdb[:, k * 2048 + n * 512:k * 2048 + (n + 1) * 512], k == 0, k == 1, [Bat, Bwd], [Bps])
                            OP("dve", "tensor_tensor", ha[:, n * 512:(n + 1) * 512], ps[:, :], ha[:, n * 512:(n + 1) * 512], ALU.add,
                               reads=[Bps, Bha], writes=[Bha])
                        jobs.append(job)
                return jobs

            pend = []
            for gi in range(22):
                wgb, Bwg = wgs.next()
                wub, Bwu = wus.next()
                wdb, Bwd = wds.next()
                DMA(wgb[:, :], WG2[gi, :, :], writes=[Bwg])
                DMA(wub[:, :], WU2[gi, :, :], writes=[Bwu])
                DMA(wdb[:, :].rearrange("p (k n) -> p k n", k=2), WD2[gi * 256:(gi + 1) * 256, :].rearrange("(k p) n -> p k n", p=128), writes=[Bwd])
                ak = []
                for k in range(2):
                    pg, Bpg = pgu.next()
                    pu, Bpu = pgu.next()
                    for (pp, Bpp, wb, Bwb) in ((pg, Bpg, wgb, Bwg), (pu, Bpu, wub, Bwu)):
                        for c in range(16):
                            MM(pp[:, :], wb[:, c * 256 + k * 128:c * 256 + (k + 1) * 128], uT[:, c * 512:(c + 1) * 512], c == 0, c == 15, [Bwb, BuT], [Bpp])
                            if c % 4 == 3 and pend:
                                pend.pop(0)()
                    sg, Bsg = sgs.next()
                    OP("act", "activation", sg[:, :], pg[:, :], AF.Silu, reads=[Bpg], writes=[Bsg])
                    at, Bat = acts.next()
                    OP("dve", "tensor_tensor", at[:, :], sg[:, :], pu[:, :], ALU.mult, reads=[Bsg, Bpu], writes=[Bat])
                    ak.append((at, Bat))
                while pend:
                    pend.pop(0)()
                pend = down_jobs(ak, wdb, Bwd)
            while pend:
                pend.pop(0)()
            for i in range(4):
                ha, Bha = haccs[i]
                DMA(out[c0 + i * 128:c0 + (i + 1) * 128, :], ha[:], reads=[Bha])
        P.flush()


OFF = {"aq": 0, "ak": 1024, "av": 2048, "dq": 3072, "dk": 4096, "dv": 5120, "dz": 6144, "db": 7168, "da": 7176}


def core_inputs(inp, c, NT):
    b, j = c // 2, c % 2
    L = NT * 128
    f = np.float32
    ah = [2 * s + j for s in range(4)]
    dh = [4 * j + k for k in range(4)]
    slopes = [2.0 ** -(h + 1) for h in ah]
    w_in = inp["w_in"][0]

    def cols(base, heads, w=128):
        return np.concatenate([w_in[:, base + h * w:base + (h + 1) * w] for h in heads], axis=1)

    wA = np.concatenate([cols(OFF["aq"], ah), cols(OFF["ak"], ah), cols(OFF["av"], ah), cols(OFF["dz"], dh),
                         cols(OFF["db"], dh, 1), cols(OFF["da"], dh, 1)], axis=1)
    wD = np.concatenate([cols(OFF["dq"], dh), cols(OFF["dk"], dh), cols(OFF["dv"], dh)], axis=1)
    w_out = inp["w_out"][0]
    rows = []
    for r in range(2):
        rows += [w_out[(2 * s + r) * 128:(2 * s + r + 1) * 128] for s in range(4)]
        rows += [w_out[1024 + (4 * r + k) * 128:1024 + (4 * r + k + 1) * 128] for k in range(4)]
    wo = np.concatenate(rows, axis=0)
    small = np.zeros((128, 572), f)
    small[:, 0] = np.tile(inp["q_norm_w"][0], 2)
    small[:, 1] = np.tile(inp["k_norm_w"][0], 2)
    small[:, 2:258] = np.concatenate([inp["lambda_q1"][0], inp["lambda_k1"][0], inp["lambda_q2"][0], inp["lambda_k2"][0]])[None, :]
    small[:, 258:386] = inp["subln_w"][0][None, :]
    small[:, 386:514] = inp["o_norm_w"][0][None, :]
    cw = inp["conv_w"][0]
    for typ in range(3):
        for k, h in enumerate(dh):
            s = typ * 4 + k
            small[:, 514 + s * 4:514 + s * 4 + 4] = cw[typ * 1024 + h * 128:typ * 1024 + (h + 1) * 128, :]
    small[:, 562:566] = inp["a_log"][0][dh][None, :]
    small[:, 566:570] = inp["dt_bias"][0][dh][None, :]
    aug = np.zeros((4, 2, 3, L), f)
    tok = np.arange(L)
    tt, ti = tok // 128, tok % 128
    real = (tok >= 128).astype(f)
    for s, sl in enumerate(slopes):
        aug[s, 0, 0] = real
        aug[s, 0, 1] = real
        aug[s, 0, 2] = real * 8.0 * sl * 128.0 * tt
        aug[s, 1, 0] = -8.0 * sl * 128.0 * tt
        aug[s, 1, 1] = -8.0 * sl * ti
        aug[s, 1, 2] = 1.0
    S = L - 128
    return {
        "x": np.ascontiguousarray(inp["x"][b][:S]),
        "meta": np.ascontiguousarray(inp["meta_tokens"]),
        "anw": np.ascontiguousarray(np.broadcast_to(inp["attn_norm_w"][0][None, :], (128, D))),
        "fnw": np.ascontiguousarray(np.broadcast_to(inp["ffn_norm_w"][0][None, :], (128, D))),
        "wA": np.ascontiguousarray(wA), "wD": np.ascontiguousarray(wD), "wo": np.ascontiguousarray(wo),
        "wg": np.ascontiguousarray(inp["w_gate"][0]), "wu": np.ascontiguousarray(inp["w_up"][0]),
        "wd": np.ascontiguousarray(inp["w_down"][0]),
        "cst": make_consts(slopes), "small": small, "aug": aug,
        "xh": np.ascontiguousarray(inp["x"][b][j * (S // 2):(j + 1) * (S // 2)]),
        "sel": np.ascontiguousarray(np.broadcast_to(np.array([1.0 - j, float(j)], f)[None, :], (128, 2))),
    }


_NC_CACHE = {}


def kernel(**inputs):
    inp = {k: np.asarray(v) for k, v in inputs.items()}
    B, S, _ = inp["x"].shape
    NT = S // 128 + 1
    if NT not in _NC_CACHE:
        _NC_CACHE[NT] = build_program(NT)
    nc = _NC_CACHE[NT]
    in_maps = [core_inputs(inp, c, NT) for c in range(8)]
    res = run_bass_kernel_spmd(nc, in_maps, core_ids=list(range(8)))
    SH = S // 2
    out = np.empty((B, S, D), np.float32)
    for c in range(8):
        b, j = c // 2, c % 2
        out[b, j * SH:(j + 1) * SH] = res.results[c]["out"]
    return out
```
